# Optimizing a Trainium2 kernel written in Bass

```python
import math
import jax, jax.numpy as jnp
from jax import lax
import numpy as np

D_MODEL = 1024
BATCH = 2
SEQ = 8192
DEPTH = 2
DEC_BATCH = 8
DEC_SEQ = 4096
PAST_LEN = 128

N_RET_HEADS = 4
RET_DK = D_MODEL // 8
RET_DV = 2 * RET_DK
RET_CHUNK = 128
N_DIFF_HEADS = D_MODEL // 128
DIFF_DH = 64
Q_BLOCK = 128
RET_W = N_RET_HEADS * RET_DV
DIFF_W = N_DIFF_HEADS * 2 * DIFF_DH
MIX_W = RET_W + DIFF_W
CONV_W = 2 * D_MODEL
CONV_K = 31
T5_BUCKETS = 32
T5_MAX_DIST = 128
ROPE_BASE = 10000.0
EPS = 1e-6

N_EVEN = (DEPTH + 1) // 2
N_ODD = DEPTH // 2

EVEN_SPLITS = [N_RET_HEADS * RET_DK, N_RET_HEADS * RET_DK, RET_W, RET_W,
               N_DIFF_HEADS * 2 * DIFF_DH, N_DIFF_HEADS * 2 * DIFF_DH, DIFF_W, DIFF_W]
EVEN_IN_W = int(sum(EVEN_SPLITS))
EVEN_SPLIT_IDX = [int(i) for i in np.cumsum(EVEN_SPLITS)[:-1]]

kernel_name = "hybrid_bidir_retention_diffattn_conformer"


def rms_norm(x, g):
    xf = x.astype(jnp.float32)
    y = xf * lax.rsqrt(jnp.mean(xf * xf, axis=-1, keepdims=True) + EPS)
    return (y * g.astype(jnp.float32)).astype(x.dtype)


def head_rms(x):
    xf = x.astype(jnp.float32)
    y = xf * lax.rsqrt(jnp.mean(xf * xf, axis=-1, keepdims=True) + EPS)
    return y.astype(x.dtype)


def layer_norm(x, g, b):
    xf = x.astype(jnp.float32)
    mu = jnp.mean(xf, axis=-1, keepdims=True)
    var = jnp.mean(jnp.square(xf - mu), axis=-1, keepdims=True)
    y = (xf - mu) * lax.rsqrt(var + EPS)
    return (y * g.astype(jnp.float32) + b.astype(jnp.float32)).astype(x.dtype)


def rope(x, pos):
    d = x.shape[-1]
    inv = 1.0 / (ROPE_BASE ** (jnp.arange(0, d, 2, dtype=jnp.float32) / d))
    ang = pos.astype(jnp.float32)[:, None] * inv[None, :]
    cos = jnp.cos(ang)[None, :, None, :]
    sin = jnp.sin(ang)[None, :, None, :]
    xf = x.astype(jnp.float32)
    x1, x2 = xf[..., : d // 2], xf[..., d // 2:]
    return jnp.concatenate([x1 * cos - x2 * sin, x1 * sin + x2 * cos], axis=-1).astype(x.dtype)


def t5_bucket(rel):
    nb = T5_BUCKETS // 2
    max_exact = nb // 2
    ret = (rel > 0).astype(jnp.int32) * nb
    n = jnp.abs(rel)
    nf = jnp.maximum(n, 1).astype(jnp.float32)
    large = max_exact + (jnp.log(nf / max_exact) / math.log(T5_MAX_DIST / max_exact)
                         * (nb - max_exact)).astype(jnp.int32)
    large = jnp.minimum(large, nb - 1)
    return ret + jnp.where(n < max_exact, n, large)


def bidir_retention(q, k, v, log_g):
    B, S, H, dk = q.shape
    dv = v.shape[-1]
    C = RET_CHUNK
    N = S // C
    dt = q.dtype
    lf, lb = log_g[0], log_g[1]
    qc = q.reshape(B, N, C, H, dk)
    kc = k.reshape(B, N, C, H, dk)
    vc = v.reshape(B, N, C, H, dv)
    pos = jnp.arange(C, dtype=jnp.float32)
    diff = pos[:, None] - pos[None, :]
    dmat = (jnp.where(diff >= 0, jnp.exp(lf[:, None, None] * jnp.maximum(diff, 0.0)), 0.0)
            + jnp.where(diff < 0, jnp.exp(lb[:, None, None] * jnp.maximum(-diff, 0.0)), 0.0))
    scores = jnp.einsum('bnqhd,bnkhd->bnhqk', qc, kc) * dmat.astype(dt)[None, None]
    intra = jnp.einsum('bnhqk,bnkhe->bnqhe', scores, vc)
    w_f = jnp.exp(lf[None, :] * (C - 1 - pos)[:, None]).astype(dt)
    w_b = jnp.exp(lb[None, :] * pos[:, None]).astype(dt)
    kv_f = jnp.einsum('bnkhd,kh,bnkhe->nbhde', kc, w_f, vc)
    kv_b = jnp.einsum('bnkhd,kh,bnkhe->nbhde', kc, w_b, vc)
    dec_f = jnp.exp(lf * C).astype(kv_f.dtype)[None, :, None, None]
    dec_b = jnp.exp(lb * C).astype(kv_b.dtype)[None, :, None, None]
    zero = jnp.zeros((B, H, dk, dv), kv_f.dtype)

    def step_f(R, kv):
        return dec_f * R + kv, R

    def step_b(L, kv):
        return dec_b * L + kv, L

    _, R_f = lax.scan(step_f, zero, kv_f)
    _, L_b = lax.scan(step_b, zero, kv_b, reverse=True)
    q_f = jnp.exp(lf[None, :] * (pos + 1.0)[:, None]).astype(dt)
    q_b = jnp.exp(lb[None, :] * (C - pos)[:, None]).astype(dt)
    inter = (jnp.einsum('bnqhd,qh,nbhde->bnqhe', qc, q_f, R_f)
             + jnp.einsum('bnqhd,qh,nbhde->bnqhe', qc, q_b, L_b))
    return (intra + inter).reshape(B, S, H, dv)


def diff_attention(q, k, v, lam, rel_bias):
    B, S, H, _, dh = q.shape
    nb = S // Q_BLOCK
    qb = q.reshape(B, nb, Q_BLOCK, H, 2, dh).transpose(1, 0, 2, 3, 4, 5)
    starts = jnp.arange(nb, dtype=jnp.int32) * Q_BLOCK
    kpos = jnp.arange(S, dtype=jnp.int32)
    scale = dh ** -0.5

    def block(args):
        qblk, s0 = args
        qpos = s0 + jnp.arange(Q_BLOCK, dtype=jnp.int32)
        bias = rel_bias[t5_bucket(kpos[None, :] - qpos[:, None])]
        bias = bias.transpose(2, 0, 1).astype(jnp.float32)
        s = jnp.einsum('bqhcd,bkhcd->bhcqk', qblk, k).astype(jnp.float32) * scale + bias[None, :, None]
        p = jax.nn.softmax(s, axis=-1)
        a = p[:, :, 0] - lam * p[:, :, 1]
        return jnp.einsum('bhqk,bkhe->bqhe', a.astype(v.dtype), v)

    o = lax.map(block, (qb, starts))
    return o.transpose(1, 0, 2, 3, 4).reshape(B, S, H, 2 * dh)


def even_layer(x, g, w_in, ret_decay, lq1, lk1, lq2, lk2, dnorm_g, w_out, rel_bias, layer_idx):
    B, S, _ = x.shape
    h = rms_norm(x, g)
    u = h @ w_in
    rq, rk, rv, rg, dq, dk, dv, dg = jnp.split(u, EVEN_SPLIT_IDX, axis=-1)
    pos = jnp.arange(S)
    rq = rope(rq.reshape(B, S, N_RET_HEADS, RET_DK), pos)
    rk = rope(rk.reshape(B, S, N_RET_HEADS, RET_DK), pos) * (RET_DK ** -0.5)
    rv = rv.reshape(B, S, N_RET_HEADS, RET_DV)
    log_g = -jnp.exp(ret_decay.astype(jnp.float32))
    ro = head_rms(bidir_retention(rq, rk, rv, log_g)).reshape(B, S, RET_W)
    ro = ro * jax.nn.silu(rg)
    lam_init = 0.8 - 0.6 * math.exp(-0.3 * layer_idx)
    lam = (jnp.exp(jnp.sum(lq1.astype(jnp.float32) * lk1.astype(jnp.float32)))
           - jnp.exp(jnp.sum(lq2.astype(jnp.float32) * lk2.astype(jnp.float32))) + lam_init)
    dq = dq.reshape(B, S, N_DIFF_HEADS, 2, DIFF_DH)
    dk = dk.reshape(B, S, N_DIFF_HEADS, 2, DIFF_DH)
    dv = dv.reshape(B, S, N_DIFF_HEADS, 2 * DIFF_DH)
    do = diff_attention(dq, dk, dv, lam, rel_bias)
    do = (rms_norm(do, dnorm_g) * (1.0 - lam_init)).reshape(B, S, DIFF_W)
    do = do * jax.nn.silu(dg)
    return x + jnp.concatenate([ro, do], axis=-1) @ w_out


def odd_layer(x, g, w_in, conv_w, conv_b, cn_g, cn_b, w_out):
    h = rms_norm(x, g)
    u = h @ w_in
    a, b, gate = jnp.split(u, [CONV_W, 2 * CONV_W], axis=-1)
    glu = a * jax.nn.sigmoid(b)
    y = lax.conv_general_dilated(glu, conv_w[:, None, :].astype(glu.dtype), window_strides=(1,),
                                 padding=((CONV_K // 2, CONV_K // 2),),
                                 dimension_numbers=('NWC', 'WIO', 'NWC'),
                                 feature_group_count=CONV_W) + conv_b
    y = jax.nn.silu(layer_norm(y, cn_g, cn_b))
    y = y * jax.nn.silu(gate)
    return x + y @ w_out


def run_trunk(x, norm_g, final_g, rel_bias, w_in_mix, ret_decay, lam_q1, lam_k1, lam_q2, lam_k2,
              diff_norm_g, w_out_mix, w_in_conv, conv_w, conv_b, conv_norm_g, conv_norm_b, w_out_conv):
    for l in range(DEPTH):
        if l % 2 == 0:
            e = l // 2
            x = even_layer(x, norm_g[l], w_in_mix[e], ret_decay[e], lam_q1[e], lam_k1[e], lam_q2[e],
                           lam_k2[e], diff_norm_g[e], w_out_mix[e], rel_bias, l)
        else:
            o = l // 2
            x = odd_layer(x, norm_g[l], w_in_conv[o], conv_w[o], conv_b[o], conv_norm_g[o],
                          conv_norm_b[o], w_out_conv[o])
    return rms_norm(x, final_g)


def setup_inputs(seed: int = 0) -> dict:
    key = jax.random.key(seed)
    ks = jax.random.split(key, 20)
    f32 = jnp.float32
    nrm = lambda k, s, sc: jax.random.normal(k, s, f32) * sc
    base_decay = jnp.log(-jnp.log(1.0 - 2.0 ** (-5.0 - jnp.arange(N_RET_HEADS, dtype=f32))))
    return {
        "x_prompt": nrm(ks[0], (BATCH, SEQ, D_MODEL), 1.0),
        "x_sample": nrm(ks[1], (DEC_BATCH, DEC_SEQ, D_MODEL), 1.0),
        "norm_g": 1.0 + nrm(ks[2], (DEPTH, D_MODEL), 0.02),
        "final_g": 1.0 + nrm(ks[3], (D_MODEL,), 0.02),
        "rel_bias": nrm(ks[4], (T5_BUCKETS, N_DIFF_HEADS), 0.5),
        "w_in_mix": nrm(ks[5], (N_EVEN, D_MODEL, EVEN_IN_W), D_MODEL ** -0.5),
        "ret_decay": base_decay[None, None, :] + nrm(ks[6], (N_EVEN, 2, N_RET_HEADS), 0.05),
        "lam_q1": nrm(ks[7], (N_EVEN, DIFF_DH), 0.1),
        "lam_k1": nrm(ks[8], (N_EVEN, DIFF_DH), 0.1),
        "lam_q2": nrm(ks[9], (N_EVEN, DIFF_DH), 0.1),
        "lam_k2": nrm(ks[10], (N_EVEN, DIFF_DH), 0.1),
        "diff_norm_g": 1.0 + nrm(ks[11], (N_EVEN, 2 * DIFF_DH), 0.02),
        "w_out_mix": nrm(ks[12], (N_EVEN, MIX_W, D_MODEL), MIX_W ** -0.5),
        "w_in_conv": nrm(ks[13], (N_ODD, D_MODEL, 3 * CONV_W), D_MODEL ** -0.5),
        "conv_w": nrm(ks[14], (N_ODD, CONV_K, CONV_W), CONV_K ** -0.5),
        "conv_b": nrm(ks[15], (N_ODD, CONV_W), 0.02),
        "conv_norm_g": 1.0 + nrm(ks[16], (N_ODD, CONV_W), 0.02),
        "conv_norm_b": nrm(ks[17], (N_ODD, CONV_W), 0.02),
        "w_out_conv": nrm(ks[18], (N_ODD, CONV_W, D_MODEL), CONV_W ** -0.5),
    }


def reference(x_prompt, x_sample, norm_g, final_g, rel_bias, w_in_mix, ret_decay, lam_q1, lam_k1,
              lam_q2, lam_k2, diff_norm_g, w_out_mix, w_in_conv, conv_w, conv_b, conv_norm_g,
              conv_norm_b, w_out_conv):
    y_prompt = run_trunk(x_prompt, norm_g, final_g, rel_bias, w_in_mix, ret_decay, lam_q1, lam_k1,
                         lam_q2, lam_k2, diff_norm_g, w_out_mix, w_in_conv, conv_w, conv_b,
                         conv_norm_g, conv_norm_b, w_out_conv)
    y_sample = run_trunk(x_sample, norm_g, final_g, rel_bias, w_in_mix, ret_decay, lam_q1, lam_k1,
                         lam_q2, lam_k2, diff_norm_g, w_out_mix, w_in_conv, conv_w, conv_b,
                         conv_norm_g, conv_norm_b, w_out_conv)
    return (y_prompt, y_sample)
```

```python
import contextlib
import math
import numpy as np
import concourse.bass as bass
import concourse.mybir as mybir
from concourse.bass_utils import run_bass_kernel_spmd

F32 = mybir.dt.float32
BF16 = mybir.dt.bfloat16
AF = mybir.ActivationFunctionType
ALU = mybir.AluOpType
AX = mybir.AxisListType

D = 1024
EPS = 1e-6


class Res:
    __slots__ = ("name", "w", "r", "multi")

    def __init__(self, name, multi=False):
        self.name = name
        self.w = []
        self.r = []
        self.multi = multi


class Sched:
    COMPUTE = ("pe", "act", "dve", "pool")
    ALL = ("pe", "act", "dve", "pool", "sp")

    def __init__(self, nc):
        self.nc = nc
        self.streams = {e: [] for e in self.ALL}
        self.tick = {e: 0 for e in self.COMPUTE}
        self.esem = {}
        self.old_esems = []
        self.nsem = 0
        self.waited = {e: {} for e in self.ALL}
        for e in self.COMPUTE:
            self._new_esem(e)
        self.chan = {}
        self.final_tokens = []
        self.ninst = 0

    def _alloc(self, name):
        self.nsem += 1
        return self.nc.alloc_semaphore(name=name)

    def _new_esem(self, e):
        if e in self.esem and self.tick[e] > 0:
            self.old_esems.append((self.esem[e], self.tick[e]))
        self.esem[e] = self._alloc("e_%s_%d" % (e, self.nsem))
        self.tick[e] = 0

    def _deps(self, reads, writes):
        toks = []
        for r in reads:
            toks.extend(r.w)
        for w in writes:
            toks.extend(w.w)
            toks.extend(w.r)
        return toks

    def _emit_waits(self, eng, toks, skip_self=False):
        wd = self.waited[eng]
        best = {}
        for (s, v) in toks:
            k = id(s)
            if skip_self and eng in self.esem and s is self.esem[eng]:
                continue
            if wd.get(k, 0) >= v:
                continue
            if best.get(k, (None, 0))[1] < v:
                best[k] = (s, v)
        for k, (s, v) in best.items():
            wd[k] = v
            self.streams[eng].append(("wait", s, v))
            self.ninst += 1

    def _commit(self, tok, reads, writes):
        for r in reads:
            r.r.append(tok)
            if len(r.r) > 64:
                r.r = self._compress(r.r)
        for w in writes:
            if w.multi:
                w.w.append(tok)
                if len(w.w) > 64:
                    w.w = self._compress(w.w)
            else:
                w.w = [tok]
                w.r = []

    @staticmethod
    def _compress(toks):
        best = {}
        for (s, v) in toks:
            if best.get(id(s), (None, 0))[1] < v:
                best[id(s)] = (s, v)
        return list(best.values())

    def op(self, eng, fn, reads=(), writes=(), inc=True, skip_self=False):
        toks = self._deps(reads, writes)
        self._emit_waits(eng, toks, skip_self=skip_self)
        if inc:
            self.tick[eng] += 1
            tok = (self.esem[eng], self.tick[eng])
            self.streams[eng].append(("op", fn, self.esem[eng]))
        else:
            tok = (self.esem[eng], self.tick[eng] + 1)
            self.streams[eng].append(("op", fn, None))
        self.ninst += 1
        self._commit(tok, reads, writes)
        return tok

    def dma(self, q, chan, fn, reads=(), writes=(), final=False):
        toks = self._deps(reads, writes)
        if chan == "su":
            toks = [t for t in toks if t[1] != INF]
        self._emit_waits(q, toks)
        if chan not in self.chan:
            self.chan[chan] = [self._alloc("c_" + chan), 0]
        c = self.chan[chan]
        c[1] += 16
        tok = (c[0], INF if chan == "su" else c[1])
        self.streams[q].append(("dma", fn, c[0]))
        self.ninst += 1
        self._commit(tok, reads, writes)
        if final:
            self.final_tokens.append(tok)
        return tok

    def all_tokens(self):
        toks = [(self.esem[e], self.tick[e]) for e in self.COMPUTE if self.tick[e] > 0]
        toks += list(self.old_esems)
        toks += [(c[0], c[1]) for k, c in self.chan.items() if c[1] > 0 and k != "su"]
        return toks

    def barrier(self):
        toks = self.all_tokens()
        for e in self.ALL:
            self._emit_waits(e, toks)
        for e in self.COMPUTE:
            if self.tick[e] > 0:
                self._new_esem(e)

    def finish(self):
        self._emit_waits("sp", self.final_tokens + self.all_tokens())

    def replay(self):
        nc = self.nc
        streams = self.streams
        with nc.Block() as block:
            def run(name, eng):
                for it in streams[name]:
                    if it[0] == "wait":
                        v = it[2]
                        if v == INF:
                            v = self.chan["su"][1]
                        eng.wait_ge(it[1], v)
                    elif it[0] == "op":
                        nm, a, k = it[1]
                        ins = getattr(eng, nm)(*a, **k)
                        if it[2] is not None:
                            ins.then_inc(it[2], 1)
                    else:
                        nm, a, k = it[1]
                        getattr(eng, nm)(*a, **k).then_inc(it[2], 16)

            @block.tensor
            def _(e):
                run("pe", e)

            @block.scalar
            def _(e):
                run("act", e)

            @block.vector
            def _(e):
                run("dve", e)

            @block.gpsimd
            def _(e):
                run("pool", e)

            @block.sync
            def _(e):
                run("sp", e)


def I_(name, *a, **k):
    return (name, a, k)


INF = float("inf")


class Ring:
    def __init__(self, items):
        self.items = items
        self.i = 0

    def slot(self):
        return (self.i - 1) % len(self.items)

    def next(self):
        x = self.items[self.i % len(self.items)]
        self.i += 1
        return x


def _t5_bucket_np(rel):
    nb = 16
    max_exact = 8
    ret = (rel > 0).astype(np.int32) * nb
    n = np.abs(rel)
    nf = np.maximum(n, 1).astype(np.float32)
    large = max_exact + (np.log(nf / np.float32(max_exact)) / np.float32(math.log(128 / max_exact))
                         * np.float32(nb - max_exact)).astype(np.int32)
    large = np.minimum(large, nb - 1)
    return ret + np.where(n < max_exact, n, large)


def _consts():
    j = np.arange(1280)
    rel = 639 - j
    b = _t5_bucket_np(rel.astype(np.int32))
    et = np.zeros((32, 1280), np.float32)
    et[b, j] = 1.0
    et[:, 1279] = 0.0
    k = np.arange(128, dtype=np.float32)
    dpos = np.maximum(k[None, :] - k[:, None], 0.0).astype(np.float32)
    dneg = np.maximum(k[:, None] - k[None, :], 0.0).astype(np.float32)
    col = np.stack([127.0 - k, k], axis=1).astype(np.float32)
    rowf = np.broadcast_to((k + 1.0)[None, :], (128, 128)).astype(np.float32).copy()
    rowb = np.broadcast_to((128.0 - k)[None, :], (128, 128)).astype(np.float32).copy()
    return dict(c_et=et, c_dpos=dpos, c_dneg=dneg, c_col=col, c_rowf=rowf, c_rowb=rowb)


def _rope_table(pos):
    inv = (1.0 / (np.float32(10000.0) ** (np.arange(0, 128, 2, dtype=np.float32) / np.float32(128)))).astype(np.float32)
    ang = pos.astype(np.float32)[:, None] * inv[None, :]
    return np.concatenate([np.cos(ang), np.sin(ang)], axis=1).astype(np.float32)


def build_program(NBA, NRB, NOB, debug=False):
    nc = bass.Bass("TRN2", target_bir_lowering=False)
    S = Sched(nc)
    SA = NBA * 128
    SRB = NRB * 128
    SBT = (NRB + NOB) * 128
    seqs = {"a": dict(nreg=NBA, noth=0, sreg=SA, sctx=SA), "b": dict(nreg=NRB, noth=NOB, sreg=SRB, sctx=SBT)}

    def din(name, shape, dt=F32):
        return nc.dram_tensor(name, list(shape), dt, kind="ExternalInput").ap()

    def dscr(name, shape, dt=BF16):
        return nc.dram_tensor(name, list(shape), dt, kind="Internal").ap()

    I = {}
    I["xa"] = din("xa", [SA, D]); I["xb"] = din("xb", [SBT, D])
    I["csa"] = din("csa", [SA, 128]); I["csb"] = din("csb", [SBT, 128])
    I["fl"] = din("fl", [1, max(NOB, 1)])
    I["norm_g"] = din("norm_g", [2, D]); I["final_g"] = din("final_g", [1, D])
    I["rel_bias"] = din("rel_bias", [32, 8])
    I["w_in_mix"] = din("w_in_mix", [D, 7168]); I["ret_decay"] = din("ret_decay", [1, 8])
    I["lam4"] = din("lam4", [1, 256]); I["diff_norm_g"] = din("diff_norm_g", [1, 128])
    I["w_out_mix"] = din("w_out_mix", [2048, D]); I["w_in_conv"] = din("w_in_conv", [D, 6144])
    I["conv_w"] = din("conv_w", [31, 2048]); I["conv_b"] = din("conv_b", [1, 2048])
    I["conv_norm_g"] = din("conv_norm_g", [1, 2048]); I["conv_norm_b"] = din("conv_norm_b", [1, 2048])
    I["w_out_conv"] = din("w_out_conv", [2048, D])
    I["c_et"] = din("c_et", [32, 1280]); I["c_dpos"] = din("c_dpos", [128, 128]); I["c_dneg"] = din("c_dneg", [128, 128])
    I["c_col"] = din("c_col", [128, 2]); I["c_rowf"] = din("c_rowf", [128, 128]); I["c_rowb"] = din("c_rowb", [128, 128])
    O = {"a": nc.dram_tensor("out_a", [SA, D], F32, kind="ExternalOutput").ap(),
         "b": nc.dram_tensor("out_b", [SRB, D], F32, kind="ExternalOutput").ap()}
    X = {"a": I["xa"], "b": I["xb"]}
    CS = {"a": I["csa"], "b": I["csb"]}

    SC = {}
    for s, q in seqs.items():
        SC[s] = dict(
            KT=dscr("KT" + s, [8, 128, q["sctx"]]), QT=dscr("QT" + s, [8, 128, q["sreg"]]),
            DVH=dscr("DVH" + s, [8, 128, q["sctx"] // 128, 128]), RV=dscr("RV" + s, [q["sctx"], 1024]),
            RK=dscr("RK" + s, [q["sctx"], 512]), RKT=dscr("RKT" + s, [4, 128, q["sreg"]]),
            RQT=dscr("RQT" + s, [4, 128, q["sreg"]]), GT=dscr("GT" + s, [q["sreg"], 2048]),
            Y=dscr("Y" + s, [q["sreg"], 2048]), X1=dscr("X1" + s, [q["sreg"], D], F32),
            GLU=dscr("GLU" + s, [2048, q["sreg"]]), SG=dscr("SG" + s, [2048, q["sreg"]]))
        SC[s]["res"] = {k: Res(k + s, multi=True) for k in list(SC[s].keys())}
    GREV = dscr("GREV", [8, 1280]); r_grev = Res("grev", multi=True)

    DBG = {}

    def dbg_out(name, shape, dt=F32):
        DBG[name] = nc.dram_tensor("dbg_" + name, list(shape), dt, kind="ExternalOutput").ap()
        return DBG[name]

    def bc_ap(src, off, n, npart=128):
        return bass.AP(src.tensor, off, [[0, npart], [1, n]])

    persist = contextlib.ExitStack()

    def sb(stack, name, shape, dt):
        return stack.enter_context(nc.sbuf_tensor(name, list(shape), dt))

    PS = []
    for i in range(8):
        t = nc.alloc_psum_tensor("psb%d" % i, [128, 512], F32)
        PS.append((t, Res("psb%d" % i)))

    qsel = {"i": 0}

    def ldq():
        return "sp"

    P = {}

    def ptile(name, shape, dt):
        t = sb(persist, name, shape, dt)
        P[name] = (t, Res(name))
        return P[name]

    ident_f, r_identf = ptile("ident_f", [128, 128], F32)
    ident_b, r_identb = ptile("ident_b", [128, 128], BF16)
    jmat_f, r_jf = ptile("jmat_f", [128, 128], F32)
    jmat_b, r_jb = ptile("jmat_b", [128, 128], BF16)
    ones_b, r_ones = ptile("ones_b", [128, 128], BF16)
    zero_c, r_zero = ptile("zero_c", [128, 1], F32)
    epsc, r_epsc = ptile("epsc", [128, 4], F32)
    g1, r_g1 = ptile("g1", [128, D], F32)
    g2, r_g2 = ptile("g2", [128, D], F32)
    gfin, r_gf = ptile("gfin", [128, D], F32)
    lamc, r_lam = ptile("lamc", [128, 4], F32)
    lf, r_lf = ptile("lf", [128, 8], F32)
    dec, r_dec = ptile("dec", [128, 8], F32)
    dmatT, r_dmat = ptile("dmatT", [128, 4, 128], F32)
    wfb, r_wfb = ptile("wfb", [128, 8], F32)
    QF, r_QF = ptile("QF", [128, 4, 128], F32)
    QB, r_QB = ptile("QB", [128, 4, 128], F32)
    farb, r_farb = ptile("farb", [128, 16], F32)
    dng, r_dng = ptile("dng", [128, 128], F32)
    NO1 = max(NOB, 1)
    flb, r_flb = ptile("flb", [128, NO1], F32)
    fla, r_fla = ptile("fla", [128, NO1], F32)
    oth_af, r_oaf = ptile("oth_af", [128, NO1, 4], F32)
    oth_ab, r_oab = ptile("oth_ab", [128, NO1, 4], F32)
    oth_fb, r_ofb = ptile("oth_fb", [128, NO1, 8], F32)
    cwT, r_cwT = ptile("cwT", [128, 16, 31], F32)
    cvb, r_cvb = ptile("cvb", [128, 16], F32)
    cng, r_cng = ptile("cng", [128, 16], F32)
    cnb, r_cnb = ptile("cnb", [128, 16], F32)

    import os as _os2
    _CUT = int(_os2.environ.get('K_SU', '99'))

    def _cut(k):
        return k >= _CUT

    def setup():
        st = contextlib.ExitStack()
        tmp = sb(st, "su_tmp", [128, 1280], F32); r_tmp = Res("su_tmp")
        tmp2 = sb(st, "su_tmp2", [128, 1280], F32); r_tmp2 = Res("su_tmp2")
        S.op("pool", I_("memset", ident_f[:], 1.0), writes=[r_identf])
        S.op("pool", I_("affine_select", out=ident_f[:], in_=ident_f[:], pattern=[[-1, 128]],
                                               compare_op=ALU.is_equal, fill=0.0, base=0, channel_multiplier=1),
             reads=[r_identf], writes=[r_identf])
        S.op("dve", I_("tensor_copy", out=ident_b[:], in_=ident_f[:]), reads=[r_identf], writes=[r_identb])
        S.op("pool", I_("memset", jmat_f[:], 1.0), writes=[r_jf])
        S.op("pool", I_("affine_select", out=jmat_f[:], in_=jmat_f[:], pattern=[[1, 128]],
                                               compare_op=ALU.is_equal, fill=0.0, base=-127, channel_multiplier=1),
             reads=[r_jf], writes=[r_jf])
        S.op("dve", I_("tensor_copy", out=jmat_b[:], in_=jmat_f[:]), reads=[r_jf], writes=[r_jb])
        S.op("dve", I_("memset", ones_b[:], 1.0), writes=[r_ones])
        S.op("dve", I_("memset", zero_c[:], 0.0), writes=[r_zero])
        for ci, cv in enumerate((D * EPS, 256 * EPS, 128 * EPS, EPS)):
            S.op("dve", I_("memset", epsc[:, ci:ci + 1], cv), writes=[r_epsc])
        if _cut(1):
            S.barrier(); st.close(); return
        for (t, r, src, off) in ((g1, r_g1, I["norm_g"], 0), (g2, r_g2, I["norm_g"], D), (gfin, r_gf, I["final_g"], 0)):
            S.dma("sp", "su", I_("dma_start", out=t[:], in_=bc_ap(src, off, D)), writes=[r])
            S.op("dve", I_("tensor_scalar", out=t[:], in0=t[:], scalar1=32.0, scalar2=None, op0=ALU.mult),
                 reads=[r], writes=[r])
        if _cut(2):
            S.barrier(); st.close(); return
        S.dma("sp", "su", I_("dma_start", out=tmp[:, 0:256], in_=bc_ap(I["lam4"], 0, 256)), writes=[r_tmp])
        S.op("dve", I_("tensor_tensor", out=tmp2[:, 0:64], in0=tmp[:, 0:64], in1=tmp[:, 64:128], op=ALU.mult), reads=[r_tmp], writes=[r_tmp2])
        S.op("dve", I_("tensor_tensor", out=tmp2[:, 64:128], in0=tmp[:, 128:192], in1=tmp[:, 192:256], op=ALU.mult), reads=[r_tmp], writes=[r_tmp2])
        S.op("dve", I_("reduce_sum", out=tmp2[:, 128:130], in_=tmp2[:, 0:128].rearrange("p (a b) -> p a b", a=2), axis=AX.X),
             reads=[r_tmp2], writes=[r_tmp2])
        S.op("act", I_("activation", out=tmp2[:, 130:132], in_=tmp2[:, 128:130], func=AF.Exp), reads=[r_tmp2], writes=[r_tmp2])
        S.op("dve", I_("tensor_tensor", out=lamc[:, 0:1], in0=tmp2[:, 130:131], in1=tmp2[:, 131:132], op=ALU.subtract),
             reads=[r_tmp2], writes=[r_lam])
        S.op("dve", I_("tensor_scalar", out=lamc[:, 0:1], in0=lamc[:, 0:1], scalar1=0.2, scalar2=None, op0=ALU.add), reads=[r_lam], writes=[r_lam])
        S.op("dve", I_("tensor_scalar", out=lamc[:, 1:2], in0=lamc[:, 0:1], scalar1=-1.0, scalar2=None, op0=ALU.mult), reads=[r_lam], writes=[r_lam])
        if _cut(3):
            S.barrier(); st.close(); return
        S.dma("sp", "su", I_("dma_start", out=lf[:], in_=bc_ap(I["ret_decay"], 0, 8)), writes=[r_lf])
        S.op("act", I_("activation", out=lf[:], in_=lf[:], func=AF.Exp), reads=[r_lf], writes=[r_lf])
        S.op("dve", I_("tensor_scalar", out=lf[:], in0=lf[:], scalar1=-1.0, scalar2=None, op0=ALU.mult), reads=[r_lf], writes=[r_lf])
        S.op("act", I_("activation", out=dec[:], in_=lf[:], func=AF.Exp, scale=128.0), reads=[r_lf], writes=[r_dec])
        if _cut(4):
            S.barrier(); st.close(); return
        dpos = sb(st, "su_dpos", [128, 128], F32); dneg = sb(st, "su_dneg", [128, 128], F32)
        col = sb(st, "su_col", [128, 2], F32); rowf = sb(st, "su_rowf", [128, 128], F32); rowb = sb(st, "su_rowb", [128, 128], F32)
        r_c = Res("su_consts")
        for (t, nm) in ((dpos, "c_dpos"), (dneg, "c_dneg"), (col, "c_col"), (rowf, "c_rowf"), (rowb, "c_rowb")):
            S.dma("sp", "su", I_("dma_start", out=t[:], in_=I[nm]), writes=[r_c])
        SCL = 128.0 ** -0.5
        for h in range(4):
            if 'd' not in _os2.environ.get('K_SKIP', ''):
                S.op("dve", I_("tensor_scalar", out=tmp[:, 0:128], in0=dpos[:], scalar1=lf[:, h:h + 1], scalar2=None, op0=ALU.mult),
                     reads=[r_c, r_lf], writes=[r_tmp])
                S.op("dve", I_("scalar_tensor_tensor", out=tmp[:, 0:128], in0=dneg[:], scalar=lf[:, 4 + h:5 + h], in1=tmp[:, 0:128],
                                                                        op0=ALU.mult, op1=ALU.add), reads=[r_c, r_lf, r_tmp], writes=[r_tmp])
                S.op("act", I_("activation", out=dmatT[:, h, :], in_=tmp[:, 0:128], func=AF.Exp), reads=[r_tmp], writes=[r_dmat])
            if 'q' not in _os2.environ.get('K_SKIP', ''):
                S.op("act", I_("activation", out=QF[:, h, :], in_=rowf[:], func=AF.Exp, scale=lf[:, h:h + 1]), reads=[r_c, r_lf], writes=[r_QF])
                S.op("act", I_("activation", out=QB[:, h, :], in_=rowb[:], func=AF.Exp, scale=lf[:, 4 + h:5 + h]), reads=[r_c, r_lf], writes=[r_QB])
        S.op("dve", I_("tensor_scalar", out=dmatT[:], in0=dmatT[:], scalar1=SCL, scalar2=None, op0=ALU.mult), reads=[r_dmat], writes=[r_dmat])
        if 'w' not in _os2.environ.get('K_SKIP', ''):
            S.op("act", I_("activation", out=wfb[:, 0:4], in_=lf[:, 0:4], func=AF.Exp, scale=col[:, 0:1]), reads=[r_c, r_lf], writes=[r_wfb])
            S.op("act", I_("activation", out=wfb[:, 4:8], in_=lf[:, 4:8], func=AF.Exp, scale=col[:, 1:2]), reads=[r_c, r_lf], writes=[r_wfb])
        S.op("dve", I_("tensor_scalar", out=wfb[:], in0=wfb[:], scalar1=SCL, scalar2=None, op0=ALU.mult), reads=[r_wfb], writes=[r_wfb])
        if _cut(5):
            S.barrier(); st.close(); return
        S.dma("sp", "su", I_("dma_start", out=farb[:, 0:8], in_=bc_ap(I["rel_bias"], 15 * 8, 8)), writes=[r_farb])
        S.dma("sp", "su", I_("dma_start", out=farb[:, 8:16], in_=bc_ap(I["rel_bias"], 31 * 8, 8)), writes=[r_farb])
        S.dma("sp", "su", I_("dma_start", out=dng[:], in_=bc_ap(I["diff_norm_g"], 0, 128)), writes=[r_dng])
        S.op("dve", I_("tensor_scalar", out=dng[:], in0=dng[:], scalar1=0.8, scalar2=None, op0=ALU.mult), reads=[r_dng], writes=[r_dng])
        S.dma("sp", "su", I_("dma_start", out=flb[:], in_=bc_ap(I["fl"], 0, NO1)), writes=[r_flb])
        S.op("dve", I_("tensor_scalar", out=fla[:], in0=flb[:], scalar1=-1.0, scalar2=1.0, op0=ALU.mult, op1=ALU.add), reads=[r_flb], writes=[r_fla])
        for h in range(4):
            S.op("dve", I_("scalar_tensor_tensor", out=oth_af[:, :, h], in0=flb[:], scalar=dec[:, h:h + 1], in1=fla[:],
                                                                    op0=ALU.mult, op1=ALU.add), reads=[r_flb, r_fla, r_dec], writes=[r_oaf])
            S.op("dve", I_("scalar_tensor_tensor", out=oth_ab[:, :, h], in0=fla[:], scalar=dec[:, 4 + h:5 + h], in1=flb[:],
                                                                    op0=ALU.mult, op1=ALU.add), reads=[r_flb, r_fla, r_dec], writes=[r_oab])
        for h in range(8):
            S.op("dve", I_("tensor_scalar", out=tmp[:, 0:NO1], in0=fla[:], scalar1=farb[:, 8 + h:9 + h], scalar2=None, op0=ALU.mult),
                 reads=[r_fla, r_farb], writes=[r_tmp])
            S.op("dve", I_("scalar_tensor_tensor", out=oth_fb[:, :, h], in0=flb[:], scalar=farb[:, h:h + 1], in1=tmp[:, 0:NO1],
                                                                    op0=ALU.mult, op1=ALU.add), reads=[r_flb, r_farb, r_tmp], writes=[r_ofb])
        if _cut(6):
            S.barrier(); st.close(); return
        cw_sb = sb(st, "su_cw", [31, 2048], F32); r_cw = Res("su_cw")
        S.dma("sp", "su", I_("dma_start", out=cw_sb[:], in_=I["conv_w"]), writes=[r_cw])
        for cb in range(16):
            pt, pr = PS[3 + (cb % 4)]
            S.op("pe", I_("transpose", pt[:, 0:31], cw_sb[:, cb * 128:(cb + 1) * 128], ident_f[0:31, 0:31]),
                 reads=[r_cw, r_identf], writes=[pr])
            S.op("act", I_("activation", out=cwT[:, cb, :], in_=pt[:, 0:31], func=AF.Copy), reads=[pr], writes=[r_cwT])
        with nc.allow_non_contiguous_dma(reason="tiny per-channel params"):
            for (t, r, nm) in ((cvb, r_cvb, "conv_b"), (cng, r_cng, "conv_norm_g"), (cnb, r_cnb, "conv_norm_b")):
                S.dma("sp", "su", I_("dma_start", out=t[:], in_=bass.AP(I[nm].tensor, 0, [[1, 128], [128, 16]]), allow_slow_non_contiguous=True), writes=[r])
        if _cut(7):
            S.barrier(); st.close(); return
        rb = sb(st, "su_rb", [32, 8], F32); et = sb(st, "su_et", [32, 1280], F32); r_rb = Res("su_rb")
        S.dma("sp", "su", I_("dma_start", out=rb[:], in_=I["rel_bias"]), writes=[r_rb])
        S.dma("sp", "su", I_("dma_start", out=et[:], in_=I["c_et"]), writes=[r_rb])
        grev_sb = sb(st, "su_grev", [8, 1280], BF16); r_gs = Res("su_grev")
        for ci, (c0, cn) in enumerate(((0, 512), (512, 512), (1024, 256))):
            pt, pr = PS[ci]
            S.op("pe", I_("matmul", pt[0:8, 0:cn], lhsT=rb[:, :], rhs=et[:, c0:c0 + cn], start=True, stop=True),
                 reads=[r_rb], writes=[pr])
            S.op("act", I_("activation", out=grev_sb[:, c0:c0 + cn], in_=pt[0:8, 0:cn], func=AF.Copy),
                 reads=[pr], writes=[r_gs])
        S.dma("sp", "su2", I_("dma_start", out=GREV, in_=grev_sb[:]), reads=[r_gs], writes=[r_grev])
        S.barrier()
        st.close()

    def phase1():
        st = contextlib.ExitStack()
        W = sb(st, "p1_W", [128, 8, 7168], BF16); r_W = Res("p1_W")
        for j in range(8):
            S.dma("pool", "p1w", I_("dma_start", out=W[:, j, :], in_=I["w_in_mix"][j * 128:(j + 1) * 128, :]), writes=[r_W])
        xr = Ring([(sb(st, "p1_x%d" % i, [128, D], F32), Res("x")) for i in range(2)])
        csr = Ring([(sb(st, "p1_cs%d" % i, [128, 128], F32), Res("cs")) for i in range(2)])
        hr = Ring([(sb(st, "p1_h%d" % i, [128, D], BF16), Res("h")) for i in range(2)])
        junk = sb(st, "p1_junk", [128, D], BF16); r_junk = Res("junk")
        ssr = Ring([(sb(st, "p1_ss%d" % i, [128, 2], F32), Res("ss")) for i in range(2)])
        hTr = Ring([(sb(st, "p1_hT%d" % i, [128, 8, 512], BF16), Res("hT")) for i in range(1)])
        qkr = Ring([(sb(st, "p1_qk%d" % i, [128, 1024], F32), Res("qk")) for i in range(1)])
        rt = [(sb(st, "p1_rt%d" % i, [128, 512], F32), Res("rt")) for i in range(4)]
        ropr = Ring([(sb(st, "p1_rop%d" % i, [128, 1024], BF16), Res("rop")) for i in range(2)])
        rvr = Ring([(sb(st, "p1_rv%d" % i, [128, 1024], BF16), Res("rv")) for i in range(2)])
        gtr = Ring([(sb(st, "p1_gt%d" % i, [128, 2048], BF16), Res("gt")) for i in range(1)])
        dvr = Ring([(sb(st, "p1_dv%d" % i, [128, 4, 1024], BF16), Res("dv")) for i in range(1)])
        fmr = Ring([(sb(st, "p1_fm%d" % i, [128, 512], BF16), Res("fm")) for i in range(4)])
        qkTr = Ring([(sb(st, "p1_qkT%d" % i, [128, 8, 512], BF16), Res("qkT")) for i in range(1)])
        psT = Ring(PS[0:2]); pstm = Ring(PS[2:5]); psfm = Ring(PS[5:7]); psT2 = Ring(PS[7:8])

        for s in ("a", "b"):
            q = seqs[s]
            sc = SC[s]; rs = sc["res"]
            ngroups = (q["nreg"] + q["noth"]) // 4
            for g in range(ngroups):
                full = (g * 4) < q["nreg"]
                t0 = g * 512
                hT, r_hT = hTr.next()
                dv, r_dv = dvr.next()
                qkT, r_qkT = qkTr.next()
                for i in range(4):
                    tt = t0 + i * 128
                    x, r_x = xr.next()
                    cs, r_cs = csr.next()
                    S.dma("sp", "p1x%d" % xr.slot(), I_("dma_start", out=x[:], in_=X[s][tt:tt + 128, :]), writes=[r_x])
                    S.dma("sp", "p1c%d" % csr.slot(), I_("dma_start", out=cs[:], in_=CS[s][tt:tt + 128, :]), writes=[r_cs])
                    ss, r_ss = ssr.next()
                    S.op("act", I_("activation", out=junk[:], in_=x[:], func=AF.Square, accum_out=ss[:, 0:1]),
                         reads=[r_x], writes=[r_junk, r_ss])
                    S.op("act", I_("activation", out=ss[:, 1:2], in_=ss[:, 0:1], func=AF.Sqrt, bias=epsc[:, 0:1]), reads=[r_ss, r_epsc], writes=[r_ss])
                    S.op("dve", I_("reciprocal", out=ss[:, 1:2], in_=ss[:, 1:2]), reads=[r_ss], writes=[r_ss])
                    h, r_h = hr.next()
                    S.op("act", I_("activation", out=x[:], in_=x[:], func=AF.Copy, scale=ss[:, 1:2]), reads=[r_x, r_ss], writes=[r_x])
                    S.op("pool", I_("tensor_tensor", out=h[:], in0=x[:], in1=g1[:], op=ALU.mult), reads=[r_x, r_g1], writes=[r_h])
                    pt, pr = psT.next()
                    ptb = pt[:].bitcast(BF16)
                    for j in range(8):
                        S.op("pe", I_("transpose", ptb[:, j * 128:(j + 1) * 128], h[:, j * 128:(j + 1) * 128], ident_b[:]),
                             reads=[r_h, r_identb], writes=[pr], inc=(j == 7), skip_self=True)
                    S.op("act", I_("activation", out=hT[:, :, i * 128:(i + 1) * 128], in_=ptb.rearrange("p (j t) -> p j t", j=8), func=AF.Copy),
                         reads=[pr], writes=[r_hT])
                    if full:
                        cgs = [("qk", 0), ("qk", 512), ("rv", 1024), ("rv", 1536), ("rg", 2048), ("rg", 2560),
                               ("dv", 5120), ("dv", 5632), ("dg", 6144), ("dg", 6656)]
                    else:
                        cgs = [("qk", 512), ("rv", 1024), ("rv", 1536), ("dv", 5120), ("dv", 5632)]
                    qk, r_qk = qkr.next()
                    rv, r_rv = rvr.next()
                    gt, r_gt = gtr.next()
                    for (kind, c0) in cgs:
                        pm, pmr = pstm.next()
                        for j in range(8):
                            S.op("pe", I_("matmul", pm[:, :], lhsT=hT[:, j, i * 128:(i + 1) * 128], rhs=W[:, j, c0:c0 + 512],
                                                                                     start=(j == 0), stop=(j == 7)),
                                 reads=[r_hT, r_W], writes=[pmr], inc=(j == 7), skip_self=True)
                        if kind == "qk":
                            S.op("act", I_("activation", out=qk[:, c0:c0 + 512], in_=pm[:, :], func=AF.Copy), reads=[pmr], writes=[r_qk])
                        elif kind == "rv":
                            S.op("dve", I_("tensor_copy", out=rv[:, c0 - 1024:c0 - 512], in_=pm[:, :]), reads=[pmr], writes=[r_rv])
                        elif kind == "rg":
                            S.op("act", I_("activation", out=gt[:, c0 - 2048:c0 - 1536], in_=pm[:, :], func=AF.Silu), reads=[pmr], writes=[r_gt])
                        elif kind == "dg":
                            S.op("act", I_("activation", out=gt[:, c0 - 6144 + 1024:c0 - 6144 + 1536], in_=pm[:, :], func=AF.Silu), reads=[pmr], writes=[r_gt])
                        elif kind == "dv":
                            S.op("dve", I_("tensor_copy", out=dv[:, i, c0 - 5120:c0 - 5120 + 512], in_=pm[:, :]), reads=[pmr], writes=[r_dv])
                    S.dma("sp", "p1rv%d" % rvr.slot(), I_("dma_start", out=sc["RV"][tt:tt + 128, :], in_=rv[:]), reads=[r_rv], writes=[rs["RV"]])
                    if full:
                        S.dma("sp", "p1gt%d" % gtr.slot(), I_("dma_start", out=sc["GT"][tt:tt + 128, :], in_=gt[:]), reads=[r_gt], writes=[rs["GT"]])
                    hlo = 0 if full else 4
                    nh = 8 - hlo
                    rop, r_rop = ropr.next()
                    v4 = qk[:].rearrange("p (h c d) -> p h c d", h=8, c=2)
                    o4 = rop[:].rearrange("p (h c d) -> p h c d", h=8, c=2)
                    x1 = v4[:, hlo:8, 0, :]; x2 = v4[:, hlo:8, 1, :]
                    cosb = cs[:, 0:64].unsqueeze(1).to_broadcast([128, nh, 64]); sinb = cs[:, 64:128].unsqueeze(1).to_broadcast([128, nh, 64])
                    ta = [rt[k][0][:, 0:nh * 64].rearrange("p (h d) -> p h d", h=nh) for k in range(4)]
                    S.op("dve", I_("tensor_tensor", out=ta[0], in0=x1, in1=cosb, op=ALU.mult), reads=[r_qk, r_cs], writes=[rt[0][1]])
                    S.op("dve", I_("tensor_tensor", out=ta[1], in0=x2, in1=sinb, op=ALU.mult), reads=[r_qk, r_cs], writes=[rt[1][1]])
                    S.op("dve", I_("tensor_tensor", out=o4[:, hlo:8, 0, :], in0=ta[0], in1=ta[1], op=ALU.subtract),
                         reads=[rt[0][1], rt[1][1]], writes=[r_rop])
                    S.op("pool", I_("tensor_tensor", out=ta[2], in0=x1, in1=sinb, op=ALU.mult), reads=[r_qk, r_cs], writes=[rt[2][1]])
                    S.op("pool", I_("tensor_tensor", out=ta[3], in0=x2, in1=cosb, op=ALU.mult), reads=[r_qk, r_cs], writes=[rt[3][1]])
                    S.op("pool", I_("tensor_tensor", out=o4[:, hlo:8, 1, :], in0=ta[2], in1=ta[3], op=ALU.add),
                         reads=[rt[2][1], rt[3][1]], writes=[r_rop])
                    S.dma("sp", "p1rk%d" % ropr.slot(), I_("dma_start", out=sc["RK"][tt:tt + 128, :], in_=rop[:, 512:1024]), reads=[r_rop], writes=[rs["RK"]])
                    if full:
                        p2, p2r = psT2.next()
                        p2b = p2[:].bitcast(BF16)
                        for j in range(8):
                            S.op("pe", I_("transpose", p2b[:, j * 128:(j + 1) * 128], rop[:, j * 128:(j + 1) * 128], ident_b[:]),
                                 reads=[r_rop, r_identb], writes=[p2r], inc=(j == 7), skip_self=True)
                        S.op("dve", I_("tensor_copy", out=qkT[:, :, i * 128:(i + 1) * 128], in_=p2b.rearrange("p (j t) -> p j t", j=8)),
                             reads=[p2r], writes=[r_qkT])
                kb0 = g * 4
                for hh in range(8):
                    S.dma("sp", "p1dv%d" % dvr.slot(), I_("dma_start", out=sc["DVH"][hh, :, kb0:kb0 + 4, :], in_=dv[:, :, hh * 128:(hh + 1) * 128]),
                          reads=[r_dv], writes=[rs["DVH"]])
                if full:
                    S.dma("sp", "p1qkT%d" % qkTr.slot(), I_("dma_start", out=sc["RQT"][:, :, t0:t0 + 512].rearrange("h d t -> d h t"), in_=qkT[:, 0:4, :]),
                          reads=[r_qkT], writes=[rs["RQT"]])
                    S.dma("sp", "p1qkT%d" % qkTr.slot(), I_("dma_start", out=sc["RKT"][:, :, t0:t0 + 512].rearrange("h d t -> d h t"), in_=qkT[:, 4:8, :]),
                          reads=[r_qkT], writes=[rs["RKT"]])
                blocks = ([("dq", hh) for hh in range(8)] if full else []) + [("dk", hh) for hh in range(8)]
                for (kind, hh) in blocks:
                    c0 = (3072 if kind == "dq" else 4096) + hh * 128
                    pm, pmr = psfm.next()
                    for j in range(8):
                        S.op("pe", I_("matmul", pm[:, :], lhsT=W[:, j, c0:c0 + 128], rhs=hT[:, j, :], start=(j == 0), stop=(j == 7)),
                             reads=[r_hT, r_W], writes=[pmr], inc=(j == 7), skip_self=True)
                    fm, r_fm = fmr.next()
                    slot = fmr.slot()
                    if kind == "dq":
                        S.op("act", I_("activation", out=fm[:], in_=pm[:, :], func=AF.Copy, scale=0.125), reads=[pmr], writes=[r_fm])
                        S.dma("sp", "p1fm%d" % slot, I_("dma_start", out=sc["QT"][hh, :, t0:t0 + 512], in_=fm[:]), reads=[r_fm], writes=[rs["QT"]])
                    else:
                        S.op("dve", I_("tensor_copy", out=fm[:], in_=pm[:, :]), reads=[pmr], writes=[r_fm])
                        S.dma("sp", "p1fm%d" % slot, I_("dma_start", out=sc["KT"][hh, :, t0:t0 + 512], in_=fm[:]), reads=[r_fm], writes=[rs["KT"]])
        S.barrier()
        st.close()

    def phase2():
        st = contextlib.ExitStack()
        NRmax = max(NBA, NRB)
        R32 = sb(st, "p2_R", [128, 4, 256], F32); r_R = Res("R")
        L32 = sb(st, "p2_L", [128, 4, 256], F32); r_L = Res("L")
        Rbf = Ring([(sb(st, "p2_Rbf%d" % i, [128, 4, 256], BF16), Res("Rbf")) for i in range(2)])
        Lst = sb(st, "p2_Lst", [128, NRmax, 4, 256], BF16)
        r_Lst = [Res("Lst%d" % n) for n in range(NRmax)]
        rkr = Ring([(sb(st, "p2_rk%d" % i, [128, 512], BF16), Res("rk")) for i in range(3)])
        rvr = Ring([(sb(st, "p2_rv%d" % i, [128, 1024], BF16), Res("rv")) for i in range(3)])
        rkwr = Ring([(sb(st, "p2_rkw%d" % i, [128, 512], BF16), Res("rkw")) for i in range(2)])
        qTr = Ring([(sb(st, "p2_qT%d" % i, [128, 4, 128], BF16), Res("qT")) for i in range(2)])
        kTr = Ring([(sb(st, "p2_kT%d" % i, [128, 4, 128], BF16), Res("kT")) for i in range(2)])
        ATr = Ring([(sb(st, "p2_AT%d" % i, [128, 128], BF16), Res("AT")) for i in range(3)])
        qfr = Ring([(sb(st, "p2_qf%d" % i, [128, 128], BF16), Res("qf")) for i in range(3)])
        qbr = Ring([(sb(st, "p2_qb%d" % i, [128, 128], BF16), Res("qb")) for i in range(3)])
        kvt = Ring([(sb(st, "p2_kvt%d" % i, [128, 256], F32), Res("kvt")) for i in range(2)])
        ssr = Ring([(sb(st, "p2_ss%d" % i, [128, 2], F32), Res("ss")) for i in range(4)])
        junk = sb(st, "p2_junk", [128, 256], BF16); r_junk = Res("junk")
        ysr = Ring([(sb(st, "p2_ys%d" % i, [128, 1024], BF16), Res("ys")) for i in range(2)])
        ps_s = Ring(PS[0:2]); ps_o = Ring(PS[2:5]); ps_kv = Ring(PS[5:8])

        def load_kv(s, tt):
            rk, r_rk = rkr.next(); rv, r_rv = rvr.next()
            S.dma("sp", "p2rk%d" % rkr.slot(), I_("dma_start", out=rk[:], in_=SC[s]["RK"][tt:tt + 128, :]), reads=[SC[s]["res"]["RK"]], writes=[r_rk])
            S.dma("sp", "p2rv%d" % rvr.slot(), I_("dma_start", out=rv[:], in_=SC[s]["RV"][tt:tt + 128, :]), reads=[SC[s]["res"]["RV"]], writes=[r_rv])
            return rk, r_rk, rv, r_rv

        def kv_summ(rk, r_rk, rv, r_rv, wofs):
            rkw, r_rkw = rkwr.next()
            wv = wfb[:, wofs:wofs + 4].unsqueeze(2).to_broadcast([128, 4, 128])
            S.op("pool", I_("tensor_tensor", out=rkw[:].rearrange("p (h d) -> p h d", h=4), in0=rk[:].rearrange("p (h d) -> p h d", h=4), in1=wv, op=ALU.mult),
                 reads=[r_rk, r_wfb], writes=[r_rkw])
            outs = []
            for hp in range(2):
                pk, pkr = ps_kv.next()
                for hq in range(2):
                    h = hp * 2 + hq
                    S.op("pe", I_("matmul", pk[:, hq * 256:(hq + 1) * 256], lhsT=rkw[:, h * 128:(h + 1) * 128], rhs=rv[:, h * 256:(h + 1) * 256], start=True, stop=True),
                         reads=[r_rkw, r_rv], writes=[pkr], skip_self=True)
                    outs.append((pk[:, hq * 256:(hq + 1) * 256], pkr))
            return outs

        for s in ("a", "b"):
            q = seqs[s]
            NR, NO = q["nreg"], q["noth"]
            S.op("dve", I_("memset", R32[:], 0.0), writes=[r_R])
            S.op("dve", I_("memset", L32[:], 0.0), writes=[r_L])
            for direction in (0, 1):
                order = range(NO) if direction == 0 else range(NO - 1, -1, -1)
                st32, r_st = (R32, r_R) if direction == 0 else (L32, r_L)
                acoef = oth_af if direction == 0 else oth_ab
                r_ac = r_oaf if direction == 0 else r_oab
                bcoef, r_bc = (flb, r_flb) if direction == 0 else (fla, r_fla)
                for jj in order:
                    tt = (NR + jj) * 128
                    rk, r_rk, rv, r_rv = load_kv(s, tt)
                    outs = kv_summ(rk, r_rk, rv, r_rv, 0 if direction == 0 else 4)
                    for h in range(4):
                        pa, par = outs[h]
                        kt, r_kt = kvt.next()
                        S.op("act", I_("activation", out=kt[:], in_=pa, func=AF.Copy, scale=bcoef[:, jj:jj + 1]), reads=[par, r_bc], writes=[r_kt])
                        S.op("dve", I_("scalar_tensor_tensor", out=st32[:, h, :], in0=st32[:, h, :], scalar=acoef[:, jj, h:h + 1], in1=kt[:], op0=ALU.mult, op1=ALU.add),
                             reads=[r_st, r_ac, r_kt], writes=[r_st])
            for n in range(NR - 1, -1, -1):
                S.op("act", I_("activation", out=Lst[:, n, :, :], in_=L32[:], func=AF.Copy), reads=[r_L], writes=[r_Lst[n]])
                if n == 0:
                    break
                rk, r_rk, rv, r_rv = load_kv(s, n * 128)
                outs = kv_summ(rk, r_rk, rv, r_rv, 4)
                for h in range(4):
                    pa, par = outs[h]
                    S.op("dve", I_("scalar_tensor_tensor", out=L32[:, h, :], in0=L32[:, h, :], scalar=dec[:, 4 + h:5 + h], in1=pa, op0=ALU.mult, op1=ALU.add),
                         reads=[r_L, r_dec, par], writes=[r_L])
            for n in range(NR):
                tt = n * 128
                rb_, r_rb = Rbf.next()
                S.op("act", I_("activation", out=rb_[:], in_=R32[:], func=AF.Copy), reads=[r_R], writes=[r_rb])
                rk, r_rk, rv, r_rv = load_kv(s, tt)
                qT, r_qT = qTr.next(); kT, r_kT = kTr.next()
                S.dma("sp", "p2qT%d" % qTr.slot(), I_("dma_start", out=qT[:], in_=SC[s]["RQT"][:, :, tt:tt + 128].rearrange("h d t -> d h t")),
                      reads=[SC[s]["res"]["RQT"]], writes=[r_qT])
                S.dma("sp", "p2kT%d" % kTr.slot(), I_("dma_start", out=kT[:], in_=SC[s]["RKT"][:, :, tt:tt + 128].rearrange("h d t -> d h t")),
                      reads=[SC[s]["res"]["RKT"]], writes=[r_kT])
                ys, r_ys = ysr.next()
                psS, psSr = ps_s.next()
                for h in range(4):
                    S.op("pe", I_("matmul", psS[:, h * 128:(h + 1) * 128], lhsT=kT[:, h, :], rhs=qT[:, h, :], start=True, stop=True),
                         reads=[r_kT, r_qT], writes=[psSr], skip_self=True)
                for h in range(4):
                    AT, r_AT = ATr.next(); qf, r_qf = qfr.next(); qb, r_qb = qbr.next()
                    S.op("dve", I_("tensor_tensor", out=AT[:], in0=psS[:, h * 128:(h + 1) * 128], in1=dmatT[:, h, :], op=ALU.mult),
                         reads=[psSr, r_dmat], writes=[r_AT])
                    S.op("pool", I_("tensor_tensor", out=qf[:], in0=qT[:, h, :], in1=QF[:, h, :], op=ALU.mult), reads=[r_qT, r_QF], writes=[r_qf])
                    S.op("pool", I_("tensor_tensor", out=qb[:], in0=qT[:, h, :], in1=QB[:, h, :], op=ALU.mult), reads=[r_qT, r_QB], writes=[r_qb])
                    po, por = ps_o.next()
                    S.op("pe", I_("matmul", po[:, 0:256], lhsT=AT[:], rhs=rv[:, h * 256:(h + 1) * 256], start=True, stop=False),
                         reads=[r_AT, r_rv], writes=[por], inc=False, skip_self=True)
                    S.op("pe", I_("matmul", po[:, 0:256], lhsT=qf[:], rhs=rb_[:, h, :], start=False, stop=False),
                         reads=[r_qf, r_rb], writes=[por], inc=False, skip_self=True)
                    S.op("pe", I_("matmul", po[:, 0:256], lhsT=qb[:], rhs=Lst[:, n, h, :], start=False, stop=True),
                         reads=[r_qb, r_Lst[n]], writes=[por], skip_self=True)
                    ss, r_ss = ssr.next()
                    S.op("act", I_("activation", out=junk[:], in_=po[:, 0:256], func=AF.Square, accum_out=ss[:, 0:1]), reads=[por], writes=[r_junk, r_ss])
                    S.op("act", I_("activation", out=ss[:, 1:2], in_=ss[:, 0:1], func=AF.Sqrt, bias=epsc[:, 1:2]), reads=[r_ss, r_epsc], writes=[r_ss])
                    S.op("dve", I_("reciprocal", out=ss[:, 1:2], in_=ss[:, 1:2]), reads=[r_ss], writes=[r_ss])
                    S.op("dve", I_("tensor_scalar", out=ys[:, h * 256:(h + 1) * 256], in0=po[:, 0:256], scalar1=ss[:, 1:2], scalar2=16.0, op0=ALU.mult, op1=ALU.mult),
                         reads=[por, r_ss], writes=[r_ys])
                S.dma("sp", "p2ys%d" % ysr.slot(), I_("dma_start", out=SC[s]["Y"][tt:tt + 128, 0:1024], in_=ys[:]), reads=[r_ys], writes=[SC[s]["res"]["Y"]])
                if n < NR - 1:
                    outs = kv_summ(rk, r_rk, rv, r_rv, 0)
                    for h in range(4):
                        pa, par = outs[h]
                        S.op("dve", I_("scalar_tensor_tensor", out=R32[:, h, :], in0=R32[:, h, :], scalar=dec[:, h:h + 1], in1=pa, op0=ALU.mult, op1=ALU.add),
                             reads=[r_R, r_dec, par], writes=[r_R])
        S.barrier()
        st.close()

    def phase3():
        st = contextlib.ExitStack()
        SCmax = max(SA, SBT); SRmax = max(SA, SRB); NKmax = SCmax // 128
        KTr = Ring([(sb(st, "p3_KT%d" % i, [128, SCmax], BF16), Res("KT")) for i in range(2)])
        QTr = Ring([(sb(st, "p3_QT%d" % i, [128, 2, SRmax], BF16), Res("QT")) for i in range(2)])
        for (qt_, r_qt_) in QTr.items:
            S.op("pool", I_("memset", qt_[64:128, 0, :], 0.0), writes=[r_qt_])
            S.op("pool", I_("memset", qt_[0:64, 1, :], 0.0), writes=[r_qt_])
        Vr = Ring([(sb(st, "p3_V%d" % i, [128, NKmax, 130], BF16), Res("V")) for i in range(2)])
        Hr = Ring([(sb(st, "p3_H%d" % i, [128, 6, 512], BF16), Res("H")) for i in range(2)])
        for (v, r_v) in Vr.items:
            S.op("pool", I_("memset", v[:, :, 128:130], 1.0), writes=[r_v])
        ptr = Ring([(sb(st, "p3_pt%d" % i, [128, 512], BF16), Res("pt")) for i in range(8)])
        Oe = [(sb(st, "p3_Oe%d" % i, [128, 130], F32), Res("Oe")) for i in range(4)]
        rcp = Ring([(sb(st, "p3_rc%d" % i, [128, 12], F32), Res("rc")) for i in range(4)])
        junkf = sb(st, "p3_junkf", [128, 128], F32); r_junkf = Res("junkf")
        dr = Ring([(sb(st, "p3_d%d" % i, [128, 128], F32), Res("d")) for i in range(4)])
        junk = sb(st, "p3_junk", [128, 128], BF16); r_junk = Res("junk")
        yor = Ring([(sb(st, "p3_yo%d" % i, [128, 128], BF16), Res("yo")) for i in range(4)])
        psS = Ring(PS[0:3]); psOT = Ring(PS[3:5]); psZ = Ring(PS[5:7]); psE = Ring(PS[7:8])
        OTnr = Ring([(sb(st, "p3_OTn%d" % i, [128, 512], F32), Res("OTn")) for i in range(2)])
        ones_f128 = sb(st, "p3_ones_f128", [128, 128], F32)
        r_onesf128 = Res("onesf128")
        S.op("pool", I_("memset", ones_f128[:], 1.0), writes=[r_onesf128])
        accr = Ring([(sb(st, "p3_acc%d" % i, [128, 512], F32), Res("acc")) for i in range(2)])
        accpr = Ring([(sb(st, "p3_accp%d" % i, [128, 512], F32), Res("accp")) for i in range(2)])
        deferred = []
        DEFER = 6
        OTer = Ring([(sb(st, "p3_OTe%d" % i, [128, 512], F32), Res("OTe")) for i in range(2)])
        ones_f = sb(st, "p3_ones_f", [128, 2], F32); r_onesf = Res("onesf")
        S.op("pool", I_("memset", ones_f[:], 1.0), writes=[r_onesf])
        rnd = {}

        LA = 2
        heads = [(s, h) for s in ("a", "b") for h in range(8)]
        bufs = [(KTr.items[i], QTr.items[i], Vr.items[i], Hr.items[i]) for i in range(2)]

        def emit_loads(hi):
            s, h = heads[hi]
            q = seqs[s]
            NR, NO, SR, SCX = q["nreg"], q["noth"], q["sreg"], q["sctx"]
            NK = NR + NO
            sc = SC[s]; rs = sc["res"]
            b = hi % 2
            (KT, r_KT), (QT, r_QT), (V, r_V), (H, r_H) = bufs[b]
            S.dma("sp", "p3kt%d" % b, I_("dma_start", out=KT[:, 0:SCX], in_=sc["KT"][h, :, :]), reads=[rs["KT"]], writes=[r_KT])
            S.dma("sp", "p3qt%d" % b, I_("dma_start", out=QT[0:64, 0, 0:SR], in_=sc["QT"][h, 0:64, :]), reads=[rs["QT"]], writes=[r_QT])
            S.dma("sp", "p3qt%d" % b, I_("dma_start", out=QT[64:128, 1, 0:SR], in_=sc["QT"][h, 64:128, :]), reads=[rs["QT"]], writes=[r_QT])
            S.dma("sp", "p3v%d" % b, I_("dma_start", out=V[:, 0:NK, 0:128], in_=sc["DVH"][h, :, :, :]), reads=[rs["DVH"]], writes=[r_V])
            for di in range(6):
                delta = di - 1
                off = h * 1280 + 512 - 128 * delta
                S.dma("sp", "p3h%d" % b, I_("dma_start", out=H[:, di, :], in_=bass.AP(GREV.tensor, off, [[1, 128], [1, 512]])),
                      reads=[r_grev], writes=[r_H])

        work = []

        def make_iter(hi, g, c, kb, first_of_head):
            s, h = heads[hi]
            q = seqs[s]
            NR, NO = q["nreg"], q["noth"]
            NK = NR + NO
            sc = SC[s]; rs = sc["res"]
            (KT, r_KT), (QT, r_QT), (V, r_V), (H, r_H) = bufs[hi % 2]
            st_ = {}

            def stage_a():
                ps, psr = psS.next()
                st_["ps"] = (ps, psr)
                near = (kb < NR) and (4 * g - 1 <= kb <= 4 * g + 4)
                S.op("pe", I_("matmul", ps[:, :], lhsT=KT[:, kb * 128:(kb + 1) * 128], rhs=QT[:, c, g * 512:(g + 1) * 512],
                              start=True, stop=(not near)),
                     reads=[r_KT, r_QT], writes=[psr], inc=(not near), skip_self=True)
                if near:
                    di = kb - 4 * g + 1
                    S.op("pe", I_("matmul", ps[:, :], lhsT=jmat_b[:], rhs=H[:, di, :], start=False, stop=True),
                         reads=[r_jb, r_H], writes=[psr], skip_self=True)
                    st_["bias"] = (zero_c[:, 0:1], r_zero)
                elif kb >= NR:
                    st_["bias"] = (oth_fb[:, kb - NR, h:h + 1], r_ofb)
                elif kb < 4 * g - 1:
                    st_["bias"] = (farb[:, h:h + 1], r_farb)
                else:
                    st_["bias"] = (farb[:, 8 + h:9 + h], r_farb)

            def stage_b():
                if first_of_head and hi + 1 < len(heads):
                    emit_loads(hi + 1)
                ps, psr = st_["ps"]
                bias_ap, r_bias = st_["bias"]
                pt, r_pt = ptr.next()
                S.op("act", I_("activation", out=pt[:], in_=ps[:, :], func=AF.Exp, bias=bias_ap),
                     reads=[psr, r_bias], writes=[r_pt])
                if kb == 0:
                    rnd["acc"] = accr.next()
                    rnd["OT"] = psOT.next()
                    rnd["Z"] = psZ.next()
                acc, r_acc = rnd["acc"]
                OT, r_OT = rnd["OT"]
                Zp, r_Zp = rnd["Z"]
                if kb == 0:
                    S.op("dve", I_("tensor_copy", out=acc[:], in_=pt[:]), reads=[r_pt], writes=[r_acc])
                elif kb % 2 == 0:
                    S.op("dve", I_("tensor_tensor", out=acc[:], in0=acc[:], in1=pt[:], op=ALU.add), reads=[r_pt, r_acc], writes=[r_acc])
                S.op("pe", I_("matmul", OT[:, :], lhsT=V[:, kb, 0:128], rhs=pt[:, :], start=(kb == 0), stop=(kb == NK - 1)),
                     reads=[r_pt, r_V], writes=[r_OT], skip_self=True)
                if kb % 2 == 1:
                    S.op("pe", I_("matmul", Zp[:, :], lhsT=ones_b[:], rhs=pt[:, :], start=(kb == 1), stop=False),
                         reads=[r_pt, r_ones], writes=[r_Zp], skip_self=True)
                if kb != NK - 1:
                    return
                rnd["pending"] = (NK, epilogue_fn(acc, r_acc, OT, r_OT, Zp, r_Zp))

            def epilogue_fn(acc, r_acc, OT, r_OT, Zp, r_Zp):
              stt = {}

              def e1():
                S.op("pe", I_("matmul", Zp[:, :], lhsT=ones_f128[:], rhs=acc[:], start=False, stop=True), reads=[r_acc, r_onesf128], writes=[r_Zp], skip_self=True)
                OTe, r_OTe = OTer.next()
                OTh = OTe[:].bitcast(BF16)[:, 0:512]
                OTl = OTe[:].bitcast(BF16)[:, 512:1024]
                S.op("dve", I_("tensor_copy", out=OTh, in_=OT[:, :]), reads=[r_OT], writes=[r_OTe])
                S.op("dve", I_("tensor_tensor", out=OTl, in0=OT[:, :], in1=OTh, op=ALU.subtract), reads=[r_OT, r_OTe], writes=[r_OTe])
                OTn, r_OTn = OTnr.next()
                rc, r_rc = rcp.next()
                stt["rc"] = (rc, r_rc)
                S.op("dve", I_("tensor_tensor", out=OTn[:].rearrange("p (a b) -> p a b", a=4), in0=Zp[:, :].rearrange("p (a b) -> p a b", a=4),
                               in1=ident_f[:].unsqueeze(1).to_broadcast([128, 4, 128]), op=ALU.mult), reads=[r_Zp, r_identf], writes=[r_OTn])
                S.op("dve", I_("reduce_sum", out=rc[:, 0:4], in_=OTn[:].rearrange("p (a b) -> p a b", a=4), axis=AX.X), reads=[r_OTn], writes=[r_rc])
                S.op("dve", I_("reciprocal", out=rc[:, 0:4], in_=rc[:, 0:4]), reads=[r_rc], writes=[r_rc])
                if c == 1:
                    S.op("dve", I_("tensor_scalar", out=rc[:, 0:4], in0=rc[:, 0:4], scalar1=lamc[:, 1:2], scalar2=None, op0=ALU.mult), reads=[r_rc, r_lam], writes=[r_rc])
                po, por = psE.next()
                for qb in range(4):
                    S.op("pe", I_("matmul", po[:, qb * 128:(qb + 1) * 128], lhsT=OTh[:, qb * 128:(qb + 1) * 128], rhs=ident_b[:], start=True, stop=False), reads=[r_OTe, r_identb], writes=[por], inc=False, skip_self=True)
                    S.op("pe", I_("matmul", po[:, qb * 128:(qb + 1) * 128], lhsT=OTl[:, qb * 128:(qb + 1) * 128], rhs=ident_b[:], start=False, stop=True), reads=[r_OTe, r_identb], writes=[por], inc=(qb == 3), skip_self=True)
                stt["d"] = []
                for qb in range(4):
                    pq = po[:, qb * 128:(qb + 1) * 128]
                    oe, r_oe = Oe[qb]
                    if c == 0:
                        S.op("dve", I_("tensor_scalar", out=oe[:, 0:128], in0=pq, scalar1=rc[:, qb:qb + 1], scalar2=None, op0=ALU.mult), reads=[por, r_rc], writes=[r_oe])
                    else:
                        d, r_d = dr.next()
                        stt["d"].append((d, r_d))
                        S.op("dve", I_("scalar_tensor_tensor", out=d[:], in0=pq, scalar=rc[:, qb:qb + 1], in1=oe[:, 0:128], op0=ALU.mult, op1=ALU.add),
                             reads=[por, r_rc, r_oe], writes=[r_d])
                        S.op("dve", I_("tensor_tensor", out=junkf[:], in0=d[:], in1=d[:], op=ALU.mult), reads=[r_d], writes=[r_junkf])
                        S.op("dve", I_("reduce_sum", out=rc[:, 4 + qb:5 + qb], in_=junkf[:], axis=AX.X), reads=[r_junkf], writes=[r_rc])

              def e2():
                rc, r_rc = stt["rc"]
                S.op("act", I_("activation", out=rc[:, 8:12], in_=rc[:, 4:8], func=AF.Ln, bias=epsc[:, 3:4], scale=1.0 / 128), reads=[r_rc, r_epsc], writes=[r_rc])
                S.op("act", I_("activation", out=rc[:, 8:12], in_=rc[:, 8:12], func=AF.Exp, scale=-0.5), reads=[r_rc], writes=[r_rc])

              def e3():
                rc, r_rc = stt["rc"]
                for qb in range(4):
                    d, r_d = stt["d"][qb]
                    yo, r_yo = yor.next()
                    S.op("dve", I_("tensor_scalar", out=d[:], in0=d[:], scalar1=rc[:, 8 + qb:9 + qb], scalar2=None, op0=ALU.mult), reads=[r_d, r_rc], writes=[r_d])
                    S.op("pool", I_("tensor_tensor", out=yo[:], in0=d[:], in1=dng[:], op=ALU.mult), reads=[r_d, r_dng], writes=[r_yo])
                    tq = (g * 4 + qb) * 128
                    S.dma("sp", "p3yo%d" % yor.slot(), I_("dma_start", out=sc["Y"][tq:tq + 128, 1024 + h * 128:1024 + (h + 1) * 128], in_=yo[:]),
                          reads=[r_yo], writes=[rs["Y"]])
              if c == 0:
                  return [(6, e1)]
              return [(6, e1), (22, e2), (26, e3)]
            return stage_a, stage_b

        for hi, (s, h) in enumerate(heads):
            q = seqs[s]
            NK = q["nreg"] + q["noth"]
            first = True
            for g in range(q["nreg"] // 4):
                for c in range(2):
                    for kb in range(NK):
                        work.append(make_iter(hi, g, c, kb, first))
                        first = False
        emit_loads(0)
        for idx in range(len(work) + LA):
            if idx < len(work):
                work[idx][0]()
            if idx >= LA:
                rnd.pop("pending", None)
                work[idx - LA][1]()
                if "pending" in rnd:
                    nk_, stages = rnd.pop("pending")
                    for (off, fn_) in stages:
                        off_ = min(off, nk_ - 2) if off == 6 else min(off, nk_ - 1 + (1 if off == 26 else 0))
                        deferred.append((idx + off_, len(deferred), fn_))
                    deferred.sort(key=lambda t: (t[0], t[1]))
                while deferred and deferred[0][0] <= idx:
                    deferred.pop(0)[2]()
        deferred.sort(key=lambda t: (t[0], t[1]))
        while deferred:
            deferred.pop(0)[2]()
        S.barrier()
        st.close()

    def phase4():
        st = contextlib.ExitStack()
        Wo = sb(st, "p4_Wo", [128, 16, D], BF16); r_Wo = Res("Wo")
        Wc = sb(st, "p4_Wc", [128, 8, 6144], BF16); r_Wc = Res("Wc")
        for j in range(16):
            S.dma("pool", "p4w", I_("dma_start", out=Wo[:, j, :], in_=I["w_out_mix"][j * 128:(j + 1) * 128, :]), writes=[r_Wo])
        for j in range(8):
            S.dma("pool", "p4w", I_("dma_start", out=Wc[:, j, :], in_=I["w_in_conv"][j * 128:(j + 1) * 128, :]), writes=[r_Wc])
        yr = Ring([(sb(st, "p4_y%d" % i, [128, 2048], BF16), Res("y")) for i in range(1)])
        gr = Ring([(sb(st, "p4_g%d" % i, [128, 2048], BF16), Res("g")) for i in range(1)])
        ygr = Ring([(sb(st, "p4_yg%d" % i, [128, 2048], BF16), Res("yg")) for i in range(1)])
        ygTr = Ring([(sb(st, "p4_ygT%d" % i, [128, 16, 128], BF16), Res("ygT")) for i in range(2)])
        xr = Ring([(sb(st, "p4_x%d" % i, [128, D], F32), Res("x")) for i in range(2)])
        x1r = Ring([(sb(st, "p4_x1%d" % i, [128, D], F32), Res("x1")) for i in range(1)])
        ssr = Ring([(sb(st, "p4_ss%d" % i, [128, 2], F32), Res("ss")) for i in range(2)])
        junk = sb(st, "p4_junk", [128, D], BF16); r_junk = Res("junk")
        hr = Ring([(sb(st, "p4_h%d" % i, [128, D], BF16), Res("h")) for i in range(2)])
        hTr = Ring([(sb(st, "p4_hT%d" % i, [128, 8, 512], BF16), Res("hT")) for i in range(1)])
        sgr = Ring([(sb(st, "p4_sg%d" % i, [128, 512], F32), Res("sg")) for i in range(2)])
        fmr = Ring([(sb(st, "p4_fm%d" % i, [128, 512], BF16), Res("fm")) for i in range(4)])
        psT = Ring(PS[0:2]); pso = Ring(PS[2:4]); psfm = Ring(PS[4:8])

        for s in ("a", "b"):
            q = seqs[s]
            sc = SC[s]; rs = sc["res"]
            for g in range(q["nreg"] // 4):
                t0 = g * 512
                hT, r_hT = hTr.next()
                for i in range(4):
                    tt = t0 + i * 128
                    y, r_y = yr.next(); gt, r_gt = gr.next(); x, r_x = xr.next()
                    S.dma("sp", "p4y%d" % yr.slot(), I_("dma_start", out=y[:], in_=sc["Y"][tt:tt + 128, :]), reads=[rs["Y"]], writes=[r_y])
                    S.dma("sp", "p4g%d" % gr.slot(), I_("dma_start", out=gt[:], in_=sc["GT"][tt:tt + 128, :]), reads=[rs["GT"]], writes=[r_gt])
                    S.dma("sp", "p4x%d" % xr.slot(), I_("dma_start", out=x[:], in_=X[s][tt:tt + 128, :]), writes=[r_x])
                    yg, r_yg = ygr.next()
                    S.op("pool", I_("tensor_tensor", out=yg[:], in0=y[:], in1=gt[:], op=ALU.mult), reads=[r_y, r_gt], writes=[r_yg])
                    ygT, r_ygT = ygTr.next()
                    for half in range(2):
                        pt, pr = psT.next()
                        ptb = pt[:].bitcast(BF16)
                        for j in range(8):
                            jj = half * 8 + j
                            S.op("pe", I_("transpose", ptb[:, j * 128:(j + 1) * 128], yg[:, jj * 128:(jj + 1) * 128], ident_b[:]),
                                 reads=[r_yg, r_identb], writes=[pr], inc=(j == 7), skip_self=True)
                        S.op("act", I_("activation", out=ygT[:, half * 8:(half + 1) * 8, :], in_=ptb.rearrange("p (j t) -> p j t", j=8), func=AF.Copy),
                             reads=[pr], writes=[r_ygT])
                    x1, r_x1 = x1r.next()
                    for cg in range(2):
                        po, por = pso.next()
                        for j in range(16):
                            S.op("pe", I_("matmul", po[:, :], lhsT=ygT[:, j, :], rhs=Wo[:, j, cg * 512:(cg + 1) * 512], start=(j == 0), stop=(j == 15)),
                                 reads=[r_ygT, r_Wo], writes=[por], inc=(j == 15), skip_self=True)
                        S.op("dve", I_("tensor_tensor", out=x1[:, cg * 512:(cg + 1) * 512], in0=po[:, :], in1=x[:, cg * 512:(cg + 1) * 512], op=ALU.add),
                             reads=[por, r_x], writes=[r_x1])
                    S.dma("sp", "p4x1%d" % x1r.slot(), I_("dma_start", out=sc["X1"][tt:tt + 128, :], in_=x1[:]), reads=[r_x1], writes=[rs["X1"]])
                    ss, r_ss = ssr.next()
                    S.op("act", I_("activation", out=junk[:], in_=x1[:], func=AF.Square, accum_out=ss[:, 0:1]), reads=[r_x1], writes=[r_junk, r_ss])
                    S.op("act", I_("activation", out=ss[:, 1:2], in_=ss[:, 0:1], func=AF.Sqrt, bias=epsc[:, 0:1]), reads=[r_ss, r_epsc], writes=[r_ss])
                    S.op("dve", I_("reciprocal", out=ss[:, 1:2], in_=ss[:, 1:2]), reads=[r_ss], writes=[r_ss])
                    h, r_h = hr.next()
                    S.op("act", I_("activation", out=x[:], in_=x1[:], func=AF.Copy, scale=ss[:, 1:2]), reads=[r_x1, r_ss, r_x], writes=[r_x])
                    S.op("pool", I_("tensor_tensor", out=h[:], in0=x[:], in1=g2[:], op=ALU.mult), reads=[r_x, r_g2], writes=[r_h])
                    pt, pr = psT.next()
                    ptb = pt[:].bitcast(BF16)
                    for j in range(8):
                        S.op("pe", I_("transpose", ptb[:, j * 128:(j + 1) * 128], h[:, j * 128:(j + 1) * 128], ident_b[:]),
                             reads=[r_h, r_identb], writes=[pr], inc=(j == 7), skip_self=True)
                    S.op("act", I_("activation", out=hT[:, :, i * 128:(i + 1) * 128], in_=ptb.rearrange("p (j t) -> p j t", j=8), func=AF.Copy),
                         reads=[pr], writes=[r_hT])
                def fm_mm(c0):
                    pm, pmr = psfm.next()
                    for j in range(8):
                        S.op("pe", I_("matmul", pm[:, :], lhsT=Wc[:, j, c0:c0 + 128], rhs=hT[:, j, :], start=(j == 0), stop=(j == 7)),
                             reads=[r_hT, r_Wc], writes=[pmr], inc=(j == 7), skip_self=True)
                    return pm, pmr
                for cb in range(16):
                    pa, par = fm_mm(cb * 128)
                    pb, pbr = fm_mm(2048 + cb * 128)
                    sg, r_sg = sgr.next()
                    S.op("act", I_("activation", out=sg[:], in_=pb[:, :], func=AF.Sigmoid), reads=[pbr], writes=[r_sg])
                    fm, r_fm = fmr.next(); slot = fmr.slot()
                    S.op("dve", I_("tensor_tensor", out=fm[:], in0=pa[:, :], in1=sg[:], op=ALU.mult), reads=[par, r_sg], writes=[r_fm])
                    S.dma("sp", "p4fm%d" % slot, I_("dma_start", out=sc["GLU"][cb * 128:(cb + 1) * 128, t0:t0 + 512], in_=fm[:]), reads=[r_fm], writes=[rs["GLU"]])
                    pg, pgr = fm_mm(4096 + cb * 128)
                    fm, r_fm = fmr.next(); slot = fmr.slot()
                    S.op("act", I_("activation", out=fm[:], in_=pg[:, :], func=AF.Silu), reads=[pgr], writes=[r_fm])
                    S.dma("sp", "p4fm%d" % slot, I_("dma_start", out=sc["SG"][cb * 128:(cb + 1) * 128, t0:t0 + 512], in_=fm[:]), reads=[r_fm], writes=[rs["SG"]])
        S.barrier()
        st.close()

    def phase5():
        st = contextlib.ExitStack()
        Wo = sb(st, "p5_Wo", [128, 16, D], BF16); r_Wo = Res("Wo5")
        for j in range(16):
            S.dma("pool", "p5w", I_("dma_start", out=Wo[:, j, :], in_=I["w_out_conv"][j * 128:(j + 1) * 128, :]), writes=[r_Wo])
        winr = Ring([(sb(st, "p5_win%d" % i, [128, 16, 544], BF16), Res("win")) for i in range(2)])
        sgr = Ring([(sb(st, "p5_sg%d" % i, [128, 16, 512], BF16), Res("sg")) for i in range(1)])
        dgr = Ring([(sb(st, "p5_dg%d" % i, [128, 16, 128], BF16), Res("dg")) for i in range(2)])
        dgor = Ring([(sb(st, "p5_dgo%d" % i, [128, 15, 128], BF16), Res("dgo")) for i in range(2)])
        ybuf = sb(st, "p5_y", [128, 16, 512], F32); r_yb = [Res("yb%d" % i) for i in range(16)]
        ybfr = Ring([(sb(st, "p5_ybf%d" % i, [128, 512], BF16), Res("ybf")) for i in range(2)])
        ysqr = Ring([(sb(st, "p5_ysq%d" % i, [128, 512], BF16), Res("ysq")) for i in range(2)])
        mean = sb(st, "p5_mean", [128, 512], F32); r_mean = Res("mean")
        msq = sb(st, "p5_msq", [128, 512], F32); r_msq = Res("msq")
        rstd = sb(st, "p5_rstd", [128, 512], F32); r_rstd = Res("rstd")
        tnr = Ring([(sb(st, "p5_tn%d" % i, [128, 512], F32), Res("tn")) for i in range(2)])
        znr = Ring([(sb(st, "p5_zn%d" % i, [128, 512], F32), Res("zn")) for i in range(2)])
        zT = sb(st, "p5_zT", [128, 16, 512], BF16); r_zT = [Res("zT%d" % i) for i in range(16)]
        x1r = Ring([(sb(st, "p5_x1%d" % i, [128, D], F32), Res("x1")) for i in range(2)])
        x2r = Ring([(sb(st, "p5_x2%d" % i, [128, D], F32), Res("x2")) for i in range(2)])
        ssr = Ring([(sb(st, "p5_ss%d" % i, [128, 2], F32), Res("ss")) for i in range(2)])
        junk = sb(st, "p5_junk", [128, D], BF16); r_junk = Res("junk")
        psy = Ring(PS[0:2]); ps_st, r_pst = PS[2]; ps_sq, r_psq = PS[3]; pso = Ring(PS[4:8])

        for s in ("a", "b"):
            q = seqs[s]
            sc = SC[s]; rs = sc["res"]
            SR = q["sreg"]
            NG = q["nreg"] // 4
            for g in range(NG):
                t0 = g * 512
                win, r_win = winr.next(); sg, r_sg = sgr.next()
                lo = max(t0 - 15, 0); hi = min(t0 + 512 + 15, SR)
                c_lo = lo - (t0 - 15)
                if g == 0 or g == NG - 1:
                    S.op("pool", I_("memset", win[:], 0.0), writes=[r_win])
                for half in range(2):
                    S.dma("sp", "p5win%d" % winr.slot(), I_("dma_start", out=win[:, half * 8:(half + 1) * 8, c_lo:c_lo + (hi - lo)],
                                                                                                                 in_=sc["GLU"][half * 1024:(half + 1) * 1024, lo:hi].rearrange("(cb p) t -> p cb t", p=128)),
                          reads=[rs["GLU"]], writes=[r_win])
                    S.dma("sp", "p5sg%d" % sgr.slot(), I_("dma_start", out=sg[:, half * 8:(half + 1) * 8, :], in_=sc["SG"][half * 1024:(half + 1) * 1024, t0:t0 + 512].rearrange("(cb p) t -> p cb t", p=128)),
                          reads=[rs["SG"]], writes=[r_sg])
                pend_stats = []
                for cb in range(16):
                    dg, r_dg = dgr.next()
                    dgo, r_dgo = dgor.next()
                    for k in range(0, 31, 2):
                        S.op("dve", I_("tensor_scalar", out=dg[:, k // 2, :], in0=ident_f[:], scalar1=cwT[:, cb, k:k + 1], scalar2=None, op0=ALU.mult),
                             reads=[r_identf, r_cwT], writes=[r_dg], inc=(k == 30), skip_self=True)
                    for k in range(1, 31, 2):
                        S.op("act", I_("activation", out=dgo[:, k // 2, :], in_=ident_f[:], func=AF.Copy, scale=cwT[:, cb, k:k + 1]),
                             reads=[r_identf, r_cwT], writes=[r_dgo], inc=(k == 29), skip_self=True)
                    py, pyr = psy.next()
                    for k in range(31):
                        S.op("pe", I_("matmul", py[:, :], lhsT=(dg if k % 2 == 0 else dgo)[:, k // 2, :], rhs=win[:, cb, k:k + 512], start=(k == 0), stop=(k == 30)),
                             reads=[r_dg, r_dgo, r_win], writes=[pyr], inc=(k == 30), skip_self=True)
                    while pend_stats:
                        pend_stats.pop(0)()
                    S.op("act", I_("activation", out=ybuf[:, cb, :], in_=py[:, :], func=AF.Identity, bias=cvb[:, cb:cb + 1]), reads=[pyr, r_cvb], writes=[r_yb[cb]])
                    ysq, r_ysq = ysqr.next(); ybf, r_ybf = ybfr.next()
                    S.op("act", I_("activation", out=ysq[:], in_=py[:, :], func=AF.Square, bias=cvb[:, cb:cb + 1]), reads=[pyr, r_cvb], writes=[r_ysq])
                    S.op("dve", I_("tensor_copy", out=ybf[:], in_=ybuf[:, cb, :]), reads=[r_yb[cb]], writes=[r_ybf])
                    def _stats(ybf=ybf, r_ybf=r_ybf, ysq=ysq, r_ysq=r_ysq, cb=cb):
                        S.op("pe", I_("matmul", ps_st[:, :], lhsT=ones_b[:], rhs=ybf[:], start=(cb == 0), stop=(cb == 15)), reads=[r_ybf, r_ones], writes=[r_pst], skip_self=True)
                        S.op("pe", I_("matmul", ps_sq[:, :], lhsT=ones_b[:], rhs=ysq[:], start=(cb == 0), stop=(cb == 15)), reads=[r_ysq, r_ones], writes=[r_psq], skip_self=True)
                    pend_stats.append(_stats)
                while pend_stats:
                    pend_stats.pop(0)()
                S.op("dve", I_("tensor_scalar", out=mean[:], in0=ps_st[:, :], scalar1=1.0 / 2048, scalar2=None, op0=ALU.mult), reads=[r_pst], writes=[r_mean])
                S.op("dve", I_("tensor_tensor", out=msq[:], in0=mean[:], in1=mean[:], op=ALU.mult), reads=[r_mean], writes=[r_msq])
                S.op("dve", I_("scalar_tensor_tensor", out=rstd[:], in0=ps_sq[:, :], scalar=1.0 / 2048, in1=msq[:], op0=ALU.mult, op1=ALU.subtract), reads=[r_psq, r_msq], writes=[r_rstd])
                S.op("act", I_("activation", out=rstd[:], in_=rstd[:], func=AF.Sqrt, bias=epsc[:, 3:4]), reads=[r_rstd, r_epsc], writes=[r_rstd])
                S.op("dve", I_("reciprocal", out=rstd[:], in_=rstd[:]), reads=[r_rstd], writes=[r_rstd])
                for cb in range(16):
                    tn, r_tn = tnr.next(); zn, r_zn = znr.next()
                    S.op("dve", I_("tensor_tensor", out=tn[:], in0=ybuf[:, cb, :], in1=mean[:], op=ALU.subtract), reads=[r_yb[cb], r_mean], writes=[r_tn])
                    S.op("dve", I_("tensor_tensor", out=tn[:], in0=tn[:], in1=rstd[:], op=ALU.mult), reads=[r_tn, r_rstd], writes=[r_tn])
                    S.op("act", I_("activation", out=zn[:], in_=tn[:], func=AF.Silu, bias=cnb[:, cb:cb + 1], scale=cng[:, cb:cb + 1]),
                         reads=[r_tn, r_cng, r_cnb], writes=[r_zn])
                    S.op("dve", I_("tensor_tensor", out=zT[:, cb, :], in0=zn[:], in1=sg[:, cb, :], op=ALU.mult), reads=[r_zn, r_sg], writes=[r_zT[cb]])
                for i in range(4):
                    tt = t0 + i * 128
                    x1, r_x1 = x1r.next()
                    S.dma("sp", "p5x1%d" % x1r.slot(), I_("dma_start", out=x1[:], in_=sc["X1"][tt:tt + 128, :]), reads=[rs["X1"]], writes=[r_x1])
                    x2, r_x2 = x2r.next()
                    for cg in range(2):
                        po, por = pso.next()
                        for cb in range(16):
                            S.op("pe", I_("matmul", po[:, :], lhsT=zT[:, cb, i * 128:(i + 1) * 128], rhs=Wo[:, cb, cg * 512:(cg + 1) * 512], start=(cb == 0), stop=(cb == 15)),
                                 reads=[r_zT[cb], r_Wo], writes=[por], inc=(cb == 15), skip_self=True)
                        S.op("dve", I_("tensor_tensor", out=x2[:, cg * 512:(cg + 1) * 512], in0=po[:, :], in1=x1[:, cg * 512:(cg + 1) * 512], op=ALU.add),
                             reads=[por, r_x1], writes=[r_x2])
                    ss, r_ss = ssr.next()
                    S.op("act", I_("activation", out=junk[:], in_=x2[:], func=AF.Square, accum_out=ss[:, 0:1]), reads=[r_x2], writes=[r_junk, r_ss])
                    S.op("act", I_("activation", out=ss[:, 1:2], in_=ss[:, 0:1], func=AF.Sqrt, bias=epsc[:, 0:1]), reads=[r_ss, r_epsc], writes=[r_ss])
                    S.op("dve", I_("reciprocal", out=ss[:, 1:2], in_=ss[:, 1:2]), reads=[r_ss], writes=[r_ss])
                    S.op("dve", I_("scalar_tensor_tensor", out=x2[:], in0=x2[:], scalar=ss[:, 1:2], in1=gfin[:], op0=ALU.mult, op1=ALU.mult),
                         reads=[r_x2, r_ss, r_gf], writes=[r_x2])
                    S.dma("sp", "p5o%d" % x2r.slot(), I_("dma_start", out=O[s][tt:tt + 128, :], in_=x2[:]), reads=[r_x2], final=True)
        st.close()

    import os as _os
    upto = int(_os.environ.get("K_UPTO", "5"))
    setup()
    if upto >= 1:
        phase1()
    if upto >= 2:
        phase2()
    if upto >= 3:
        phase3()
    if upto >= 4:
        phase4()
    if upto >= 5:
        phase5()
    S.finish()
    S.replay()
    return nc, S


_CACHE = {}


def _prep_common(inp):
    f = lambda a: np.ascontiguousarray(np.asarray(a, dtype=np.float32))
    c = dict(
        norm_g=f(inp["norm_g"]), final_g=f(inp["final_g"]).reshape(1, D), rel_bias=f(inp["rel_bias"]),
        w_in_mix=f(inp["w_in_mix"][0]), ret_decay=f(inp["ret_decay"][0]).reshape(1, 8),
        lam4=np.concatenate([f(inp["lam_q1"][0]), f(inp["lam_k1"][0]), f(inp["lam_q2"][0]), f(inp["lam_k2"][0])]).reshape(1, 256),
        diff_norm_g=f(inp["diff_norm_g"][0]).reshape(1, 128), w_out_mix=f(inp["w_out_mix"][0]),
        w_in_conv=f(inp["w_in_conv"][0]), conv_w=f(inp["conv_w"][0]), conv_b=f(inp["conv_b"][0]).reshape(1, 2048),
        conv_norm_g=f(inp["conv_norm_g"][0]).reshape(1, 2048), conv_norm_b=f(inp["conv_norm_b"][0]).reshape(1, 2048),
        w_out_conv=f(inp["w_out_conv"][0]))
    c.update(_consts())
    return c


def run_config(inp, NBA, NRB, NOB, assign):
    key = (NBA, NRB, NOB)
    if key not in _CACHE:
        _CACHE[key] = build_program(NBA, NRB, NOB)
    nc, _ = _CACHE[key]
    common = _prep_common(inp)
    xs = np.asarray(inp["x_sample"], dtype=np.float32)
    xp = np.asarray(inp["x_prompt"], dtype=np.float32)
    NB = NRB + NOB
    in_maps = []
    for (si, pi, rs) in assign:
        reg = list(range(rs, rs + NRB))
        oth = [b for b in range(NB) if b < rs or b >= rs + NRB]
        order = reg + oth
        tok = (np.array(order)[:, None] * 128 + np.arange(128)[None, :]).reshape(-1)
        m = dict(common)
        m["xa"] = np.ascontiguousarray(xs[si])
        m["xb"] = np.ascontiguousarray(xp[pi][tok])
        m["csa"] = _rope_table(np.arange(NBA * 128))
        m["csb"] = _rope_table(tok)
        fl = np.array([1.0 if b < rs else 0.0 for b in oth], dtype=np.float32).reshape(1, -1)
        if fl.shape[1] == 0:
            fl = np.zeros((1, 1), np.float32)
        m["fl"] = fl
        in_maps.append(m)
    res = run_bass_kernel_spmd(nc, in_maps, core_ids=list(range(len(assign))))
    return res.results


def kernel(**inputs):
    NBA, NRB, NOB = 32, 20, 44
    starts = [0, 14, 30, 44]
    assign = [(c, c // 4, starts[c % 4]) for c in range(8)]
    res = run_config(inputs, NBA, NRB, NOB, assign)
    y_prompt = np.zeros((2, 8192, D), np.float32)
    y_sample = np.zeros((8, 4096, D), np.float32)
    for c in range(8):
        y_sample[c] = res[c]["out_a"]
        qd = c % 4
        lo = (16 * qd - starts[qd]) * 128
        y_prompt[c // 4, qd * 2048:(qd + 1) * 2048] = res[c]["out_b"][lo:lo + 2048]
    return (y_prompt, y_sample)
```

```python
import contextlib
import math
import numpy as np
import concourse.bass as bass
import concourse.mybir as mybir
from concourse.bass_utils import run_bass_kernel_spmd

F32 = mybir.dt.float32
BF16 = mybir.dt.bfloat16
AF = mybir.ActivationFunctionType
ALU = mybir.AluOpType
AX = mybir.AxisListType

D = 1024
EPS = 1e-6


class Res:
    __slots__ = ("name", "w", "r", "multi")

    def __init__(self, name, multi=False):
        self.name = name
        self.w = []
        self.r = []
        self.multi = multi


class Sched:
    COMPUTE = ("pe", "act", "dve", "pool")
    ALL = ("pe", "act", "dve", "pool", "sp")

    def __init__(self, nc):
        self.nc = nc
        self.streams = {e: [] for e in self.ALL}
        self.tick = {e: 0 for e in self.COMPUTE}
        self.esem = {}
        self.old_esems = []
        self.nsem = 0
        self.waited = {e: {} for e in self.ALL}
        for e in self.COMPUTE:
            self._new_esem(e)
        self.chan = {}
        self.final_tokens = []
        self.ninst = 0

    def _alloc(self, name):
        self.nsem += 1
        return self.nc.alloc_semaphore(name=name)

    def _new_esem(self, e):
        if e in self.esem and self.tick[e] > 0:
            self.old_esems.append((self.esem[e], self.tick[e]))
        self.esem[e] = self._alloc("e_%s_%d" % (e, self.nsem))
        self.tick[e] = 0

    def _deps(self, reads, writes):
        toks = []
        for r in reads:
            toks.extend(r.w)
        for w in writes:
            toks.extend(w.w)
            toks.extend(w.r)
        return toks

    def _emit_waits(self, eng, toks, skip_self=False):
        wd = self.waited[eng]
        best = {}
        for (s, v) in toks:
            k = id(s)
            if skip_self and eng in self.esem and s is self.esem[eng]:
                continue
            if wd.get(k, 0) >= v:
                continue
            if best.get(k, (None, 0))[1] < v:
                best[k] = (s, v)
        for k, (s, v) in best.items():
            wd[k] = v
            self.streams[eng].append(("wait", s, v))
            self.ninst += 1

    def _commit(self, tok, reads, writes):
        for r in reads:
            r.r.append(tok)
            if len(r.r) > 64:
                r.r = self._compress(r.r)
        for w in writes:
            if w.multi:
                w.w.append(tok)
                if len(w.w) > 64:
                    w.w = self._compress(w.w)
            else:
                w.w = [tok]
                w.r = []

    @staticmethod
    def _compress(toks):
        best = {}
        for (s, v) in toks:
            if best.get(id(s), (None, 0))[1] < v:
                best[id(s)] = (s, v)
        return list(best.values())

    def op(self, eng, fn, reads=(), writes=(), inc=True, skip_self=False):
        toks = self._deps(reads, writes)
        self._emit_waits(eng, toks, skip_self=skip_self)
        if inc:
            self.tick[eng] += 1
            tok = (self.esem[eng], self.tick[eng])
            self.streams[eng].append(("op", fn, self.esem[eng]))
        else:
            tok = (self.esem[eng], self.tick[eng] + 1)
            self.streams[eng].append(("op", fn, None))
        self.ninst += 1
        self._commit(tok, reads, writes)
        return tok

    def dma(self, q, chan, fn, reads=(), writes=(), final=False):
        toks = self._deps(reads, writes)
        if chan == "su":
            toks = [t for t in toks if t[1] != INF]
        self._emit_waits(q, toks)
        if chan not in self.chan:
            self.chan[chan] = [self._alloc("c_" + chan), 0]
        c = self.chan[chan]
        c[1] += 16
        tok = (c[0], INF if chan == "su" else c[1])
        self.streams[q].append(("dma", fn, c[0]))
        self.ninst += 1
        self._commit(tok, reads, writes)
        if final:
            self.final_tokens.append(tok)
        return tok

    def all_tokens(self):
        toks = [(self.esem[e], self.tick[e]) for e in self.COMPUTE if self.tick[e] > 0]
        toks += list(self.old_esems)
        toks += [(c[0], c[1]) for k, c in self.chan.items() if c[1] > 0 and k != "su"]
        return toks

    def barrier(self):
        toks = self.all_tokens()
        for e in self.ALL:
            self._emit_waits(e, toks)
        for e in self.COMPUTE:
            if self.tick[e] > 0:
                self._new_esem(e)

    def finish(self):
        self._emit_waits("sp", self.final_tokens + self.all_tokens())

    def replay(self):
        nc = self.nc
        streams = self.streams
        with nc.Block() as block:
            def run(name, eng):
                for it in streams[name]:
                    if it[0] == "wait":
                        v = it[2]
                        if v == INF:
                            v = self.chan["su"][1]
                        eng.wait_ge(it[1], v)
                    elif it[0] == "op":
                        nm, a, k = it[1]
                        ins = getattr(eng, nm)(*a, **k)
                        if it[2] is not None:
                            ins.then_inc(it[2], 1)
                    else:
                        nm, a, k = it[1]
                        getattr(eng, nm)(*a, **k).then_inc(it[2], 16)

            @block.tensor
            def _(e):
                run("pe", e)

            @block.scalar
            def _(e):
                run("act", e)

            @block.vector
            def _(e):
                run("dve", e)

            @block.gpsimd
            def _(e):
                run("pool", e)

            @block.sync
            def _(e):
                run("sp", e)


def I_(name, *a, **k):
    return (name, a, k)


INF = float("inf")


class Ring:
    def __init__(self, items):
        self.items = items
        self.i = 0

    def slot(self):
        return (self.i - 1) % len(self.items)

    def next(self):
        x = self.items[self.i % len(self.items)]
        self.i += 1
        return x


def _t5_bucket_np(rel):
    nb = 16
    max_exact = 8
    ret = (rel > 0).astype(np.int32) * nb
    n = np.abs(rel)
    nf = np.maximum(n, 1).astype(np.float32)
    large = max_exact + (np.log(nf / np.float32(max_exact)) / np.float32(math.log(128 / max_exact))
                         * np.float32(nb - max_exact)).astype(np.int32)
    large = np.minimum(large, nb - 1)
    return ret + np.where(n < max_exact, n, large)


def _consts():
    j = np.arange(1280)
    rel = 639 - j
    b = _t5_bucket_np(rel.astype(np.int32))
    et = np.zeros((32, 1280), np.float32)
    et[b, j] = 1.0
    et[:, 1279] = 0.0
    k = np.arange(128, dtype=np.float32)
    dpos = np.maximum(k[None, :] - k[:, None], 0.0).astype(np.float32)
    dneg = np.maximum(k[:, None] - k[None, :], 0.0).astype(np.float32)
    col = np.stack([127.0 - k, k], axis=1).astype(np.float32)
    rowf = np.broadcast_to((k + 1.0)[None, :], (128, 128)).astype(np.float32).copy()
    rowb = np.broadcast_to((128.0 - k)[None, :], (128, 128)).astype(np.float32).copy()
    return dict(c_et=et, c_dpos=dpos, c_dneg=dneg, c_col=col, c_rowf=rowf, c_rowb=rowb)


def _rope_table(pos):
    inv = (1.0 / (np.float32(10000.0) ** (np.arange(0, 128, 2, dtype=np.float32) / np.float32(128)))).astype(np.float32)
    ang = pos.astype(np.float32)[:, None] * inv[None, :]
    return np.concatenate([np.cos(ang), np.sin(ang)], axis=1).astype(np.float32)


def build_program(NBA, NRB, NOB, debug=False):
    nc = bass.Bass("TRN2", target_bir_lowering=False)
    S = Sched(nc)
    SA = NBA * 128
    SRB = NRB * 128
    SBT = (NRB + NOB) * 128
    seqs = {"a": dict(nreg=NBA, noth=0, sreg=SA, sctx=SA), "b": dict(nreg=NRB, noth=NOB, sreg=SRB, sctx=SBT)}

    def din(name, shape, dt=F32):
        return nc.dram_tensor(name, list(shape), dt, kind="ExternalInput").ap()

    def dscr(name, shape, dt=BF16):
        return nc.dram_tensor(name, list(shape), dt, kind="Internal").ap()

    I = {}
    I["xa"] = din("xa", [SA, D]); I["xb"] = din("xb", [SBT, D])
    I["csa"] = din("csa", [SA, 128]); I["csb"] = din("csb", [SBT, 128])
    I["fl"] = din("fl", [1, max(NOB, 1)])
    I["norm_g"] = din("norm_g", [2, D]); I["final_g"] = din("final_g", [1, D])
    I["rel_bias"] = din("rel_bias", [32, 8])
    I["w_in_mix"] = din("w_in_mix", [D, 7168]); I["ret_decay"] = din("ret_decay", [1, 8])
    I["lam4"] = din("lam4", [1, 256]); I["diff_norm_g"] = din("diff_norm_g", [1, 128])
    I["w_out_mix"] = din("w_out_mix", [2048, D]); I["w_in_conv"] = din("w_in_conv", [D, 6144])
    I["conv_w"] = din("conv_w", [31, 2048]); I["conv_b"] = din("conv_b", [1, 2048])
    I["conv_norm_g"] = din("conv_norm_g", [1, 2048]); I["conv_norm_b"] = din("conv_norm_b", [1, 2048])
    I["w_out_conv"] = din("w_out_conv", [2048, D])
    I["c_et"] = din("c_et", [32, 1280]); I["c_dpos"] = din("c_dpos", [128, 128]); I["c_dneg"] = din("c_dneg", [128, 128])
    I["c_col"] = din("c_col", [128, 2]); I["c_rowf"] = din("c_rowf", [128, 128]); I["c_rowb"] = din("c_rowb", [128, 128])
    O = {"a": nc.dram_tensor("out_a", [SA, D], F32, kind="ExternalOutput").ap(),
         "b": nc.dram_tensor("out_b", [SRB, D], F32, kind="ExternalOutput").ap()}
    X = {"a": I["xa"], "b": I["xb"]}
    CS = {"a": I["csa"], "b": I["csb"]}

    SC = {}
    for s, q in seqs.items():
        SC[s] = dict(
            KT=dscr("KT" + s, [8, 128, q["sctx"]]), QT=dscr("QT" + s, [8, 128, q["sreg"]]),
            DVH=dscr("DVH" + s, [8, 128, q["sctx"] // 128, 128]), RV=dscr("RV" + s, [q["sctx"], 1024]),
            RK=dscr("RK" + s, [q["sctx"], 512]), RKT=dscr("RKT" + s, [4, 128, q["sreg"]]),
            RQT=dscr("RQT" + s, [4, 128, q["sreg"]]), GT=dscr("GT" + s, [q["sreg"], 2048]),
            Y=dscr("Y" + s, [q["sreg"], 2048]), X1=dscr("X1" + s, [q["sreg"], D], F32),
            GLU=dscr("GLU" + s, [2048, q["sreg"]]), SG=dscr("SG" + s, [2048, q["sreg"]]))
        SC[s]["res"] = {k: Res(k + s, multi=True) for k in list(SC[s].keys())}
    GREV = dscr("GREV", [8, 1280]); r_grev = Res("grev", multi=True)

    DBG = {}

    def dbg_out(name, shape, dt=F32):
        DBG[name] = nc.dram_tensor("dbg_" + name, list(shape), dt, kind="ExternalOutput").ap()
        return DBG[name]

    def bc_ap(src, off, n, npart=128):
        return bass.AP(src.tensor, off, [[0, npart], [1, n]])

    persist = contextlib.ExitStack()

    def sb(stack, name, shape, dt):
        return stack.enter_context(nc.sbuf_tensor(name, list(shape), dt))

    PS = []
    for i in range(8):
        t = nc.alloc_psum_tensor("psb%d" % i, [128, 512], F32)
        PS.append((t, Res("psb%d" % i)))

    qsel = {"i": 0}

    def ldq():
        return "sp"

    P = {}

    def ptile(name, shape, dt):
        t = sb(persist, name, shape, dt)
        P[name] = (t, Res(name))
        return P[name]

    ident_f, r_identf = ptile("ident_f", [128, 128], F32)
    ident_b, r_identb = ptile("ident_b", [128, 128], BF16)
    jmat_f, r_jf = ptile("jmat_f", [128, 128], F32)
    jmat_b, r_jb = ptile("jmat_b", [128, 128], BF16)
    ones_b, r_ones = ptile("ones_b", [128, 128], BF16)
    zero_c, r_zero = ptile("zero_c", [128, 1], F32)
    epsc, r_epsc = ptile("epsc", [128, 4], F32)
    g1, r_g1 = ptile("g1", [128, D], F32)
    g2, r_g2 = ptile("g2", [128, D], F32)
    gfin, r_gf = ptile("gfin", [128, D], F32)
    lamc, r_lam = ptile("lamc", [128, 4], F32)
    lf, r_lf = ptile("lf", [128, 8], F32)
    dec, r_dec = ptile("dec", [128, 8], F32)
    dmatT, r_dmat = ptile("dmatT", [128, 4, 128], F32)
    wfb, r_wfb = ptile("wfb", [128, 8], F32)
    QF, r_QF = ptile("QF", [128, 4, 128], F32)
    QB, r_QB = ptile("QB", [128, 4, 128], F32)
    farb, r_farb = ptile("farb", [128, 16], F32)
    dng, r_dng = ptile("dng", [128, 128], F32)
    NO1 = max(NOB, 1)
    flb, r_flb = ptile("flb", [128, NO1], F32)
    fla, r_fla = ptile("fla", [128, NO1], F32)
    oth_af, r_oaf = ptile("oth_af", [128, NO1, 4], F32)
    oth_ab, r_oab = ptile("oth_ab", [128, NO1, 4], F32)
    oth_fb, r_ofb = ptile("oth_fb", [128, NO1, 8], F32)
    cwT, r_cwT = ptile("cwT", [128, 16, 31], F32)
    cvb, r_cvb = ptile("cvb", [128, 16], F32)
    cng, r_cng = ptile("cng", [128, 16], F32)
    cnb, r_cnb = ptile("cnb", [128, 16], F32)

    import os as _os2
    _CUT = int(_os2.environ.get('K_SU', '99'))

    def _cut(k):
        return k >= _CUT

    def setup():
        st = contextlib.ExitStack()
        tmp = sb(st, "su_tmp", [128, 1280], F32); r_tmp = Res("su_tmp")
        tmp2 = sb(st, "su_tmp2", [128, 1280], F32); r_tmp2 = Res("su_tmp2")
        S.op("pool", I_("memset", ident_f[:], 1.0), writes=[r_identf])
        S.op("pool", I_("affine_select", out=ident_f[:], in_=ident_f[:], pattern=[[-1, 128]],
                                               compare_op=ALU.is_equal, fill=0.0, base=0, channel_multiplier=1),
             reads=[r_identf], writes=[r_identf])
        S.op("dve", I_("tensor_copy", out=ident_b[:], in_=ident_f[:]), reads=[r_identf], writes=[r_identb])
        S.op("pool", I_("memset", jmat_f[:], 1.0), writes=[r_jf])
        S.op("pool", I_("affine_select", out=jmat_f[:], in_=jmat_f[:], pattern=[[1, 128]],
                                               compare_op=ALU.is_equal, fill=0.0, base=-127, channel_multiplier=1),
             reads=[r_jf], writes=[r_jf])
        S.op("dve", I_("tensor_copy", out=jmat_b[:], in_=jmat_f[:]), reads=[r_jf], writes=[r_jb])
        S.op("dve", I_("memset", ones_b[:], 1.0), writes=[r_ones])
        S.op("dve", I_("memset", zero_c[:], 0.0), writes=[r_zero])
        for ci, cv in enumerate((D * EPS, 256 * EPS, 128 * EPS, EPS)):
            S.op("dve", I_("memset", epsc[:, ci:ci + 1], cv), writes=[r_epsc])
        if _cut(1):
            S.barrier(); st.close(); return
        for (t, r, src, off) in ((g1, r_g1, I["norm_g"], 0), (g2, r_g2, I["norm_g"], D), (gfin, r_gf, I["final_g"], 0)):
            S.dma("sp", "su", I_("dma_start", out=t[:], in_=bc_ap(src, off, D)), writes=[r])
            S.op("dve", I_("tensor_scalar", out=t[:], in0=t[:], scalar1=32.0, scalar2=None, op0=ALU.mult),
                 reads=[r], writes=[r])
        if _cut(2):
            S.barrier(); st.close(); return
        S.dma("sp", "su", I_("dma_start", out=tmp[:, 0:256], in_=bc_ap(I["lam4"], 0, 256)), writes=[r_tmp])
        S.op("dve", I_("tensor_tensor", out=tmp2[:, 0:64], in0=tmp[:, 0:64], in1=tmp[:, 64:128], op=ALU.mult), reads=[r_tmp], writes=[r_tmp2])
        S.op("dve", I_("tensor_tensor", out=tmp2[:, 64:128], in0=tmp[:, 128:192], in1=tmp[:, 192:256], op=ALU.mult), reads=[r_tmp], writes=[r_tmp2])
        S.op("dve", I_("reduce_sum", out=tmp2[:, 128:130], in_=tmp2[:, 0:128].rearrange("p (a b) -> p a b", a=2), axis=AX.X),
             reads=[r_tmp2], writes=[r_tmp2])
        S.op("act", I_("activation", out=tmp2[:, 130:132], in_=tmp2[:, 128:130], func=AF.Exp), reads=[r_tmp2], writes=[r_tmp2])
        S.op("dve", I_("tensor_tensor", out=lamc[:, 0:1], in0=tmp2[:, 130:131], in1=tmp2[:, 131:132], op=ALU.subtract),
             reads=[r_tmp2], writes=[r_lam])
        S.op("dve", I_("tensor_scalar", out=lamc[:, 0:1], in0=lamc[:, 0:1], scalar1=0.2, scalar2=None, op0=ALU.add), reads=[r_lam], writes=[r_lam])
        S.op("dve", I_("tensor_scalar", out=lamc[:, 1:2], in0=lamc[:, 0:1], scalar1=-1.0, scalar2=None, op0=ALU.mult), reads=[r_lam], writes=[r_lam])
        if _cut(3):
            S.barrier(); st.close(); return
        S.dma("sp", "su", I_("dma_start", out=lf[:], in_=bc_ap(I["ret_decay"], 0, 8)), writes=[r_lf])
        S.op("act", I_("activation", out=lf[:], in_=lf[:], func=AF.Exp), reads=[r_lf], writes=[r_lf])
        S.op("dve", I_("tensor_scalar", out=lf[:], in0=lf[:], scalar1=-1.0, scalar2=None, op0=ALU.mult), reads=[r_lf], writes=[r_lf])
        S.op("act", I_("activation", out=dec[:], in_=lf[:], func=AF.Exp, scale=128.0), reads=[r_lf], writes=[r_dec])
        if _cut(4):
            S.barrier(); st.close(); return
        dpos = sb(st, "su_dpos", [128, 128], F32); dneg = sb(st, "su_dneg", [128, 128], F32)
        col = sb(st, "su_col", [128, 2], F32); rowf = sb(st, "su_rowf", [128, 128], F32); rowb = sb(st, "su_rowb", [128, 128], F32)
        r_c = Res("su_consts")
        for (t, nm) in ((dpos, "c_dpos"), (dneg, "c_dneg"), (col, "c_col"), (rowf, "c_rowf"), (rowb, "c_rowb")):
            S.dma("sp", "su", I_("dma_start", out=t[:], in_=I[nm]), writes=[r_c])
        SCL = 128.0 ** -0.5
        for h in range(4):
            if 'd' not in _os2.environ.get('K_SKIP', ''):
                S.op("dve", I_("tensor_scalar", out=tmp[:, 0:128], in0=dpos[:], scalar1=lf[:, h:h + 1], scalar2=None, op0=ALU.mult),
                     reads=[r_c, r_lf], writes=[r_tmp])
                S.op("dve", I_("scalar_tensor_tensor", out=tmp[:, 0:128], in0=dneg[:], scalar=lf[:, 4 + h:5 + h], in1=tmp[:, 0:128],
                                                                        op0=ALU.mult, op1=ALU.add), reads=[r_c, r_lf, r_tmp], writes=[r_tmp])
                S.op("act", I_("activation", out=dmatT[:, h, :], in_=tmp[:, 0:128], func=AF.Exp), reads=[r_tmp], writes=[r_dmat])
            if 'q' not in _os2.environ.get('K_SKIP', ''):
                S.op("act", I_("activation", out=QF[:, h, :], in_=rowf[:], func=AF.Exp, scale=lf[:, h:h + 1]), reads=[r_c, r_lf], writes=[r_QF])
                S.op("act", I_("activation", out=QB[:, h, :], in_=rowb[:], func=AF.Exp, scale=lf[:, 4 + h:5 + h]), reads=[r_c, r_lf], writes=[r_QB])
        S.op("dve", I_("tensor_scalar", out=dmatT[:], in0=dmatT[:], scalar1=SCL, scalar2=None, op0=ALU.mult), reads=[r_dmat], writes=[r_dmat])
        if 'w' not in _os2.environ.get('K_SKIP', ''):
            S.op("act", I_("activation", out=wfb[:, 0:4], in_=lf[:, 0:4], func=AF.Exp, scale=col[:, 0:1]), reads=[r_c, r_lf], writes=[r_wfb])
            S.op("act", I_("activation", out=wfb[:, 4:8], in_=lf[:, 4:8], func=AF.Exp, scale=col[:, 1:2]), reads=[r_c, r_lf], writes=[r_wfb])
        S.op("dve", I_("tensor_scalar", out=wfb[:], in0=wfb[:], scalar1=SCL, scalar2=None, op0=ALU.mult), reads=[r_wfb], writes=[r_wfb])
        if _cut(5):
            S.barrier(); st.close(); return
        S.dma("sp", "su", I_("dma_start", out=farb[:, 0:8], in_=bc_ap(I["rel_bias"], 15 * 8, 8)), writes=[r_farb])
        S.dma("sp", "su", I_("dma_start", out=farb[:, 8:16], in_=bc_ap(I["rel_bias"], 31 * 8, 8)), writes=[r_farb])
        S.dma("sp", "su", I_("dma_start", out=dng[:], in_=bc_ap(I["diff_norm_g"], 0, 128)), writes=[r_dng])
        S.op("dve", I_("tensor_scalar", out=dng[:], in0=dng[:], scalar1=0.8, scalar2=None, op0=ALU.mult), reads=[r_dng], writes=[r_dng])
        S.dma("sp", "su", I_("dma_start", out=flb[:], in_=bc_ap(I["fl"], 0, NO1)), writes=[r_flb])
        S.op("dve", I_("tensor_scalar", out=fla[:], in0=flb[:], scalar1=-1.0, scalar2=1.0, op0=ALU.mult, op1=ALU.add), reads=[r_flb], writes=[r_fla])
        for h in range(4):
            S.op("dve", I_("scalar_tensor_tensor", out=oth_af[:, :, h], in0=flb[:], scalar=dec[:, h:h + 1], in1=fla[:],
                                                                    op0=ALU.mult, op1=ALU.add), reads=[r_flb, r_fla, r_dec], writes=[r_oaf])
            S.op("dve", I_("scalar_tensor_tensor", out=oth_ab[:, :, h], in0=fla[:], scalar=dec[:, 4 + h:5 + h], in1=flb[:],
                                                                    op0=ALU.mult, op1=ALU.add), reads=[r_flb, r_fla, r_dec], writes=[r_oab])
        for h in range(8):
            S.op("dve", I_("tensor_scalar", out=tmp[:, 0:NO1], in0=fla[:], scalar1=farb[:, 8 + h:9 + h], scalar2=None, op0=ALU.mult),
                 reads=[r_fla, r_farb], writes=[r_tmp])
            S.op("dve", I_("scalar_tensor_tensor", out=oth_fb[:, :, h], in0=flb[:], scalar=farb[:, h:h + 1], in1=tmp[:, 0:NO1],
                                                                    op0=ALU.mult, op1=ALU.add), reads=[r_flb, r_farb, r_tmp], writes=[r_ofb])
        if _cut(6):
            S.barrier(); st.close(); return
        cw_sb = sb(st, "su_cw", [31, 2048], F32); r_cw = Res("su_cw")
        S.dma("sp", "su", I_("dma_start", out=cw_sb[:], in_=I["conv_w"]), writes=[r_cw])
        for cb in range(16):
            pt, pr = PS[3 + (cb % 4)]
            S.op("pe", I_("transpose", pt[:, 0:31], cw_sb[:, cb * 128:(cb + 1) * 128], ident_f[0:31, 0:31]),
                 reads=[r_cw, r_identf], writes=[pr])
            S.op("act", I_("activation", out=cwT[:, cb, :], in_=pt[:, 0:31], func=AF.Copy), reads=[pr], writes=[r_cwT])
        with nc.allow_non_contiguous_dma(reason="tiny per-channel params"):
            for (t, r, nm) in ((cvb, r_cvb, "conv_b"), (cng, r_cng, "conv_norm_g"), (cnb, r_cnb, "conv_norm_b")):
                S.dma("sp", "su", I_("dma_start", out=t[:], in_=bass.AP(I[nm].tensor, 0, [[1, 128], [128, 16]]), allow_slow_non_contiguous=True), writes=[r])
        if _cut(7):
            S.barrier(); st.close(); return
        rb = sb(st, "su_rb", [32, 8], F32); et = sb(st, "su_et", [32, 1280], F32); r_rb = Res("su_rb")
        S.dma("sp", "su", I_("dma_start", out=rb[:], in_=I["rel_bias"]), writes=[r_rb])
        S.dma("sp", "su", I_("dma_start", out=et[:], in_=I["c_et"]), writes=[r_rb])
        grev_sb = sb(st, "su_grev", [8, 1280], BF16); r_gs = Res("su_grev")
        for ci, (c0, cn) in enumerate(((0, 512), (512, 512), (1024, 256))):
            pt, pr = PS[ci]
            S.op("pe", I_("matmul", pt[0:8, 0:cn], lhsT=rb[:, :], rhs=et[:, c0:c0 + cn], start=True, stop=True),
                 reads=[r_rb], writes=[pr])
            S.op("act", I_("activation", out=grev_sb[:, c0:c0 + cn], in_=pt[0:8, 0:cn], func=AF.Copy),
                 reads=[pr], writes=[r_gs])
        S.dma("sp", "su2", I_("dma_start", out=GREV, in_=grev_sb[:]), reads=[r_gs], writes=[r_grev])
        S.barrier()
        st.close()

    def phase1():
        st = contextlib.ExitStack()
        W = sb(st, "p1_W", [128, 8, 7168], BF16); r_W = Res("p1_W")
        for j in range(8):
            S.dma("pool", "p1w", I_("dma_start", out=W[:, j, :], in_=I["w_in_mix"][j * 128:(j + 1) * 128, :]), writes=[r_W])
        xr = Ring([(sb(st, "p1_x%d" % i, [128, D], F32), Res("x")) for i in range(2)])
        csr = Ring([(sb(st, "p1_cs%d" % i, [128, 128], F32), Res("cs")) for i in range(2)])
        hr = Ring([(sb(st, "p1_h%d" % i, [128, D], BF16), Res("h")) for i in range(2)])
        junk = sb(st, "p1_junk", [128, D], BF16); r_junk = Res("junk")
        ssr = Ring([(sb(st, "p1_ss%d" % i, [128, 2], F32), Res("ss")) for i in range(2)])
        hTr = Ring([(sb(st, "p1_hT%d" % i, [128, 8, 512], BF16), Res("hT")) for i in range(1)])
        qkr = Ring([(sb(st, "p1_qk%d" % i, [128, 1024], F32), Res("qk")) for i in range(1)])
        rt = [(sb(st, "p1_rt%d" % i, [128, 512], F32), Res("rt")) for i in range(4)]
        ropr = Ring([(sb(st, "p1_rop%d" % i, [128, 1024], BF16), Res("rop")) for i in range(2)])
        rvr = Ring([(sb(st, "p1_rv%d" % i, [128, 1024], BF16), Res("rv")) for i in range(2)])
        gtr = Ring([(sb(st, "p1_gt%d" % i, [128, 2048], BF16), Res("gt")) for i in range(1)])
        dvr = Ring([(sb(st, "p1_dv%d" % i, [128, 4, 1024], BF16), Res("dv")) for i in range(1)])
        fmr = Ring([(sb(st, "p1_fm%d" % i, [128, 512], BF16), Res("fm")) for i in range(4)])
        qkTr = Ring([(sb(st, "p1_qkT%d" % i, [128, 8, 512], BF16), Res("qkT")) for i in range(1)])
        psT = Ring(PS[0:2]); pstm = Ring(PS[2:5]); psfm = Ring(PS[5:7]); psT2 = Ring(PS[7:8])

        for s in ("a", "b"):
            q = seqs[s]
            sc = SC[s]; rs = sc["res"]
            ngroups = (q["nreg"] + q["noth"]) // 4
            for g in range(ngroups):
                full = (g * 4) < q["nreg"]
                t0 = g * 512
                hT, r_hT = hTr.next()
                dv, r_dv = dvr.next()
                qkT, r_qkT = qkTr.next()
                for i in range(4):
                    tt = t0 + i * 128
                    x, r_x = xr.next()
                    cs, r_cs = csr.next()
                    S.dma("sp", "p1x%d" % xr.slot(), I_("dma_start", out=x[:], in_=X[s][tt:tt + 128, :]), writes=[r_x])
                    S.dma("sp", "p1c%d" % csr.slot(), I_("dma_start", out=cs[:], in_=CS[s][tt:tt + 128, :]), writes=[r_cs])
                    ss, r_ss = ssr.next()
                    S.op("act", I_("activation", out=junk[:], in_=x[:], func=AF.Square, accum_out=ss[:, 0:1]),
                         reads=[r_x], writes=[r_junk, r_ss])
                    S.op("act", I_("activation", out=ss[:, 1:2], in_=ss[:, 0:1], func=AF.Sqrt, bias=epsc[:, 0:1]), reads=[r_ss, r_epsc], writes=[r_ss])
                    S.op("dve", I_("reciprocal", out=ss[:, 1:2], in_=ss[:, 1:2]), reads=[r_ss], writes=[r_ss])
                    h, r_h = hr.next()
                    S.op("act", I_("activation", out=x[:], in_=x[:], func=AF.Copy, scale=ss[:, 1:2]), reads=[r_x, r_ss], writes=[r_x])
                    S.op("pool", I_("tensor_tensor", out=h[:], in0=x[:], in1=g1[:], op=ALU.mult), reads=[r_x, r_g1], writes=[r_h])
                    pt, pr = psT.next()
                    ptb = pt[:].bitcast(BF16)
                    for j in range(8):
                        S.op("pe", I_("transpose", ptb[:, j * 128:(j + 1) * 128], h[:, j * 128:(j + 1) * 128], ident_b[:]),
                             reads=[r_h, r_identb], writes=[pr], inc=(j == 7), skip_self=True)
                    S.op("act", I_("activation", out=hT[:, :, i * 128:(i + 1) * 128], in_=ptb.rearrange("p (j t) -> p j t", j=8), func=AF.Copy),
                         reads=[pr], writes=[r_hT])
                    if full:
                        cgs = [("qk", 0), ("qk", 512), ("rv", 1024), ("rv", 1536), ("rg", 2048), ("rg", 2560),
                               ("dv", 5120), ("dv", 5632), ("dg", 6144), ("dg", 6656)]
                    else:
                        cgs = [("qk", 512), ("rv", 1024), ("rv", 1536), ("dv", 5120), ("dv", 5632)]
                    qk, r_qk = qkr.next()
                    rv, r_rv = rvr.next()
                    gt, r_gt = gtr.next()
                    for (kind, c0) in cgs:
                        pm, pmr = pstm.next()
                        for j in range(8):
                            S.op("pe", I_("matmul", pm[:, :], lhsT=hT[:, j, i * 128:(i + 1) * 128], rhs=W[:, j, c0:c0 + 512],
                                                                                     start=(j == 0), stop=(j == 7)),
                                 reads=[r_hT, r_W], writes=[pmr], inc=(j == 7), skip_self=True)
                        if kind == "qk":
                            S.op("act", I_("activation", out=qk[:, c0:c0 + 512], in_=pm[:, :], func=AF.Copy), reads=[pmr], writes=[r_qk])
                        elif kind == "rv":
                            S.op("dve", I_("tensor_copy", out=rv[:, c0 - 1024:c0 - 512], in_=pm[:, :]), reads=[pmr], writes=[r_rv])
                        elif kind == "rg":
                            S.op("act", I_("activation", out=gt[:, c0 - 2048:c0 - 1536], in_=pm[:, :], func=AF.Silu), reads=[pmr], writes=[r_gt])
                        elif kind == "dg":
                            S.op("act", I_("activation", out=gt[:, c0 - 6144 + 1024:c0 - 6144 + 1536], in_=pm[:, :], func=AF.Silu), reads=[pmr], writes=[r_gt])
                        elif kind == "dv":
                            S.op("dve", I_("tensor_copy", out=dv[:, i, c0 - 5120:c0 - 5120 + 512], in_=pm[:, :]), reads=[pmr], writes=[r_dv])
                    S.dma("sp", "p1rv%d" % rvr.slot(), I_("dma_start", out=sc["RV"][tt:tt + 128, :], in_=rv[:]), reads=[r_rv], writes=[rs["RV"]])
                    if full:
                        S.dma("sp", "p1gt%d" % gtr.slot(), I_("dma_start", out=sc["GT"][tt:tt + 128, :], in_=gt[:]), reads=[r_gt], writes=[rs["GT"]])
                    hlo = 0 if full else 4
                    nh = 8 - hlo
                    rop, r_rop = ropr.next()
                    v4 = qk[:].rearrange("p (h c d) -> p h c d", h=8, c=2)
                    o4 = rop[:].rearrange("p (h c d) -> p h c d", h=8, c=2)
                    x1 = v4[:, hlo:8, 0, :]; x2 = v4[:, hlo:8, 1, :]
                    cosb = cs[:, 0:64].unsqueeze(1).to_broadcast([128, nh, 64]); sinb = cs[:, 64:128].unsqueeze(1).to_broadcast([128, nh, 64])
                    ta = [rt[k][0][:, 0:nh * 64].rearrange("p (h d) -> p h d", h=nh) for k in range(4)]
                    S.op("dve", I_("tensor_tensor", out=ta[0], in0=x1, in1=cosb, op=ALU.mult), reads=[r_qk, r_cs], writes=[rt[0][1]])
                    S.op("dve", I_("tensor_tensor", out=ta[1], in0=x2, in1=sinb, op=ALU.mult), reads=[r_qk, r_cs], writes=[rt[1][1]])
                    S.op("dve", I_("tensor_tensor", out=o4[:, hlo:8, 0, :], in0=ta[0], in1=ta[1], op=ALU.subtract),
                         reads=[rt[0][1], rt[1][1]], writes=[r_rop])
                    S.op("pool", I_("tensor_tensor", out=ta[2], in0=x1, in1=sinb, op=ALU.mult), reads=[r_qk, r_cs], writes=[rt[2][1]])
                    S.op("pool", I_("tensor_tensor", out=ta[3], in0=x2, in1=cosb, op=ALU.mult), reads=[r_qk, r_cs], writes=[rt[3][1]])
                    S.op("pool", I_("tensor_tensor", out=o4[:, hlo:8, 1, :], in0=ta[2], in1=ta[3], op=ALU.add),
                         reads=[rt[2][1], rt[3][1]], writes=[r_rop])
                    S.dma("sp", "p1rk%d" % ropr.slot(), I_("dma_start", out=sc["RK"][tt:tt + 128, :], in_=rop[:, 512:1024]), reads=[r_rop], writes=[rs["RK"]])
                    if full:
                        p2, p2r = psT2.next()
                        p2b = p2[:].bitcast(BF16)
                        for j in range(8):
                            S.op("pe", I_("transpose", p2b[:, j * 128:(j + 1) * 128], rop[:, j * 128:(j + 1) * 128], ident_b[:]),
                                 reads=[r_rop, r_identb], writes=[p2r], inc=(j == 7), skip_self=True)
                        S.op("dve", I_("tensor_copy", out=qkT[:, :, i * 128:(i + 1) * 128], in_=p2b.rearrange("p (j t) -> p j t", j=8)),
                             reads=[p2r], writes=[r_qkT])
                kb0 = g * 4
                for hh in range(8):
                    S.dma("sp", "p1dv%d" % dvr.slot(), I_("dma_start", out=sc["DVH"][hh, :, kb0:kb0 + 4, :], in_=dv[:, :, hh * 128:(hh + 1) * 128]),
                          reads=[r_dv], writes=[rs["DVH"]])
                if full:
                    S.dma("sp", "p1qkT%d" % qkTr.slot(), I_("dma_start", out=sc["RQT"][:, :, t0:t0 + 512].rearrange("h d t -> d h t"), in_=qkT[:, 0:4, :]),
                          reads=[r_qkT], writes=[rs["RQT"]])
                    S.dma("sp", "p1qkT%d" % qkTr.slot(), I_("dma_start", out=sc["RKT"][:, :, t0:t0 + 512].rearrange("h d t -> d h t"), in_=qkT[:, 4:8, :]),
                          reads=[r_qkT], writes=[rs["RKT"]])
                blocks = ([("dq", hh) for hh in range(8)] if full else []) + [("dk", hh) for hh in range(8)]
                for (kind, hh) in blocks:
                    c0 = (3072 if kind == "dq" else 4096) + hh * 128
                    pm, pmr = psfm.next()
                    for j in range(8):
                        S.op("pe", I_("matmul", pm[:, :], lhsT=W[:, j, c0:c0 + 128], rhs=hT[:, j, :], start=(j == 0), stop=(j == 7)),
                             reads=[r_hT, r_W], writes=[pmr], inc=(j == 7), skip_self=True)
                    fm, r_fm = fmr.next()
                    slot = fmr.slot()
                    if kind == "dq":
                        S.op("act", I_("activation", out=fm[:], in_=pm[:, :], func=AF.Copy, scale=0.125), reads=[pmr], writes=[r_fm])
                        S.dma("sp", "p1fm%d" % slot, I_("dma_start", out=sc["QT"][hh, :, t0:t0 + 512], in_=fm[:]), reads=[r_fm], writes=[rs["QT"]])
                    else:
                        S.op("dve", I_("tensor_copy", out=fm[:], in_=pm[:, :]), reads=[pmr], writes=[r_fm])
                        S.dma("sp", "p1fm%d" % slot, I_("dma_start", out=sc["KT"][hh, :, t0:t0 + 512], in_=fm[:]), reads=[r_fm], writes=[rs["KT"]])
        S.barrier()
        st.close()

    def phase2():
        st = contextlib.ExitStack()
        NRmax = max(NBA, NRB)
        R32 = sb(st, "p2_R", [128, 4, 256], F32); r_R = Res("R")
        L32 = sb(st, "p2_L", [128, 4, 256], F32); r_L = Res("L")
        Rbf = Ring([(sb(st, "p2_Rbf%d" % i, [128, 4, 256], BF16), Res("Rbf")) for i in range(2)])
        Lst = sb(st, "p2_Lst", [128, NRmax, 4, 256], BF16)
        r_Lst = [Res("Lst%d" % n) for n in range(NRmax)]
        rkr = Ring([(sb(st, "p2_rk%d" % i, [128, 512], BF16), Res("rk")) for i in range(3)])
        rvr = Ring([(sb(st, "p2_rv%d" % i, [128, 1024], BF16), Res("rv")) for i in range(3)])
        rkwr = Ring([(sb(st, "p2_rkw%d" % i, [128, 512], BF16), Res("rkw")) for i in range(2)])
        qTr = Ring([(sb(st, "p2_qT%d" % i, [128, 4, 128], BF16), Res("qT")) for i in range(2)])
        kTr = Ring([(sb(st, "p2_kT%d" % i, [128, 4, 128], BF16), Res("kT")) for i in range(2)])
        ATr = Ring([(sb(st, "p2_AT%d" % i, [128, 128], BF16), Res("AT")) for i in range(3)])
        qfr = Ring([(sb(st, "p2_qf%d" % i, [128, 128], BF16), Res("qf")) for i in range(3)])
        qbr = Ring([(sb(st, "p2_qb%d" % i, [128, 128], BF16), Res("qb")) for i in range(3)])
        kvt = Ring([(sb(st, "p2_kvt%d" % i, [128, 256], F32), Res("kvt")) for i in range(2)])
        ssr = Ring([(sb(st, "p2_ss%d" % i, [128, 2], F32), Res("ss")) for i in range(4)])
        junk = sb(st, "p2_junk", [128, 256], BF16); r_junk = Res("junk")
        ysr = Ring([(sb(st, "p2_ys%d" % i, [128, 1024], BF16), Res("ys")) for i in range(2)])
        ps_s = Ring(PS[0:2]); ps_o = Ring(PS[2:5]); ps_kv = Ring(PS[5:8])

        def load_kv(s, tt):
            rk, r_rk = rkr.next(); rv, r_rv = rvr.next()
            S.dma("sp", "p2rk%d" % rkr.slot(), I_("dma_start", out=rk[:], in_=SC[s]["RK"][tt:tt + 128, :]), reads=[SC[s]["res"]["RK"]], writes=[r_rk])
            S.dma("sp", "p2rv%d" % rvr.slot(), I_("dma_start", out=rv[:], in_=SC[s]["RV"][tt:tt + 128, :]), reads=[SC[s]["res"]["RV"]], writes=[r_rv])
            return rk, r_rk, rv, r_rv

        def kv_summ(rk, r_rk, rv, r_rv, wofs):
            rkw, r_rkw = rkwr.next()
            wv = wfb[:, wofs:wofs + 4].unsqueeze(2).to_broadcast([128, 4, 128])
            S.op("pool", I_("tensor_tensor", out=rkw[:].rearrange("p (h d) -> p h d", h=4), in0=rk[:].rearrange("p (h d) -> p h d", h=4), in1=wv, op=ALU.mult),
                 reads=[r_rk, r_wfb], writes=[r_rkw])
            outs = []
            for hp in range(2):
                pk, pkr = ps_kv.next()
                for hq in range(2):
                    h = hp * 2 + hq
                    S.op("pe", I_("matmul", pk[:, hq * 256:(hq + 1) * 256], lhsT=rkw[:, h * 128:(h + 1) * 128], rhs=rv[:, h * 256:(h + 1) * 256], start=True, stop=True),
                         reads=[r_rkw, r_rv], writes=[pkr], skip_self=True)
                    outs.append((pk[:, hq * 256:(hq + 1) * 256], pkr))
            return outs

        for s in ("a", "b"):
            q = seqs[s]
            NR, NO = q["nreg"], q["noth"]
            S.op("dve", I_("memset", R32[:], 0.0), writes=[r_R])
            S.op("dve", I_("memset", L32[:], 0.0), writes=[r_L])
            for direction in (0, 1):
                order = range(NO) if direction == 0 else range(NO - 1, -1, -1)
                st32, r_st = (R32, r_R) if direction == 0 else (L32, r_L)
                acoef = oth_af if direction == 0 else oth_ab
                r_ac = r_oaf if direction == 0 else r_oab
                bcoef, r_bc = (flb, r_flb) if direction == 0 else (fla, r_fla)
                for jj in order:
                    tt = (NR + jj) * 128
                    rk, r_rk, rv, r_rv = load_kv(s, tt)
                    outs = kv_summ(rk, r_rk, rv, r_rv, 0 if direction == 0 else 4)
                    for h in range(4):
                        pa, par = outs[h]
                        kt, r_kt = kvt.next()
                        S.op("act", I_("activation", out=kt[:], in_=pa, func=AF.Copy, scale=bcoef[:, jj:jj + 1]), reads=[par, r_bc], writes=[r_kt])
                        S.op("dve", I_("scalar_tensor_tensor", out=st32[:, h, :], in0=st32[:, h, :], scalar=acoef[:, jj, h:h + 1], in1=kt[:], op0=ALU.mult, op1=ALU.add),
                             reads=[r_st, r_ac, r_kt], writes=[r_st])
            for n in range(NR - 1, -1, -1):
                S.op("act", I_("activation", out=Lst[:, n, :, :], in_=L32[:], func=AF.Copy), reads=[r_L], writes=[r_Lst[n]])
                if n == 0:
                    break
                rk, r_rk, rv, r_rv = load_kv(s, n * 128)
                outs = kv_summ(rk, r_rk, rv, r_rv, 4)
                for h in range(4):
                    pa, par = outs[h]
                    S.op("dve", I_("scalar_tensor_tensor", out=L32[:, h, :], in0=L32[:, h, :], scalar=dec[:, 4 + h:5 + h], in1=pa, op0=ALU.mult, op1=ALU.add),
                         reads=[r_L, r_dec, par], writes=[r_L])
            for n in range(NR):
                tt = n * 128
                rb_, r_rb = Rbf.next()
                S.op("act", I_("activation", out=rb_[:], in_=R32[:], func=AF.Copy), reads=[r_R], writes=[r_rb])
                rk, r_rk, rv, r_rv = load_kv(s, tt)
                qT, r_qT = qTr.next(); kT, r_kT = kTr.next()
                S.dma("sp", "p2qT%d" % qTr.slot(), I_("dma_start", out=qT[:], in_=SC[s]["RQT"][:, :, tt:tt + 128].rearrange("h d t -> d h t")),
                      reads=[SC[s]["res"]["RQT"]], writes=[r_qT])
                S.dma("sp", "p2kT%d" % kTr.slot(), I_("dma_start", out=kT[:], in_=SC[s]["RKT"][:, :, tt:tt + 128].rearrange("h d t -> d h t")),
                      reads=[SC[s]["res"]["RKT"]], writes=[r_kT])
                ys, r_ys = ysr.next()
                psS, psSr = ps_s.next()
                for h in range(4):
                    S.op("pe", I_("matmul", psS[:, h * 128:(h + 1) * 128], lhsT=kT[:, h, :], rhs=qT[:, h, :], start=True, stop=True),
                         reads=[r_kT, r_qT], writes=[psSr], skip_self=True)
                for h in range(4):
                    AT, r_AT = ATr.next(); qf, r_qf = qfr.next(); qb, r_qb = qbr.next()
                    S.op("dve", I_("tensor_tensor", out=AT[:], in0=psS[:, h * 128:(h + 1) * 128], in1=dmatT[:, h, :], op=ALU.mult),
                         reads=[psSr, r_dmat], writes=[r_AT])
                    S.op("pool", I_("tensor_tensor", out=qf[:], in0=qT[:, h, :], in1=QF[:, h, :], op=ALU.mult), reads=[r_qT, r_QF], writes=[r_qf])
                    S.op("pool", I_("tensor_tensor", out=qb[:], in0=qT[:, h, :], in1=QB[:, h, :], op=ALU.mult), reads=[r_qT, r_QB], writes=[r_qb])
                    po, por = ps_o.next()
                    S.op("pe", I_("matmul", po[:, 0:256], lhsT=AT[:], rhs=rv[:, h * 256:(h + 1) * 256], start=True, stop=False),
                         reads=[r_AT, r_rv], writes=[por], inc=False, skip_self=True)
                    S.op("pe", I_("matmul", po[:, 0:256], lhsT=qf[:], rhs=rb_[:, h, :], start=False, stop=False),
                         reads=[r_qf, r_rb], writes=[por], inc=False, skip_self=True)
                    S.op("pe", I_("matmul", po[:, 0:256], lhsT=qb[:], rhs=Lst[:, n, h, :], start=False, stop=True),
                         reads=[r_qb, r_Lst[n]], writes=[por], skip_self=True)
                    ss, r_ss = ssr.next()
                    S.op("act", I_("activation", out=junk[:], in_=po[:, 0:256], func=AF.Square, accum_out=ss[:, 0:1]), reads=[por], writes=[r_junk, r_ss])
                    S.op("act", I_("activation", out=ss[:, 1:2], in_=ss[:, 0:1], func=AF.Sqrt, bias=epsc[:, 1:2]), reads=[r_ss, r_epsc], writes=[r_ss])
                    S.op("dve", I_("reciprocal", out=ss[:, 1:2], in_=ss[:, 1:2]), reads=[r_ss], writes=[r_ss])
                    S.op("dve", I_("tensor_scalar", out=ys[:, h * 256:(h + 1) * 256], in0=po[:, 0:256], scalar1=ss[:, 1:2], scalar2=16.0, op0=ALU.mult, op1=ALU.mult),
                         reads=[por, r_ss], writes=[r_ys])
                S.dma("sp", "p2ys%d" % ysr.slot(), I_("dma_start", out=SC[s]["Y"][tt:tt + 128, 0:1024], in_=ys[:]), reads=[r_ys], writes=[SC[s]["res"]["Y"]])
                if n < NR - 1:
                    outs = kv_summ(rk, r_rk, rv, r_rv, 0)
                    for h in range(4):
                        pa, par = outs[h]
                        S.op("dve", I_("scalar_tensor_tensor", out=R32[:, h, :], in0=R32[:, h, :], scalar=dec[:, h:h + 1], in1=pa, op0=ALU.mult, op1=ALU.add),
                             reads=[r_R, r_dec, par], writes=[r_R])
        S.barrier()
        st.close()

    def phase3():
        st = contextlib.ExitStack()
        SCmax = max(SA, SBT); SRmax = max(SA, SRB); NKmax = SCmax // 128
        KTr = Ring([(sb(st, "p3_KT%d" % i, [128, SCmax], BF16), Res("KT")) for i in range(2)])
        QTr = Ring([(sb(st, "p3_QT%d" % i, [128, 2, SRmax], BF16), Res("QT")) for i in range(2)])
        for (qt_, r_qt_) in QTr.items:
            S.op("pool", I_("memset", qt_[64:128, 0, :], 0.0), writes=[r_qt_])
            S.op("pool", I_("memset", qt_[0:64, 1, :], 0.0), writes=[r_qt_])
        Vr = Ring([(sb(st, "p3_V%d" % i, [128, NKmax, 130], BF16), Res("V")) for i in range(2)])
        Hr = Ring([(sb(st, "p3_H%d" % i, [128, 6, 512], BF16), Res("H")) for i in range(2)])
        for (v, r_v) in Vr.items:
            S.op("pool", I_("memset", v[:, :, 128:130], 1.0), writes=[r_v])
        ptr = Ring([(sb(st, "p3_pt%d" % i, [128, 512], BF16), Res("pt")) for i in range(8)])
        Oe = [(sb(st, "p3_Oe%d" % i, [128, 130], F32), Res("Oe")) for i in range(4)]
        rcp = Ring([(sb(st, "p3_rc%d" % i, [128, 12], F32), Res("rc")) for i in range(4)])
        junkf = sb(st, "p3_junkf", [128, 128], F32); r_junkf = Res("junkf")
        dr = Ring([(sb(st, "p3_d%d" % i, [128, 128], F32), Res("d")) for i in range(4)])
        junk = sb(st, "p3_junk", [128, 128], BF16); r_junk = Res("junk")
        yor = Ring([(sb(st, "p3_yo%d" % i, [128, 128], BF16), Res("yo")) for i in range(4)])
        psS = Ring(PS[0:3]); psOT = Ring(PS[3:5]); psZ = Ring(PS[5:7]); psE = Ring(PS[7:8])
        OTnr = Ring([(sb(st, "p3_OTn%d" % i, [128, 512], F32), Res("OTn")) for i in range(2)])
        ones_f128 = sb(st, "p3_ones_f128", [128, 128], F32)
        r_onesf128 = Res("onesf128")
        S.op("pool", I_("memset", ones_f128[:], 1.0), writes=[r_onesf128])
        accr = Ring([(sb(st, "p3_acc%d" % i, [128, 512], F32), Res("acc")) for i in range(2)])
        accpr = Ring([(sb(st, "p3_accp%d" % i, [128, 512], F32), Res("accp")) for i in range(2)])
        deferred = []
        DEFER = 6
        OTer = Ring([(sb(st, "p3_OTe%d" % i, [128, 512], F32), Res("OTe")) for i in range(2)])
        ones_f = sb(st, "p3_ones_f", [128, 2], F32); r_onesf = Res("onesf")
        S.op("pool", I_("memset", ones_f[:], 1.0), writes=[r_onesf])
        rnd = {}

        LA = 2
        heads = [(s, h) for s in ("a", "b") for h in range(8)]
        bufs = [(KTr.items[i], QTr.items[i], Vr.items[i], Hr.items[i]) for i in range(2)]

        def emit_loads(hi):
            s, h = heads[hi]
            q = seqs[s]
            NR, NO, SR, SCX = q["nreg"], q["noth"], q["sreg"], q["sctx"]
            NK = NR + NO
            sc = SC[s]; rs = sc["res"]
            b = hi % 2
            (KT, r_KT), (QT, r_QT), (V, r_V), (H, r_H) = bufs[b]
            S.dma("sp", "p3kt%d" % b, I_("dma_start", out=KT[:, 0:SCX], in_=sc["KT"][h, :, :]), reads=[rs["KT"]], writes=[r_KT])
            S.dma("sp", "p3qt%d" % b, I_("dma_start", out=QT[0:64, 0, 0:SR], in_=sc["QT"][h, 0:64, :]), reads=[rs["QT"]], writes=[r_QT])
            S.dma("sp", "p3qt%d" % b, I_("dma_start", out=QT[64:128, 1, 0:SR], in_=sc["QT"][h, 64:128, :]), reads=[rs["QT"]], writes=[r_QT])
            S.dma("sp", "p3v%d" % b, I_("dma_start", out=V[:, 0:NK, 0:128], in_=sc["DVH"][h, :, :, :]), reads=[rs["DVH"]], writes=[r_V])
            for di in range(6):
                delta = di - 1
                off = h * 1280 + 512 - 128 * delta
                S.dma("sp", "p3h%d" % b, I_("dma_start", out=H[:, di, :], in_=bass.AP(GREV.tensor, off, [[1, 128], [1, 512]])),
                      reads=[r_grev], writes=[r_H])

        work = []

        def make_iter(hi, g, c, kb, first_of_head):
            s, h = heads[hi]
            q = seqs[s]
            NR, NO = q["nreg"], q["noth"]
            NK = NR + NO
            sc = SC[s]; rs = sc["res"]
            (KT, r_KT), (QT, r_QT), (V, r_V), (H, r_H) = bufs[hi % 2]
            st_ = {}

            def stage_a():
                ps, psr = psS.next()
                st_["ps"] = (ps, psr)
                near = (kb < NR) and (4 * g - 1 <= kb <= 4 * g + 4)
                S.op("pe", I_("matmul", ps[:, :], lhsT=KT[:, kb * 128:(kb + 1) * 128], rhs=QT[:, c, g * 512:(g + 1) * 512],
                              start=True, stop=(not near)),
                     reads=[r_KT, r_QT], writes=[psr], inc=(not near), skip_self=True)
                if near:
                    di = kb - 4 * g + 1
                    S.op("pe", I_("matmul", ps[:, :], lhsT=jmat_b[:], rhs=H[:, di, :], start=False, stop=True),
                         reads=[r_jb, r_H], writes=[psr], skip_self=True)
                    st_["bias"] = (zero_c[:, 0:1], r_zero)
                elif kb >= NR:
                    st_["bias"] = (oth_fb[:, kb - NR, h:h + 1], r_ofb)
                elif kb < 4 * g - 1:
                    st_["bias"] = (farb[:, h:h + 1], r_farb)
                else:
                    st_["bias"] = (farb[:, 8 + h:9 + h], r_farb)

            def stage_b():
                if first_of_head and hi + 1 < len(heads):
                    emit_loads(hi + 1)
                ps, psr = st_["ps"]
                bias_ap, r_bias = st_["bias"]
                pt, r_pt = ptr.next()
                S.op("act", I_("activation", out=pt[:], in_=ps[:, :], func=AF.Exp, bias=bias_ap),
                     reads=[psr, r_bias], writes=[r_pt])
                if kb == 0:
                    rnd["acc"] = accr.next()
                    rnd["OT"] = psOT.next()
                    rnd["Z"] = psZ.next()
                acc, r_acc = rnd["acc"]
                OT, r_OT = rnd["OT"]
                Zp, r_Zp = rnd["Z"]
                if kb == 0:
                    S.op("dve", I_("tensor_copy", out=acc[:], in_=pt[:]), reads=[r_pt], writes=[r_acc])
                elif kb % 2 == 0:
                    S.op("dve", I_("tensor_tensor", out=acc[:], in0=acc[:], in1=pt[:], op=ALU.add), reads=[r_pt, r_acc], writes=[r_acc])
                S.op("pe", I_("matmul", OT[:, :], lhsT=V[:, kb, 0:128], rhs=pt[:, :], start=(kb == 0), stop=(kb == NK - 1)),
                     reads=[r_pt, r_V], writes=[r_OT], skip_self=True)
                if kb % 2 == 1:
                    S.op("pe", I_("matmul", Zp[:, :], lhsT=ones_b[:], rhs=pt[:, :], start=(kb == 1), stop=False),
                         reads=[r_pt, r_ones], writes=[r_Zp], skip_self=True)
                if kb != NK - 1:
                    return
                rnd["pending"] = (NK, epilogue_fn(acc, r_acc, OT, r_OT, Zp, r_Zp))

            def epilogue_fn(acc, r_acc, OT, r_OT, Zp, r_Zp):
              stt = {}

              def e1():
                S.op("pe", I_("matmul", Zp[:, :], lhsT=ones_f128[:], rhs=acc[:], start=False, stop=True), reads=[r_acc, r_onesf128], writes=[r_Zp], skip_self=True)
                OTe, r_OTe = OTer.next()
                OTh = OTe[:].bitcast(BF16)[:, 0:512]
                OTl = OTe[:].bitcast(BF16)[:, 512:1024]
                S.op("dve", I_("tensor_copy", out=OTh, in_=OT[:, :]), reads=[r_OT], writes=[r_OTe])
                S.op("dve", I_("tensor_tensor", out=OTl, in0=OT[:, :], in1=OTh, op=ALU.subtract), reads=[r_OT, r_OTe], writes=[r_OTe])
                OTn, r_OTn = OTnr.next()
                rc, r_rc = rcp.next()
                stt["rc"] = (rc, r_rc)
                S.op("dve", I_("tensor_tensor", out=OTn[:].rearrange("p (a b) -> p a b", a=4), in0=Zp[:, :].rearrange("p (a b) -> p a b", a=4),
                               in1=ident_f[:].unsqueeze(1).to_broadcast([128, 4, 128]), op=ALU.mult), reads=[r_Zp, r_identf], writes=[r_OTn])
                S.op("dve", I_("reduce_sum", out=rc[:, 0:4], in_=OTn[:].rearrange("p (a b) -> p a b", a=4), axis=AX.X), reads=[r_OTn], writes=[r_rc])
                S.op("dve", I_("reciprocal", out=rc[:, 0:4], in_=rc[:, 0:4]), reads=[r_rc], writes=[r_rc])
                if c == 1:
                    S.op("dve", I_("tensor_scalar", out=rc[:, 0:4], in0=rc[:, 0:4], scalar1=lamc[:, 1:2], scalar2=None, op0=ALU.mult), reads=[r_rc, r_lam], writes=[r_rc])
                po, por = psE.next()
                for qb in range(4):
                    S.op("pe", I_("matmul", po[:, qb * 128:(qb + 1) * 128], lhsT=OTh[:, qb * 128:(qb + 1) * 128], rhs=ident_b[:], start=True, stop=False), reads=[r_OTe, r_identb], writes=[por], inc=False, skip_self=True)
                    S.op("pe", I_("matmul", po[:, qb * 128:(qb + 1) * 128], lhsT=OTl[:, qb * 128:(qb + 1) * 128], rhs=ident_b[:], start=False, stop=True), reads=[r_OTe, r_identb], writes=[por], inc=(qb == 3), skip_self=True)
                stt["d"] = []
                for qb in range(4):
                    pq = po[:, qb * 128:(qb + 1) * 128]
                    oe, r_oe = Oe[qb]
                    if c == 0:
                        S.op("dve", I_("tensor_scalar", out=oe[:, 0:128], in0=pq, scalar1=rc[:, qb:qb + 1], scalar2=None, op0=ALU.mult), reads=[por, r_rc], writes=[r_oe])
                    else:
                        d, r_d = dr.next()
                        stt["d"].append((d, r_d))
                        S.op("dve", I_("scalar_tensor_tensor", out=d[:], in0=pq, scalar=rc[:, qb:qb + 1], in1=oe[:, 0:128], op0=ALU.mult, op1=ALU.add),
                             reads=[por, r_rc, r_oe], writes=[r_d])
                        S.op("dve", I_("tensor_tensor", out=junkf[:], in0=d[:], in1=d[:], op=ALU.mult), reads=[r_d], writes=[r_junkf])
                        S.op("dve", I_("reduce_sum", out=rc[:, 4 + qb:5 + qb], in_=junkf[:], axis=AX.X), reads=[r_junkf], writes=[r_rc])

              def e2():
                rc, r_rc = stt["rc"]
                S.op("act", I_("activation", out=rc[:, 8:12], in_=rc[:, 4:8], func=AF.Ln, bias=epsc[:, 3:4], scale=1.0 / 128), reads=[r_rc, r_epsc], writes=[r_rc])
                S.op("act", I_("activation", out=rc[:, 8:12], in_=rc[:, 8:12], func=AF.Exp, scale=-0.5), reads=[r_rc], writes=[r_rc])

              def e3():
                rc, r_rc = stt["rc"]
                for qb in range(4):
                    d, r_d = stt["d"][qb]
                    yo, r_yo = yor.next()
                    S.op("dve", I_("tensor_scalar", out=d[:], in0=d[:], scalar1=rc[:, 8 + qb:9 + qb], scalar2=None, op0=ALU.mult), reads=[r_d, r_rc], writes=[r_d])
                    S.op("pool", I_("tensor_tensor", out=yo[:], in0=d[:], in1=dng[:], op=ALU.mult), reads=[r_d, r_dng], writes=[r_yo])
                    tq = (g * 4 + qb) * 128
                    S.dma("sp", "p3yo%d" % yor.slot(), I_("dma_start", out=sc["Y"][tq:tq + 128, 1024 + h * 128:1024 + (h + 1) * 128], in_=yo[:]),
                          reads=[r_yo], writes=[rs["Y"]])
              if c == 0:
                  return [(6, e1)]
              return [(6, e1), (22, e2), (26, e3)]
            return stage_a, stage_b

        for hi, (s, h) in enumerate(heads):
            q = seqs[s]
            NK = q["nreg"] + q["noth"]
            first = True
            for g in range(q["nreg"] // 4):
                for c in range(2):
                    for kb in range(NK):
                        work.append(make_iter(hi, g, c, kb, first))
                        first = False
        emit_loads(0)
        for idx in range(len(work) + LA):
            if idx < len(work):
                work[idx][0]()
            if idx >= LA:
                rnd.pop("pending", None)
                work[idx - LA][1]()
                if "pending" in rnd:
                    nk_, stages = rnd.pop("pending")
                    for (off, fn_) in stages:
                        off_ = min(off, nk_ - 2) if off == 6 else min(off, nk_ - 1 + (1 if off == 26 else 0))
                        deferred.append((idx + off_, len(deferred), fn_))
                    deferred.sort(key=lambda t: (t[0], t[1]))
                while deferred and deferred[0][0] <= idx:
                    deferred.pop(0)[2]()
        deferred.sort(key=lambda t: (t[0], t[1]))
        while deferred:
            deferred.pop(0)[2]()
        S.barrier()
        st.close()

    def phase4():
        st = contextlib.ExitStack()
        Wo = sb(st, "p4_Wo", [128, 16, D], BF16); r_Wo = Res("Wo")
        Wc = sb(st, "p4_Wc", [128, 8, 6144], BF16); r_Wc = Res("Wc")
        for j in range(16):
            S.dma("pool", "p4w", I_("dma_start", out=Wo[:, j, :], in_=I["w_out_mix"][j * 128:(j + 1) * 128, :]), writes=[r_Wo])
        for j in range(8):
            S.dma("pool", "p4w", I_("dma_start", out=Wc[:, j, :], in_=I["w_in_conv"][j * 128:(j + 1) * 128, :]), writes=[r_Wc])
        yr = Ring([(sb(st, "p4_y%d" % i, [128, 2048], BF16), Res("y")) for i in range(1)])
        gr = Ring([(sb(st, "p4_g%d" % i, [128, 2048], BF16), Res("g")) for i in range(1)])
        ygr = Ring([(sb(st, "p4_yg%d" % i, [128, 2048], BF16), Res("yg")) for i in range(1)])
        ygTr = Ring([(sb(st, "p4_ygT%d" % i, [128, 16, 128], BF16), Res("ygT")) for i in range(2)])
        xr = Ring([(sb(st, "p4_x%d" % i, [128, D], F32), Res("x")) for i in range(2)])
        x1r = Ring([(sb(st, "p4_x1%d" % i, [128, D], F32), Res("x1")) for i in range(1)])
        ssr = Ring([(sb(st, "p4_ss%d" % i, [128, 2], F32), Res("ss")) for i in range(2)])
        junk = sb(st, "p4_junk", [128, D], BF16); r_junk = Res("junk")
        hr = Ring([(sb(st, "p4_h%d" % i, [128, D], BF16), Res("h")) for i in range(2)])
        hTr = Ring([(sb(st, "p4_hT%d" % i, [128, 8, 512], BF16), Res("hT")) for i in range(1)])
        sgr = Ring([(sb(st, "p4_sg%d" % i, [128, 512], F32), Res("sg")) for i in range(2)])
        fmr = Ring([(sb(st, "p4_fm%d" % i, [128, 512], BF16), Res("fm")) for i in range(4)])
        psT = Ring(PS[0:2]); pso = Ring(PS[2:4]); psfm = Ring(PS[4:8])

        for s in ("a", "b"):
            q = seqs[s]
            sc = SC[s]; rs = sc["res"]
            for g in range(q["nreg"] // 4):
                t0 = g * 512
                hT, r_hT = hTr.next()
                for i in range(4):
                    tt = t0 + i * 128
                    y, r_y = yr.next(); gt, r_gt = gr.next(); x, r_x = xr.next()
                    S.dma("sp", "p4y%d" % yr.slot(), I_("dma_start", out=y[:], in_=sc["Y"][tt:tt + 128, :]), reads=[rs["Y"]], writes=[r_y])
                    S.dma("sp", "p4g%d" % gr.slot(), I_("dma_start", out=gt[:], in_=sc["GT"][tt:tt + 128, :]), reads=[rs["GT"]], writes=[r_gt])
                    S.dma("sp", "p4x%d" % xr.slot(), I_("dma_start", out=x[:], in_=X[s][tt:tt + 128, :]), writes=[r_x])
                    yg, r_yg = ygr.next()
                    S.op("pool", I_("tensor_tensor", out=yg[:], in0=y[:], in1=gt[:], op=ALU.mult), reads=[r_y, r_gt], writes=[r_yg])
                    ygT, r_ygT = ygTr.next()
                    for half in range(2):
                        pt, pr = psT.next()
                        ptb = pt[:].bitcast(BF16)
                        for j in range(8):
                            jj = half * 8 + j
                            S.op("pe", I_("transpose", ptb[:, j * 128:(j + 1) * 128], yg[:, jj * 128:(jj + 1) * 128], ident_b[:]),
                                 reads=[r_yg, r_identb], writes=[pr], inc=(j == 7), skip_self=True)
                        S.op("act", I_("activation", out=ygT[:, half * 8:(half + 1) * 8, :], in_=ptb.rearrange("p (j t) -> p j t", j=8), func=AF.Copy),
                             reads=[pr], writes=[r_ygT])
                    x1, r_x1 = x1r.next()
                    for cg in range(2):
                        po, por = pso.next()
                        for j in range(16):
                            S.op("pe", I_("matmul", po[:, :], lhsT=ygT[:, j, :], rhs=Wo[:, j, cg * 512:(cg + 1) * 512], start=(j == 0), stop=(j == 15)),
                                 reads=[r_ygT, r_Wo], writes=[por], inc=(j == 15), skip_self=True)
                        S.op("dve", I_("tensor_tensor", out=x1[:, cg * 512:(cg + 1) * 512], in0=po[:, :], in1=x[:, cg * 512:(cg + 1) * 512], op=ALU.add),
                             reads=[por, r_x], writes=[r_x1])
                    S.dma("sp", "p4x1%d" % x1r.slot(), I_("dma_start", out=sc["X1"][tt:tt + 128, :], in_=x1[:]), reads=[r_x1], writes=[rs["X1"]])
                    ss, r_ss = ssr.next()
                    S.op("act", I_("activation", out=junk[:], in_=x1[:], func=AF.Square, accum_out=ss[:, 0:1]), reads=[r_x1], writes=[r_junk, r_ss])
                    S.op("act", I_("activation", out=ss[:, 1:2], in_=ss[:, 0:1], func=AF.Sqrt, bias=epsc[:, 0:1]), reads=[r_ss, r_epsc], writes=[r_ss])
                    S.op("dve", I_("reciprocal", out=ss[:, 1:2], in_=ss[:, 1:2]), reads=[r_ss], writes=[r_ss])
                    h, r_h = hr.next()
                    S.op("act", I_("activation", out=x[:], in_=x1[:], func=AF.Copy, scale=ss[:, 1:2]), reads=[r_x1, r_ss, r_x], writes=[r_x])
                    S.op("pool", I_("tensor_tensor", out=h[:], in0=x[:], in1=g2[:], op=ALU.mult), reads=[r_x, r_g2], writes=[r_h])
                    pt, pr = psT.next()
                    ptb = pt[:].bitcast(BF16)
                    for j in range(8):
                        S.op("pe", I_("transpose", ptb[:, j * 128:(j + 1) * 128], h[:, j * 128:(j + 1) * 128], ident_b[:]),
                             reads=[r_h, r_identb], writes=[pr], inc=(j == 7), skip_self=True)
                    S.op("act", I_("activation", out=hT[:, :, i * 128:(i + 1) * 128], in_=ptb.rearrange("p (j t) -> p j t", j=8), func=AF.Copy),
                         reads=[pr], writes=[r_hT])
                def fm_mm(c0):
                    pm, pmr = psfm.next()
                    for j in range(8):
                        S.op("pe", I_("matmul", pm[:, :], lhsT=Wc[:, j, c0:c0 + 128], rhs=hT[:, j, :], start=(j == 0), stop=(j == 7)),
                             reads=[r_hT, r_Wc], writes=[pmr], inc=(j == 7), skip_self=True)
                    return pm, pmr
                for cb in range(16):
                    pa, par = fm_mm(cb * 128)
                    pb, pbr = fm_mm(2048 + cb * 128)
                    sg, r_sg = sgr.next()
                    S.op("act", I_("activation", out=sg[:], in_=pb[:, :], func=AF.Sigmoid), reads=[pbr], writes=[r_sg])
                    fm, r_fm = fmr.next(); slot = fmr.slot()
                    S.op("dve", I_("tensor_tensor", out=fm[:], in0=pa[:, :], in1=sg[:], op=ALU.mult), reads=[par, r_sg], writes=[r_fm])
                    S.dma("sp", "p4fm%d" % slot, I_("dma_start", out=sc["GLU"][cb * 128:(cb + 1) * 128, t0:t0 + 512], in_=fm[:]), reads=[r_fm], writes=[rs["GLU"]])
                    pg, pgr = fm_mm(4096 + cb * 128)
                    fm, r_fm = fmr.next(); slot = fmr.slot()
                    S.op("act", I_("activation", out=fm[:], in_=pg[:, :], func=AF.Silu), reads=[pgr], writes=[r_fm])
                    S.dma("sp", "p4fm%d" % slot, I_("dma_start", out=sc["SG"][cb * 128:(cb + 1) * 128, t0:t0 + 512], in_=fm[:]), reads=[r_fm], writes=[rs["SG"]])
        S.barrier()
        st.close()

    def phase5():
        st = contextlib.ExitStack()
        Wo = sb(st, "p5_Wo", [128, 16, D], BF16); r_Wo = Res("Wo5")
        for j in range(16):
            S.dma("pool", "p5w", I_("dma_start", out=Wo[:, j, :], in_=I["w_out_conv"][j * 128:(j + 1) * 128, :]), writes=[r_Wo])
        winr = Ring([(sb(st, "p5_win%d" % i, [128, 16, 544], BF16), Res("win")) for i in range(2)])
        sgr = Ring([(sb(st, "p5_sg%d" % i, [128, 16, 512], BF16), Res("sg")) for i in range(1)])
        dgr = Ring([(sb(st, "p5_dg%d" % i, [128, 16, 128], BF16), Res("dg")) for i in range(2)])
        dgor = Ring([(sb(st, "p5_dgo%d" % i, [128, 15, 128], BF16), Res("dgo")) for i in range(2)])
        ybuf = sb(st, "p5_y", [128, 16, 512], F32); r_yb = [Res("yb%d" % i) for i in range(16)]
        ybfr = Ring([(sb(st, "p5_ybf%d" % i, [128, 512], BF16), Res("ybf")) for i in range(2)])
        ysqr = Ring([(sb(st, "p5_ysq%d" % i, [128, 512], BF16), Res("ysq")) for i in range(2)])
        mean = sb(st, "p5_mean", [128, 512], F32); r_mean = Res("mean")
        msq = sb(st, "p5_msq", [128, 512], F32); r_msq = Res("msq")
        rstd = sb(st, "p5_rstd", [128, 512], F32); r_rstd = Res("rstd")
        tnr = Ring([(sb(st, "p5_tn%d" % i, [128, 512], F32), Res("tn")) for i in range(2)])
        znr = Ring([(sb(st, "p5_zn%d" % i, [128, 512], F32), Res("zn")) for i in range(2)])
        zT = sb(st, "p5_zT", [128, 16, 512], BF16); r_zT = [Res("zT%d" % i) for i in range(16)]
        x1r = Ring([(sb(st, "p5_x1%d" % i, [128, D], F32), Res("x1")) for i in range(2)])
        x2r = Ring([(sb(st, "p5_x2%d" % i, [128, D], F32), Res("x2")) for i in range(2)])
        ssr = Ring([(sb(st, "p5_ss%d" % i, [128, 2], F32), Res("ss")) for i in range(2)])
        junk = sb(st, "p5_junk", [128, D], BF16); r_junk = Res("junk")
        psy = Ring(PS[0:2]); ps_st, r_pst = PS[2]; ps_sq, r_psq = PS[3]; pso = Ring(PS[4:8])

        for s in ("a", "b"):
            q = seqs[s]
            sc = SC[s]; rs = sc["res"]
            SR = q["sreg"]
            NG = q["nreg"] // 4
            for g in range(NG):
                t0 = g * 512
                win, r_win = winr.next(); sg, r_sg = sgr.next()
                lo = max(t0 - 15, 0); hi = min(t0 + 512 + 15, SR)
                c_lo = lo - (t0 - 15)
                if g == 0 or g == NG - 1:
                    S.op("pool", I_("memset", win[:], 0.0), writes=[r_win])
                for half in range(2):
                    S.dma("sp", "p5win%d" % winr.slot(), I_("dma_start", out=win[:, half * 8:(half + 1) * 8, c_lo:c_lo + (hi - lo)],
                                                                                                                 in_=sc["GLU"][half * 1024:(half + 1) * 1024, lo:hi].rearrange("(cb p) t -> p cb t", p=128)),
                          reads=[rs["GLU"]], writes=[r_win])
                    S.dma("sp", "p5sg%d" % sgr.slot(), I_("dma_start", out=sg[:, half * 8:(half + 1) * 8, :], in_=sc["SG"][half * 1024:(half + 1) * 1024, t0:t0 + 512].rearrange("(cb p) t -> p cb t", p=128)),
                          reads=[rs["SG"]], writes=[r_sg])
                pend_stats = []
                for cb in range(16):
                    dg, r_dg = dgr.next()
                    dgo, r_dgo = dgor.next()
                    for k in range(0, 31, 2):
                        S.op("dve", I_("tensor_scalar", out=dg[:, k // 2, :], in0=ident_f[:], scalar1=cwT[:, cb, k:k + 1], scalar2=None, op0=ALU.mult),
                             reads=[r_identf, r_cwT], writes=[r_dg], inc=(k == 30), skip_self=True)
                    for k in range(1, 31, 2):
                        S.op("act", I_("activation", out=dgo[:, k // 2, :], in_=ident_f[:], func=AF.Copy, scale=cwT[:, cb, k:k + 1]),
                             reads=[r_identf, r_cwT], writes=[r_dgo], inc=(k == 29), skip_self=True)
                    py, pyr = psy.next()
                    for k in range(31):
                        S.op("pe", I_("matmul", py[:, :], lhsT=(dg if k % 2 == 0 else dgo)[:, k // 2, :], rhs=win[:, cb, k:k + 512], start=(k == 0), stop=(k == 30)),
                             reads=[r_dg, r_dgo, r_win], writes=[pyr], inc=(k == 30), skip_self=True)
                    while pend_stats:
                        pend_stats.pop(0)()
                    S.op("act", I_("activation", out=ybuf[:, cb, :], in_=py[:, :], func=AF.Identity, bias=cvb[:, cb:cb + 1]), reads=[pyr, r_cvb], writes=[r_yb[cb]])
                    ysq, r_ysq = ysqr.next(); ybf, r_ybf = ybfr.next()
                    S.op("act", I_("activation", out=ysq[:], in_=py[:, :], func=AF.Square, bias=cvb[:, cb:cb + 1]), reads=[pyr, r_cvb], writes=[r_ysq])
                    S.op("dve", I_("tensor_copy", out=ybf[:], in_=ybuf[:, cb, :]), reads=[r_yb[cb]], writes=[r_ybf])
                    def _stats(ybf=ybf, r_ybf=r_ybf, ysq=ysq, r_ysq=r_ysq, cb=cb):
                        S.op("pe", I_("matmul", ps_st[:, :], lhsT=ones_b[:], rhs=ybf[:], start=(cb == 0), stop=(cb == 15)), reads=[r_ybf, r_ones], writes=[r_pst], skip_self=True)
                        S.op("pe", I_("matmul", ps_sq[:, :], lhsT=ones_b[:], rhs=ysq[:], start=(cb == 0), stop=(cb == 15)), reads=[r_ysq, r_ones], writes=[r_psq], skip_self=True)
                    pend_stats.append(_stats)
                while pend_stats:
                    pend_stats.pop(0)()
                S.op("dve", I_("tensor_scalar", out=mean[:], in0=ps_st[:, :], scalar1=1.0 / 2048, scalar2=None, op0=ALU.mult), reads=[r_pst], writes=[r_mean])
                S.op("dve", I_("tensor_tensor", out=msq[:], in0=mean[:], in1=mean[:], op=ALU.mult), reads=[r_mean], writes=[r_msq])
                S.op("dve", I_("scalar_tensor_tensor", out=rstd[:], in0=ps_sq[:, :], scalar=1.0 / 2048, in1=msq[:], op0=ALU.mult, op1=ALU.subtract), reads=[r_psq, r_msq], writes=[r_rstd])
                S.op("act", I_("activation", out=rstd[:], in_=rstd[:], func=AF.Sqrt, bias=epsc[:, 3:4]), reads=[r_rstd, r_epsc], writes=[r_rstd])
                S.op("dve", I_("reciprocal", out=rstd[:], in_=rstd[:]), reads=[r_rstd], writes=[r_rstd])
                for cb in range(16):
                    tn, r_tn = tnr.next(); zn, r_zn = znr.next()
                    S.op("dve", I_("tensor_tensor", out=tn[:], in0=ybuf[:, cb, :], in1=mean[:], op=ALU.subtract), reads=[r_yb[cb], r_mean], writes=[r_tn])
                    S.op("pool", I_("tensor_tensor", out=tn[:], in0=tn[:], in1=rstd[:], op=ALU.mult), reads=[r_tn, r_rstd], writes=[r_tn])
                    S.op("act", I_("activation", out=zn[:], in_=tn[:], func=AF.Silu, bias=cnb[:, cb:cb + 1], scale=cng[:, cb:cb + 1]),
                         reads=[r_tn, r_cng, r_cnb], writes=[r_zn])
                    S.op("dve", I_("tensor_tensor", out=zT[:, cb, :], in0=zn[:], in1=sg[:, cb, :], op=ALU.mult), reads=[r_zn, r_sg], writes=[r_zT[cb]])
                for i in range(4):
                    tt = t0 + i * 128
                    x1, r_x1 = x1r.next()
                    S.dma("sp", "p5x1%d" % x1r.slot(), I_("dma_start", out=x1[:], in_=sc["X1"][tt:tt + 128, :]), reads=[rs["X1"]], writes=[r_x1])
                    x2, r_x2 = x2r.next()
                    for cg in range(2):
                        po, por = pso.next()
                        for cb in range(16):
                            S.op("pe", I_("matmul", po[:, :], lhsT=zT[:, cb, i * 128:(i + 1) * 128], rhs=Wo[:, cb, cg * 512:(cg + 1) * 512], start=(cb == 0), stop=(cb == 15)),
                                 reads=[r_zT[cb], r_Wo], writes=[por], inc=(cb == 15), skip_self=True)
                        S.op("dve", I_("tensor_tensor", out=x2[:, cg * 512:(cg + 1) * 512], in0=po[:, :], in1=x1[:, cg * 512:(cg + 1) * 512], op=ALU.add),
                             reads=[por, r_x1], writes=[r_x2])
                    ss, r_ss = ssr.next()
                    S.op("act", I_("activation", out=junk[:], in_=x2[:], func=AF.Square, accum_out=ss[:, 0:1]), reads=[r_x2], writes=[r_junk, r_ss])
                    S.op("act", I_("activation", out=ss[:, 1:2], in_=ss[:, 0:1], func=AF.Sqrt, bias=epsc[:, 0:1]), reads=[r_ss, r_epsc], writes=[r_ss])
                    S.op("dve", I_("reciprocal", out=ss[:, 1:2], in_=ss[:, 1:2]), reads=[r_ss], writes=[r_ss])
                    S.op("dve", I_("scalar_tensor_tensor", out=x2[:], in0=x2[:], scalar=ss[:, 1:2], in1=gfin[:], op0=ALU.mult, op1=ALU.mult),
                         reads=[r_x2, r_ss, r_gf], writes=[r_x2])
                    S.dma("sp", "p5o%d" % x2r.slot(), I_("dma_start", out=O[s][tt:tt + 128, :], in_=x2[:]), reads=[r_x2], final=True)
        st.close()

    import os as _os
    upto = int(_os.environ.get("K_UPTO", "5"))
    setup()
    if upto >= 1:
        phase1()
    if upto >= 2:
        phase2()
    if upto >= 3:
        phase3()
    if upto >= 4:
        phase4()
    if upto >= 5:
        phase5()
    S.finish()
    S.replay()
    return nc, S


_CACHE = {}


def _prep_common(inp):
    f = lambda a: np.ascontiguousarray(np.asarray(a, dtype=np.float32))
    c = dict(
        norm_g=f(inp["norm_g"]), final_g=f(inp["final_g"]).reshape(1, D), rel_bias=f(inp["rel_bias"]),
        w_in_mix=f(inp["w_in_mix"][0]), ret_decay=f(inp["ret_decay"][0]).reshape(1, 8),
        lam4=np.concatenate([f(inp["lam_q1"][0]), f(inp["lam_k1"][0]), f(inp["lam_q2"][0]), f(inp["lam_k2"][0])]).reshape(1, 256),
        diff_norm_g=f(inp["diff_norm_g"][0]).reshape(1, 128), w_out_mix=f(inp["w_out_mix"][0]),
        w_in_conv=f(inp["w_in_conv"][0]), conv_w=f(inp["conv_w"][0]), conv_b=f(inp["conv_b"][0]).reshape(1, 2048),
        conv_norm_g=f(inp["conv_norm_g"][0]).reshape(1, 2048), conv_norm_b=f(inp["conv_norm_b"][0]).reshape(1, 2048),
        w_out_conv=f(inp["w_out_conv"][0]))
    c.update(_consts())
    return c


def run_config(inp, NBA, NRB, NOB, assign):
    key = (NBA, NRB, NOB)
    if key not in _CACHE:
        _CACHE[key] = build_program(NBA, NRB, NOB)
    nc, _ = _CACHE[key]
    common = _prep_common(inp)
    xs = np.asarray(inp["x_sample"], dtype=np.float32)
    xp = np.asarray(inp["x_prompt"], dtype=np.float32)
    NB = NRB + NOB
    in_maps = []
    for (si, pi, rs) in assign:
        reg = list(range(rs, rs + NRB))
        oth = [b for b in range(NB) if b < rs or b >= rs + NRB]
        order = reg + oth
        tok = (np.array(order)[:, None] * 128 + np.arange(128)[None, :]).reshape(-1)
        m = dict(common)
        m["xa"] = np.ascontiguousarray(xs[si])
        m["xb"] = np.ascontiguousarray(xp[pi][tok])
        m["csa"] = _rope_table(np.arange(NBA * 128))
        m["csb"] = _rope_table(tok)
        fl = np.array([1.0 if b < rs else 0.0 for b in oth], dtype=np.float32).reshape(1, -1)
        if fl.shape[1] == 0:
            fl = np.zeros((1, 1), np.float32)
        m["fl"] = fl
        in_maps.append(m)
    res = run_bass_kernel_spmd(nc, in_maps, core_ids=list(range(len(assign))))
    return res.results


def kernel(**inputs):
    NBA, NRB, NOB = 32, 20, 44
    starts = [0, 14, 30, 44]
    assign = [(c, c // 4, starts[c % 4]) for c in range(8)]
    res = run_config(inputs, NBA, NRB, NOB, assign)
    y_prompt = np.zeros((2, 8192, D), np.float32)
    y_sample = np.zeros((8, 4096, D), np.float32)
    for c in range(8):
        y_sample[c] = res[c]["out_a"]
        qd = c % 4
        lo = (16 * qd - starts[qd]) * 128
        y_prompt[c // 4, qd * 2048:(qd + 1) * 2048] = res[c]["out_b"][lo:lo + 2048]
    return (y_prompt, y_sample)
```

```python
import contextlib
import math
import numpy as np
import concourse.bass as bass
import concourse.mybir as mybir
from concourse.bass_utils import run_bass_kernel_spmd

F32 = mybir.dt.float32
BF16 = mybir.dt.bfloat16
AF = mybir.ActivationFunctionType
ALU = mybir.AluOpType
AX = mybir.AxisListType

D = 1024
EPS = 1e-6


class Res:
    __slots__ = ("name", "w", "r", "multi")

    def __init__(self, name, multi=False):
        self.name = name
        self.w = []
        self.r = []
        self.multi = multi


class Sched:
    COMPUTE = ("pe", "act", "dve", "pool")
    ALL = ("pe", "act", "dve", "pool", "sp")

    def __init__(self, nc):
        self.nc = nc
        self.streams = {e: [] for e in self.ALL}
        self.tick = {e: 0 for e in self.COMPUTE}
        self.esem = {}
        self.old_esems = []
        self.nsem = 0
        self.waited = {e: {} for e in self.ALL}
        for e in self.COMPUTE:
            self._new_esem(e)
        self.chan = {}
        self.final_tokens = []
        self.ninst = 0

    def _alloc(self, name):
        self.nsem += 1
        return self.nc.alloc_semaphore(name=name)

    def _new_esem(self, e):
        if e in self.esem and self.tick[e] > 0:
            self.old_esems.append((self.esem[e], self.tick[e]))
        self.esem[e] = self._alloc("e_%s_%d" % (e, self.nsem))
        self.tick[e] = 0

    def _deps(self, reads, writes):
        toks = []
        for r in reads:
            toks.extend(r.w)
        for w in writes:
            toks.extend(w.w)
            toks.extend(w.r)
        return toks

    def _emit_waits(self, eng, toks, skip_self=False):
        wd = self.waited[eng]
        best = {}
        for (s, v) in toks:
            k = id(s)
            if skip_self and eng in self.esem and s is self.esem[eng]:
                continue
            if wd.get(k, 0) >= v:
                continue
            if best.get(k, (None, 0))[1] < v:
                best[k] = (s, v)
        for k, (s, v) in best.items():
            wd[k] = v
            self.streams[eng].append(("wait", s, v))
            self.ninst += 1

    def _commit(self, tok, reads, writes):
        for r in reads:
            r.r.append(tok)
            if len(r.r) > 64:
                r.r = self._compress(r.r)
        for w in writes:
            if w.multi:
                w.w.append(tok)
                if len(w.w) > 64:
                    w.w = self._compress(w.w)
            else:
                w.w = [tok]
                w.r = []

    @staticmethod
    def _compress(toks):
        best = {}
        for (s, v) in toks:
            if best.get(id(s), (None, 0))[1] < v:
                best[id(s)] = (s, v)
        return list(best.values())

    def op(self, eng, fn, reads=(), writes=(), inc=True, skip_self=False):
        toks = self._deps(reads, writes)
        self._emit_waits(eng, toks, skip_self=skip_self)
        if inc:
            self.tick[eng] += 1
            tok = (self.esem[eng], self.tick[eng])
            self.streams[eng].append(("op", fn, self.esem[eng]))
        else:
            tok = (self.esem[eng], self.tick[eng] + 1)
            self.streams[eng].append(("op", fn, None))
        self.ninst += 1
        self._commit(tok, reads, writes)
        return tok

    def dma(self, q, chan, fn, reads=(), writes=(), final=False):
        toks = self._deps(reads, writes)
        if chan == "su":
            toks = [t for t in toks if t[1] != INF]
        self._emit_waits(q, toks)
        if chan not in self.chan:
            self.chan[chan] = [self._alloc("c_" + chan), 0]
        c = self.chan[chan]
        c[1] += 16
        tok = (c[0], INF if chan == "su" else c[1])
        self.streams[q].append(("dma", fn, c[0]))
        self.ninst += 1
        self._commit(tok, reads, writes)
        if final:
            self.final_tokens.append(tok)
        return tok

    def all_tokens(self):
        toks = [(self.esem[e], self.tick[e]) for e in self.COMPUTE if self.tick[e] > 0]
        toks += list(self.old_esems)
        toks += [(c[0], c[1]) for k, c in self.chan.items() if c[1] > 0 and k != "su"]
        return toks

    def barrier(self):
        toks = self.all_tokens()
        for e in self.ALL:
            self._emit_waits(e, toks)
        for e in self.COMPUTE:
            if self.tick[e] > 0:
                self._new_esem(e)

    def finish(self):
        self._emit_waits("sp", self.final_tokens + self.all_tokens())

    def replay(self):
        nc = self.nc
        streams = self.streams
        with nc.Block() as block:
            def run(name, eng):
                for it in streams[name]:
                    if it[0] == "wait":
                        v = it[2]
                        if v == INF:
                            v = self.chan["su"][1]
                        eng.wait_ge(it[1], v)
                    elif it[0] == "op":
                        nm, a, k = it[1]
                        ins = getattr(eng, nm)(*a, **k)
                        if it[2] is not None:
                            ins.then_inc(it[2], 1)
                    else:
                        nm, a, k = it[1]
                        getattr(eng, nm)(*a, **k).then_inc(it[2], 16)

            @block.tensor
            def _(e):
                run("pe", e)

            @block.scalar
            def _(e):
                run("act", e)

            @block.vector
            def _(e):
                run("dve", e)

            @block.gpsimd
            def _(e):
                run("pool", e)

            @block.sync
            def _(e):
                run("sp", e)


def I_(name, *a, **k):
    return (name, a, k)


INF = float("inf")


class Ring:
    def __init__(self, items):
        self.items = items
        self.i = 0

    def slot(self):
        return (self.i - 1) % len(self.items)

    def next(self):
        x = self.items[self.i % len(self.items)]
        self.i += 1
        return x


def _t5_bucket_np(rel):
    nb = 16
    max_exact = 8
    ret = (rel > 0).astype(np.int32) * nb
    n = np.abs(rel)
    nf = np.maximum(n, 1).astype(np.float32)
    large = max_exact + (np.log(nf / np.float32(max_exact)) / np.float32(math.log(128 / max_exact))
                         * np.float32(nb - max_exact)).astype(np.int32)
    large = np.minimum(large, nb - 1)
    return ret + np.where(n < max_exact, n, large)


def _consts():
    j = np.arange(1280)
    rel = 639 - j
    b = _t5_bucket_np(rel.astype(np.int32))
    et = np.zeros((32, 1280), np.float32)
    et[b, j] = 1.0
    et[:, 1279] = 0.0
    k = np.arange(128, dtype=np.float32)
    dpos = np.maximum(k[None, :] - k[:, None], 0.0).astype(np.float32)
    dneg = np.maximum(k[:, None] - k[None, :], 0.0).astype(np.float32)
    col = np.stack([127.0 - k, k], axis=1).astype(np.float32)
    rowf = np.broadcast_to((k + 1.0)[None, :], (128, 128)).astype(np.float32).copy()
    rowb = np.broadcast_to((128.0 - k)[None, :], (128, 128)).astype(np.float32).copy()
    return dict(c_et=et, c_dpos=dpos, c_dneg=dneg, c_col=col, c_rowf=rowf, c_rowb=rowb)


def _rope_table(pos):
    inv = (1.0 / (np.float32(10000.0) ** (np.arange(0, 128, 2, dtype=np.float32) / np.float32(128)))).astype(np.float32)
    ang = pos.astype(np.float32)[:, None] * inv[None, :]
    return np.concatenate([np.cos(ang), np.sin(ang)], axis=1).astype(np.float32)


def build_program(NBA, NRB, NOB, debug=False):
    nc = bass.Bass("TRN2", target_bir_lowering=False)
    S = Sched(nc)
    SA = NBA * 128
    SRB = NRB * 128
    SBT = (NRB + NOB) * 128
    seqs = {"a": dict(nreg=NBA, noth=0, sreg=SA, sctx=SA), "b": dict(nreg=NRB, noth=NOB, sreg=SRB, sctx=SBT)}

    def din(name, shape, dt=F32):
        return nc.dram_tensor(name, list(shape), dt, kind="ExternalInput").ap()

    def dscr(name, shape, dt=BF16):
        return nc.dram_tensor(name, list(shape), dt, kind="Internal").ap()

    I = {}
    I["xa"] = din("xa", [SA, D]); I["xb"] = din("xb", [SBT, D])
    I["csa"] = din("csa", [SA, 128]); I["csb"] = din("csb", [SBT, 128])
    I["fl"] = din("fl", [1, max(NOB, 1)])
    I["norm_g"] = din("norm_g", [2, D]); I["final_g"] = din("final_g", [1, D])
    I["rel_bias"] = din("rel_bias", [32, 8])
    I["w_in_mix"] = din("w_in_mix", [D, 7168]); I["ret_decay"] = din("ret_decay", [1, 8])
    I["lam4"] = din("lam4", [1, 256]); I["diff_norm_g"] = din("diff_norm_g", [1, 128])
    I["w_out_mix"] = din("w_out_mix", [2048, D]); I["w_in_conv"] = din("w_in_conv", [D, 6144])
    I["conv_w"] = din("conv_w", [31, 2048]); I["conv_b"] = din("conv_b", [1, 2048])
    I["conv_norm_g"] = din("conv_norm_g", [1, 2048]); I["conv_norm_b"] = din("conv_norm_b", [1, 2048])
    I["w_out_conv"] = din("w_out_conv", [2048, D])
    I["c_et"] = din("c_et", [32, 1280]); I["c_dpos"] = din("c_dpos", [128, 128]); I["c_dneg"] = din("c_dneg", [128, 128])
    I["c_col"] = din("c_col", [128, 2]); I["c_rowf"] = din("c_rowf", [128, 128]); I["c_rowb"] = din("c_rowb", [128, 128])
    O = {"a": nc.dram_tensor("out_a", [SA, D], F32, kind="ExternalOutput").ap(),
         "b": nc.dram_tensor("out_b", [SRB, D], F32, kind="ExternalOutput").ap()}
    X = {"a": I["xa"], "b": I["xb"]}
    CS = {"a": I["csa"], "b": I["csb"]}

    SC = {}
    for s, q in seqs.items():
        SC[s] = dict(
            KT=dscr("KT" + s, [8, 128, q["sctx"]]), QT=dscr("QT" + s, [8, 128, q["sreg"]]),
            DVH=dscr("DVH" + s, [8, 128, q["sctx"] // 128, 128]), RV=dscr("RV" + s, [q["sctx"], 1024]),
            RK=dscr("RK" + s, [q["sctx"], 512]), RKT=dscr("RKT" + s, [4, 128, q["sreg"]]),
            RQT=dscr("RQT" + s, [4, 128, q["sreg"]]), GT=dscr("GT" + s, [q["sreg"], 2048]),
            Y=dscr("Y" + s, [q["sreg"], 2048]), X1=dscr("X1" + s, [q["sreg"], D], F32),
            GLU=dscr("GLU" + s, [2048, q["sreg"]]), SG=dscr("SG" + s, [2048, q["sreg"]]))
        SC[s]["res"] = {k: Res(k + s, multi=True) for k in list(SC[s].keys())}
    GREV = dscr("GREV", [8, 1280]); r_grev = Res("grev", multi=True)

    DBG = {}

    def dbg_out(name, shape, dt=F32):
        DBG[name] = nc.dram_tensor("dbg_" + name, list(shape), dt, kind="ExternalOutput").ap()
        return DBG[name]

    def bc_ap(src, off, n, npart=128):
        return bass.AP(src.tensor, off, [[0, npart], [1, n]])

    persist = contextlib.ExitStack()

    def sb(stack, name, shape, dt):
        return stack.enter_context(nc.sbuf_tensor(name, list(shape), dt))

    PS = []
    for i in range(8):
        t = nc.alloc_psum_tensor("psb%d" % i, [128, 512], F32)
        PS.append((t, Res("psb%d" % i)))

    qsel = {"i": 0}

    def ldq():
        return "sp"

    P = {}

    def ptile(name, shape, dt):
        t = sb(persist, name, shape, dt)
        P[name] = (t, Res(name))
        return P[name]

    ident_f, r_identf = ptile("ident_f", [128, 128], F32)
    ident_b, r_identb = ptile("ident_b", [128, 128], BF16)
    jmat_f, r_jf = ptile("jmat_f", [128, 128], F32)
    jmat_b, r_jb = ptile("jmat_b", [128, 128], BF16)
    ones_b, r_ones = ptile("ones_b", [128, 128], BF16)
    zero_c, r_zero = ptile("zero_c", [128, 1], F32)
    epsc, r_epsc = ptile("epsc", [128, 4], F32)
    g1, r_g1 = ptile("g1", [128, D], F32)
    g2, r_g2 = ptile("g2", [128, D], F32)
    gfin, r_gf = ptile("gfin", [128, D], F32)
    lamc, r_lam = ptile("lamc", [128, 4], F32)
    lf, r_lf = ptile("lf", [128, 8], F32)
    dec, r_dec = ptile("dec", [128, 8], F32)
    dmatT, r_dmat = ptile("dmatT", [128, 4, 128], F32)
    wfb, r_wfb = ptile("wfb", [128, 8], F32)
    QF, r_QF = ptile("QF", [128, 4, 128], F32)
    QB, r_QB = ptile("QB", [128, 4, 128], F32)
    farb, r_farb = ptile("farb", [128, 16], F32)
    dng, r_dng = ptile("dng", [128, 128], F32)
    NO1 = max(NOB, 1)
    flb, r_flb = ptile("flb", [128, NO1], F32)
    fla, r_fla = ptile("fla", [128, NO1], F32)
    oth_af, r_oaf = ptile("oth_af", [128, NO1, 4], F32)
    oth_ab, r_oab = ptile("oth_ab", [128, NO1, 4], F32)
    oth_fb, r_ofb = ptile("oth_fb", [128, NO1, 8], F32)
    cwT, r_cwT = ptile("cwT", [128, 16, 31], F32)
    cvb, r_cvb = ptile("cvb", [128, 16], F32)
    cng, r_cng = ptile("cng", [128, 16], F32)
    cnb, r_cnb = ptile("cnb", [128, 16], F32)

    import os as _os2
    _CUT = int(_os2.environ.get('K_SU', '99'))

    def _cut(k):
        return k >= _CUT

    def setup():
        st = contextlib.ExitStack()
        tmp = sb(st, "su_tmp", [128, 1280], F32); r_tmp = Res("su_tmp")
        tmp2 = sb(st, "su_tmp2", [128, 1280], F32); r_tmp2 = Res("su_tmp2")
        S.op("pool", I_("memset", ident_f[:], 1.0), writes=[r_identf])
        S.op("pool", I_("affine_select", out=ident_f[:], in_=ident_f[:], pattern=[[-1, 128]],
                                               compare_op=ALU.is_equal, fill=0.0, base=0, channel_multiplier=1),
             reads=[r_identf], writes=[r_identf])
        S.op("dve", I_("tensor_copy", out=ident_b[:], in_=ident_f[:]), reads=[r_identf], writes=[r_identb])
        S.op("pool", I_("memset", jmat_f[:], 1.0), writes=[r_jf])
        S.op("pool", I_("affine_select", out=jmat_f[:], in_=jmat_f[:], pattern=[[1, 128]],
                                               compare_op=ALU.is_equal, fill=0.0, base=-127, channel_multiplier=1),
             reads=[r_jf], writes=[r_jf])
        S.op("dve", I_("tensor_copy", out=jmat_b[:], in_=jmat_f[:]), reads=[r_jf], writes=[r_jb])
        S.op("dve", I_("memset", ones_b[:], 1.0), writes=[r_ones])
        S.op("dve", I_("memset", zero_c[:], 0.0), writes=[r_zero])
        for ci, cv in enumerate((D * EPS, 256 * EPS, 128 * EPS, EPS)):
            S.op("dve", I_("memset", epsc[:, ci:ci + 1], cv), writes=[r_epsc])
        if _cut(1):
            S.barrier(); st.close(); return
        for (t, r, src, off) in ((g1, r_g1, I["norm_g"], 0), (g2, r_g2, I["norm_g"], D), (gfin, r_gf, I["final_g"], 0)):
            S.dma("sp", "su", I_("dma_start", out=t[:], in_=bc_ap(src, off, D)), writes=[r])
            S.op("dve", I_("tensor_scalar", out=t[:], in0=t[:], scalar1=32.0, scalar2=None, op0=ALU.mult),
                 reads=[r], writes=[r])
        if _cut(2):
            S.barrier(); st.close(); return
        S.dma("sp", "su", I_("dma_start", out=tmp[:, 0:256], in_=bc_ap(I["lam4"], 0, 256)), writes=[r_tmp])
        S.op("dve", I_("tensor_tensor", out=tmp2[:, 0:64], in0=tmp[:, 0:64], in1=tmp[:, 64:128], op=ALU.mult), reads=[r_tmp], writes=[r_tmp2])
        S.op("dve", I_("tensor_tensor", out=tmp2[:, 64:128], in0=tmp[:, 128:192], in1=tmp[:, 192:256], op=ALU.mult), reads=[r_tmp], writes=[r_tmp2])
        S.op("dve", I_("reduce_sum", out=tmp2[:, 128:130], in_=tmp2[:, 0:128].rearrange("p (a b) -> p a b", a=2), axis=AX.X),
             reads=[r_tmp2], writes=[r_tmp2])
        S.op("act", I_("activation", out=tmp2[:, 130:132], in_=tmp2[:, 128:130], func=AF.Exp), reads=[r_tmp2], writes=[r_tmp2])
        S.op("dve", I_("tensor_tensor", out=lamc[:, 0:1], in0=tmp2[:, 130:131], in1=tmp2[:, 131:132], op=ALU.subtract),
             reads=[r_tmp2], writes=[r_lam])
        S.op("dve", I_("tensor_scalar", out=lamc[:, 0:1], in0=lamc[:, 0:1], scalar1=0.2, scalar2=None, op0=ALU.add), reads=[r_lam], writes=[r_lam])
        S.op("dve", I_("tensor_scalar", out=lamc[:, 1:2], in0=lamc[:, 0:1], scalar1=-1.0, scalar2=None, op0=ALU.mult), reads=[r_lam], writes=[r_lam])
        if _cut(3):
            S.barrier(); st.close(); return
        S.dma("sp", "su", I_("dma_start", out=lf[:], in_=bc_ap(I["ret_decay"], 0, 8)), writes=[r_lf])
        S.op("act", I_("activation", out=lf[:], in_=lf[:], func=AF.Exp), reads=[r_lf], writes=[r_lf])
        S.op("dve", I_("tensor_scalar", out=lf[:], in0=lf[:], scalar1=-1.0, scalar2=None, op0=ALU.mult), reads=[r_lf], writes=[r_lf])
        S.op("act", I_("activation", out=dec[:], in_=lf[:], func=AF.Exp, scale=128.0), reads=[r_lf], writes=[r_dec])
        if _cut(4):
            S.barrier(); st.close(); return
        dpos = sb(st, "su_dpos", [128, 128], F32); dneg = sb(st, "su_dneg", [128, 128], F32)
        col = sb(st, "su_col", [128, 2], F32); rowf = sb(st, "su_rowf", [128, 128], F32); rowb = sb(st, "su_rowb", [128, 128], F32)
        r_c = Res("su_consts")
        for (t, nm) in ((dpos, "c_dpos"), (dneg, "c_dneg"), (col, "c_col"), (rowf, "c_rowf"), (rowb, "c_rowb")):
            S.dma("sp", "su", I_("dma_start", out=t[:], in_=I[nm]), writes=[r_c])
        SCL = 128.0 ** -0.5
        for h in range(4):
            if 'd' not in _os2.environ.get('K_SKIP', ''):
                S.op("dve", I_("tensor_scalar", out=tmp[:, 0:128], in0=dpos[:], scalar1=lf[:, h:h + 1], scalar2=None, op0=ALU.mult),
                     reads=[r_c, r_lf], writes=[r_tmp])
                S.op("dve", I_("scalar_tensor_tensor", out=tmp[:, 0:128], in0=dneg[:], scalar=lf[:, 4 + h:5 + h], in1=tmp[:, 0:128],
                                                                        op0=ALU.mult, op1=ALU.add), reads=[r_c, r_lf, r_tmp], writes=[r_tmp])
                S.op("act", I_("activation", out=dmatT[:, h, :], in_=tmp[:, 0:128], func=AF.Exp), reads=[r_tmp], writes=[r_dmat])
            if 'q' not in _os2.environ.get('K_SKIP', ''):
                S.op("act", I_("activation", out=QF[:, h, :], in_=rowf[:], func=AF.Exp, scale=lf[:, h:h + 1]), reads=[r_c, r_lf], writes=[r_QF])
                S.op("act", I_("activation", out=QB[:, h, :], in_=rowb[:], func=AF.Exp, scale=lf[:, 4 + h:5 + h]), reads=[r_c, r_lf], writes=[r_QB])
        S.op("dve", I_("tensor_scalar", out=dmatT[:], in0=dmatT[:], scalar1=SCL, scalar2=None, op0=ALU.mult), reads=[r_dmat], writes=[r_dmat])
        if 'w' not in _os2.environ.get('K_SKIP', ''):
            S.op("act", I_("activation", out=wfb[:, 0:4], in_=lf[:, 0:4], func=AF.Exp, scale=col[:, 0:1]), reads=[r_c, r_lf], writes=[r_wfb])
            S.op("act", I_("activation", out=wfb[:, 4:8], in_=lf[:, 4:8], func=AF.Exp, scale=col[:, 1:2]), reads=[r_c, r_lf], writes=[r_wfb])
        S.op("dve", I_("tensor_scalar", out=wfb[:], in0=wfb[:], scalar1=SCL, scalar2=None, op0=ALU.mult), reads=[r_wfb], writes=[r_wfb])
        if _cut(5):
            S.barrier(); st.close(); return
        S.dma("sp", "su", I_("dma_start", out=farb[:, 0:8], in_=bc_ap(I["rel_bias"], 15 * 8, 8)), writes=[r_farb])
        S.dma("sp", "su", I_("dma_start", out=farb[:, 8:16], in_=bc_ap(I["rel_bias"], 31 * 8, 8)), writes=[r_farb])
        S.dma("sp", "su", I_("dma_start", out=dng[:], in_=bc_ap(I["diff_norm_g"], 0, 128)), writes=[r_dng])
        S.op("dve", I_("tensor_scalar", out=dng[:], in0=dng[:], scalar1=0.8, scalar2=None, op0=ALU.mult), reads=[r_dng], writes=[r_dng])
        S.dma("sp", "su", I_("dma_start", out=flb[:], in_=bc_ap(I["fl"], 0, NO1)), writes=[r_flb])
        S.op("dve", I_("tensor_scalar", out=fla[:], in0=flb[:], scalar1=-1.0, scalar2=1.0, op0=ALU.mult, op1=ALU.add), reads=[r_flb], writes=[r_fla])
        for h in range(4):
            S.op("dve", I_("scalar_tensor_tensor", out=oth_af[:, :, h], in0=flb[:], scalar=dec[:, h:h + 1], in1=fla[:],
                                                                    op0=ALU.mult, op1=ALU.add), reads=[r_flb, r_fla, r_dec], writes=[r_oaf])
            S.op("dve", I_("scalar_tensor_tensor", out=oth_ab[:, :, h], in0=fla[:], scalar=dec[:, 4 + h:5 + h], in1=flb[:],
                                                                    op0=ALU.mult, op1=ALU.add), reads=[r_flb, r_fla, r_dec], writes=[r_oab])
        for h in range(8):
            S.op("dve", I_("tensor_scalar", out=tmp[:, 0:NO1], in0=fla[:], scalar1=farb[:, 8 + h:9 + h], scalar2=None, op0=ALU.mult),
                 reads=[r_fla, r_farb], writes=[r_tmp])
            S.op("dve", I_("scalar_tensor_tensor", out=oth_fb[:, :, h], in0=flb[:], scalar=farb[:, h:h + 1], in1=tmp[:, 0:NO1],
                                                                    op0=ALU.mult, op1=ALU.add), reads=[r_flb, r_farb, r_tmp], writes=[r_ofb])
        if _cut(6):
            S.barrier(); st.close(); return
        cw_sb = sb(st, "su_cw", [31, 2048], F32); r_cw = Res("su_cw")
        S.dma("sp", "su", I_("dma_start", out=cw_sb[:], in_=I["conv_w"]), writes=[r_cw])
        for cb in range(16):
            pt, pr = PS[3 + (cb % 4)]
            S.op("pe", I_("transpose", pt[:, 0:31], cw_sb[:, cb * 128:(cb + 1) * 128], ident_f[0:31, 0:31]),
                 reads=[r_cw, r_identf], writes=[pr])
            S.op("act", I_("activation", out=cwT[:, cb, :], in_=pt[:, 0:31], func=AF.Copy), reads=[pr], writes=[r_cwT])
        with nc.allow_non_contiguous_dma(reason="tiny per-channel params"):
            for (t, r, nm) in ((cvb, r_cvb, "conv_b"), (cng, r_cng, "conv_norm_g"), (cnb, r_cnb, "conv_norm_b")):
                S.dma("sp", "su", I_("dma_start", out=t[:], in_=bass.AP(I[nm].tensor, 0, [[1, 128], [128, 16]]), allow_slow_non_contiguous=True), writes=[r])
        if _cut(7):
            S.barrier(); st.close(); return
        rb = sb(st, "su_rb", [32, 8], F32); et = sb(st, "su_et", [32, 1280], F32); r_rb = Res("su_rb")
        S.dma("sp", "su", I_("dma_start", out=rb[:], in_=I["rel_bias"]), writes=[r_rb])
        S.dma("sp", "su", I_("dma_start", out=et[:], in_=I["c_et"]), writes=[r_rb])
        grev_sb = sb(st, "su_grev", [8, 1280], BF16); r_gs = Res("su_grev")
        for ci, (c0, cn) in enumerate(((0, 512), (512, 512), (1024, 256))):
            pt, pr = PS[ci]
            S.op("pe", I_("matmul", pt[0:8, 0:cn], lhsT=rb[:, :], rhs=et[:, c0:c0 + cn], start=True, stop=True),
                 reads=[r_rb], writes=[pr])
            S.op("act", I_("activation", out=grev_sb[:, c0:c0 + cn], in_=pt[0:8, 0:cn], func=AF.Copy),
                 reads=[pr], writes=[r_gs])
        S.dma("sp", "su2", I_("dma_start", out=GREV, in_=grev_sb[:]), reads=[r_gs], writes=[r_grev])
        S.barrier()
        st.close()

    def phase1():
        st = contextlib.ExitStack()
        W = sb(st, "p1_W", [128, 8, 7168], BF16); r_W = Res("p1_W")
        for j in range(8):
            S.dma("pool", "p1w", I_("dma_start", out=W[:, j, :], in_=I["w_in_mix"][j * 128:(j + 1) * 128, :]), writes=[r_W])
        xr = Ring([(sb(st, "p1_x%d" % i, [128, D], F32), Res("x")) for i in range(2)])
        csr = Ring([(sb(st, "p1_cs%d" % i, [128, 128], F32), Res("cs")) for i in range(2)])
        hr = Ring([(sb(st, "p1_h%d" % i, [128, D], BF16), Res("h")) for i in range(2)])
        junk = sb(st, "p1_junk", [128, D], BF16); r_junk = Res("junk")
        ssr = Ring([(sb(st, "p1_ss%d" % i, [128, 2], F32), Res("ss")) for i in range(2)])
        hTr = Ring([(sb(st, "p1_hT%d" % i, [128, 8, 512], BF16), Res("hT")) for i in range(1)])
        qkr = Ring([(sb(st, "p1_qk%d" % i, [128, 1024], F32), Res("qk")) for i in range(1)])
        rt = [(sb(st, "p1_rt%d" % i, [128, 512], F32), Res("rt")) for i in range(4)]
        ropr = Ring([(sb(st, "p1_rop%d" % i, [128, 1024], BF16), Res("rop")) for i in range(2)])
        rvr = Ring([(sb(st, "p1_rv%d" % i, [128, 1024], BF16), Res("rv")) for i in range(2)])
        gtr = Ring([(sb(st, "p1_gt%d" % i, [128, 2048], BF16), Res("gt")) for i in range(1)])
        dvr = Ring([(sb(st, "p1_dv%d" % i, [128, 4, 1024], BF16), Res("dv")) for i in range(1)])
        fmr = Ring([(sb(st, "p1_fm%d" % i, [128, 512], BF16), Res("fm")) for i in range(4)])
        qkTr = Ring([(sb(st, "p1_qkT%d" % i, [128, 8, 512], BF16), Res("qkT")) for i in range(1)])
        psT = Ring(PS[0:2]); pstm = Ring(PS[2:5]); psfm = Ring(PS[5:7]); psT2 = Ring(PS[7:8])

        for s in ("a", "b"):
            q = seqs[s]
            sc = SC[s]; rs = sc["res"]
            ngroups = (q["nreg"] + q["noth"]) // 4
            for g in range(ngroups):
                full = (g * 4) < q["nreg"]
                t0 = g * 512
                hT, r_hT = hTr.next()
                dv, r_dv = dvr.next()
                qkT, r_qkT = qkTr.next()
                for i in range(4):
                    tt = t0 + i * 128
                    x, r_x = xr.next()
                    cs, r_cs = csr.next()
                    S.dma("sp", "p1x%d" % xr.slot(), I_("dma_start", out=x[:], in_=X[s][tt:tt + 128, :]), writes=[r_x])
                    S.dma("sp", "p1c%d" % csr.slot(), I_("dma_start", out=cs[:], in_=CS[s][tt:tt + 128, :]), writes=[r_cs])
                    ss, r_ss = ssr.next()
                    S.op("act", I_("activation", out=junk[:], in_=x[:], func=AF.Square, accum_out=ss[:, 0:1]),
                         reads=[r_x], writes=[r_junk, r_ss])
                    S.op("act", I_("activation", out=ss[:, 1:2], in_=ss[:, 0:1], func=AF.Sqrt, bias=epsc[:, 0:1]), reads=[r_ss, r_epsc], writes=[r_ss])
                    S.op("dve", I_("reciprocal", out=ss[:, 1:2], in_=ss[:, 1:2]), reads=[r_ss], writes=[r_ss])
                    h, r_h = hr.next()
                    S.op("act", I_("activation", out=x[:], in_=x[:], func=AF.Copy, scale=ss[:, 1:2]), reads=[r_x, r_ss], writes=[r_x])
                    S.op("pool", I_("tensor_tensor", out=h[:], in0=x[:], in1=g1[:], op=ALU.mult), reads=[r_x, r_g1], writes=[r_h])
                    pt, pr = psT.next()
                    ptb = pt[:].bitcast(BF16)
                    for j in range(8):
                        S.op("pe", I_("transpose", ptb[:, j * 128:(j + 1) * 128], h[:, j * 128:(j + 1) * 128], ident_b[:]),
                             reads=[r_h, r_identb], writes=[pr], inc=(j == 7), skip_self=True)
                    S.op("act", I_("activation", out=hT[:, :, i * 128:(i + 1) * 128], in_=ptb.rearrange("p (j t) -> p j t", j=8), func=AF.Copy),
                         reads=[pr], writes=[r_hT])
                    if full:
                        cgs = [("qk", 0), ("qk", 512), ("rv", 1024), ("rv", 1536), ("rg", 2048), ("rg", 2560),
                               ("dv", 5120), ("dv", 5632), ("dg", 6144), ("dg", 6656)]
                    else:
                        cgs = [("qk", 512), ("rv", 1024), ("rv", 1536), ("dv", 5120), ("dv", 5632)]
                    qk, r_qk = qkr.next()
                    rv, r_rv = rvr.next()
                    gt, r_gt = gtr.next()
                    for (kind, c0) in cgs:
                        pm, pmr = pstm.next()
                        for j in range(8):
                            S.op("pe", I_("matmul", pm[:, :], lhsT=hT[:, j, i * 128:(i + 1) * 128], rhs=W[:, j, c0:c0 + 512],
                                                                                     start=(j == 0), stop=(j == 7)),
                                 reads=[r_hT, r_W], writes=[pmr], inc=(j == 7), skip_self=True)
                        if kind == "qk":
                            S.op("act", I_("activation", out=qk[:, c0:c0 + 512], in_=pm[:, :], func=AF.Copy), reads=[pmr], writes=[r_qk])
                        elif kind == "rv":
                            S.op("dve", I_("tensor_copy", out=rv[:, c0 - 1024:c0 - 512], in_=pm[:, :]), reads=[pmr], writes=[r_rv])
                        elif kind == "rg":
                            S.op("act", I_("activation", out=gt[:, c0 - 2048:c0 - 1536], in_=pm[:, :], func=AF.Silu), reads=[pmr], writes=[r_gt])
                        elif kind == "dg":
                            S.op("act", I_("activation", out=gt[:, c0 - 6144 + 1024:c0 - 6144 + 1536], in_=pm[:, :], func=AF.Silu), reads=[pmr], writes=[r_gt])
                        elif kind == "dv":
                            S.op("dve", I_("tensor_copy", out=dv[:, i, c0 - 5120:c0 - 5120 + 512], in_=pm[:, :]), reads=[pmr], writes=[r_dv])
                    S.dma("sp", "p1rv%d" % rvr.slot(), I_("dma_start", out=sc["RV"][tt:tt + 128, :], in_=rv[:]), reads=[r_rv], writes=[rs["RV"]])
                    if full:
                        S.dma("sp", "p1gt%d" % gtr.slot(), I_("dma_start", out=sc["GT"][tt:tt + 128, :], in_=gt[:]), reads=[r_gt], writes=[rs["GT"]])
                    hlo = 0 if full else 4
                    nh = 8 - hlo
                    rop, r_rop = ropr.next()
                    v4 = qk[:].rearrange("p (h c d) -> p h c d", h=8, c=2)
                    o4 = rop[:].rearrange("p (h c d) -> p h c d", h=8, c=2)
                    x1 = v4[:, hlo:8, 0, :]; x2 = v4[:, hlo:8, 1, :]
                    cosb = cs[:, 0:64].unsqueeze(1).to_broadcast([128, nh, 64]); sinb = cs[:, 64:128].unsqueeze(1).to_broadcast([128, nh, 64])
                    ta = [rt[k][0][:, 0:nh * 64].rearrange("p (h d) -> p h d", h=nh) for k in range(4)]
                    S.op("dve", I_("tensor_tensor", out=ta[0], in0=x1, in1=cosb, op=ALU.mult), reads=[r_qk, r_cs], writes=[rt[0][1]])
                    S.op("dve", I_("tensor_tensor", out=ta[1], in0=x2, in1=sinb, op=ALU.mult), reads=[r_qk, r_cs], writes=[rt[1][1]])
                    S.op("dve", I_("tensor_tensor", out=o4[:, hlo:8, 0, :], in0=ta[0], in1=ta[1], op=ALU.subtract),
                         reads=[rt[0][1], rt[1][1]], writes=[r_rop])
                    S.op("pool", I_("tensor_tensor", out=ta[2], in0=x1, in1=sinb, op=ALU.mult), reads=[r_qk, r_cs], writes=[rt[2][1]])
                    S.op("pool", I_("tensor_tensor", out=ta[3], in0=x2, in1=cosb, op=ALU.mult), reads=[r_qk, r_cs], writes=[rt[3][1]])
                    S.op("pool", I_("tensor_tensor", out=o4[:, hlo:8, 1, :], in0=ta[2], in1=ta[3], op=ALU.add),
                         reads=[rt[2][1], rt[3][1]], writes=[r_rop])
                    S.dma("sp", "p1rk%d" % ropr.slot(), I_("dma_start", out=sc["RK"][tt:tt + 128, :], in_=rop[:, 512:1024]), reads=[r_rop], writes=[rs["RK"]])
                    if full:
                        p2, p2r = psT2.next()
                        p2b = p2[:].bitcast(BF16)
                        for j in range(8):
                            S.op("pe", I_("transpose", p2b[:, j * 128:(j + 1) * 128], rop[:, j * 128:(j + 1) * 128], ident_b[:]),
                                 reads=[r_rop, r_identb], writes=[p2r], inc=(j == 7), skip_self=True)
                        S.op("dve", I_("tensor_copy", out=qkT[:, :, i * 128:(i + 1) * 128], in_=p2b.rearrange("p (j t) -> p j t", j=8)),
                             reads=[p2r], writes=[r_qkT])
                kb0 = g * 4
                for hh in range(8):
                    S.dma("sp", "p1dv%d" % dvr.slot(), I_("dma_start", out=sc["DVH"][hh, :, kb0:kb0 + 4, :], in_=dv[:, :, hh * 128:(hh + 1) * 128]),
                          reads=[r_dv], writes=[rs["DVH"]])
                if full:
                    S.dma("sp", "p1qkT%d" % qkTr.slot(), I_("dma_start", out=sc["RQT"][:, :, t0:t0 + 512].rearrange("h d t -> d h t"), in_=qkT[:, 0:4, :]),
                          reads=[r_qkT], writes=[rs["RQT"]])
                    S.dma("sp", "p1qkT%d" % qkTr.slot(), I_("dma_start", out=sc["RKT"][:, :, t0:t0 + 512].rearrange("h d t -> d h t"), in_=qkT[:, 4:8, :]),
                          reads=[r_qkT], writes=[rs["RKT"]])
                blocks = ([("dq", hh) for hh in range(8)] if full else []) + [("dk", hh) for hh in range(8)]
                for (kind, hh) in blocks:
                    c0 = (3072 if kind == "dq" else 4096) + hh * 128
                    pm, pmr = psfm.next()
                    for j in range(8):
                        S.op("pe", I_("matmul", pm[:, :], lhsT=W[:, j, c0:c0 + 128], rhs=hT[:, j, :], start=(j == 0), stop=(j == 7)),
                             reads=[r_hT, r_W], writes=[pmr], inc=(j == 7), skip_self=True)
                    fm, r_fm = fmr.next()
                    slot = fmr.slot()
                    if kind == "dq":
                        S.op("act", I_("activation", out=fm[:], in_=pm[:, :], func=AF.Copy, scale=0.125), reads=[pmr], writes=[r_fm])
                        S.dma("sp", "p1fm%d" % slot, I_("dma_start", out=sc["QT"][hh, :, t0:t0 + 512], in_=fm[:]), reads=[r_fm], writes=[rs["QT"]])
                    else:
                        S.op("dve", I_("tensor_copy", out=fm[:], in_=pm[:, :]), reads=[pmr], writes=[r_fm])
                        S.dma("sp", "p1fm%d" % slot, I_("dma_start", out=sc["KT"][hh, :, t0:t0 + 512], in_=fm[:]), reads=[r_fm], writes=[rs["KT"]])
        S.barrier()
        st.close()

    def phase2():
        st = contextlib.ExitStack()
        NRmax = max(NBA, NRB)
        R32 = sb(st, "p2_R", [128, 4, 256], F32); r_R = Res("R")
        L32 = sb(st, "p2_L", [128, 4, 256], F32); r_L = Res("L")
        Rbf = Ring([(sb(st, "p2_Rbf%d" % i, [128, 4, 256], BF16), Res("Rbf")) for i in range(2)])
        Lst = sb(st, "p2_Lst", [128, NRmax, 4, 256], BF16)
        r_Lst = [Res("Lst%d" % n) for n in range(NRmax)]
        rkr = Ring([(sb(st, "p2_rk%d" % i, [128, 512], BF16), Res("rk")) for i in range(3)])
        rvr = Ring([(sb(st, "p2_rv%d" % i, [128, 1024], BF16), Res("rv")) for i in range(3)])
        rkwr = Ring([(sb(st, "p2_rkw%d" % i, [128, 512], BF16), Res("rkw")) for i in range(2)])
        qTr = Ring([(sb(st, "p2_qT%d" % i, [128, 4, 128], BF16), Res("qT")) for i in range(2)])
        kTr = Ring([(sb(st, "p2_kT%d" % i, [128, 4, 128], BF16), Res("kT")) for i in range(2)])
        ATr = Ring([(sb(st, "p2_AT%d" % i, [128, 128], BF16), Res("AT")) for i in range(3)])
        qfr = Ring([(sb(st, "p2_qf%d" % i, [128, 128], BF16), Res("qf")) for i in range(3)])
        qbr = Ring([(sb(st, "p2_qb%d" % i, [128, 128], BF16), Res("qb")) for i in range(3)])
        kvt = Ring([(sb(st, "p2_kvt%d" % i, [128, 256], F32), Res("kvt")) for i in range(2)])
        ssr = Ring([(sb(st, "p2_ss%d" % i, [128, 2], F32), Res("ss")) for i in range(4)])
        junk = sb(st, "p2_junk", [128, 256], BF16); r_junk = Res("junk")
        ysr = Ring([(sb(st, "p2_ys%d" % i, [128, 1024], BF16), Res("ys")) for i in range(2)])
        ps_s = Ring(PS[0:2]); ps_o = Ring(PS[2:5]); ps_kv = Ring(PS[5:8])

        def load_kv(s, tt):
            rk, r_rk = rkr.next(); rv, r_rv = rvr.next()
            S.dma("sp", "p2rk%d" % rkr.slot(), I_("dma_start", out=rk[:], in_=SC[s]["RK"][tt:tt + 128, :]), reads=[SC[s]["res"]["RK"]], writes=[r_rk])
            S.dma("sp", "p2rv%d" % rvr.slot(), I_("dma_start", out=rv[:], in_=SC[s]["RV"][tt:tt + 128, :]), reads=[SC[s]["res"]["RV"]], writes=[r_rv])
            return rk, r_rk, rv, r_rv

        def kv_summ(rk, r_rk, rv, r_rv, wofs):
            rkw, r_rkw = rkwr.next()
            wv = wfb[:, wofs:wofs + 4].unsqueeze(2).to_broadcast([128, 4, 128])
            S.op("pool", I_("tensor_tensor", out=rkw[:].rearrange("p (h d) -> p h d", h=4), in0=rk[:].rearrange("p (h d) -> p h d", h=4), in1=wv, op=ALU.mult),
                 reads=[r_rk, r_wfb], writes=[r_rkw])
            outs = []
            for hp in range(2):
                pk, pkr = ps_kv.next()
                for hq in range(2):
                    h = hp * 2 + hq
                    S.op("pe", I_("matmul", pk[:, hq * 256:(hq + 1) * 256], lhsT=rkw[:, h * 128:(h + 1) * 128], rhs=rv[:, h * 256:(h + 1) * 256], start=True, stop=True),
                         reads=[r_rkw, r_rv], writes=[pkr], skip_self=True)
                    outs.append((pk[:, hq * 256:(hq + 1) * 256], pkr))
            return outs

        for s in ("a", "b"):
            q = seqs[s]
            NR, NO = q["nreg"], q["noth"]
            S.op("dve", I_("memset", R32[:], 0.0), writes=[r_R])
            S.op("dve", I_("memset", L32[:], 0.0), writes=[r_L])
            for direction in (0, 1):
                order = range(NO) if direction == 0 else range(NO - 1, -1, -1)
                st32, r_st = (R32, r_R) if direction == 0 else (L32, r_L)
                acoef = oth_af if direction == 0 else oth_ab
                r_ac = r_oaf if direction == 0 else r_oab
                bcoef, r_bc = (flb, r_flb) if direction == 0 else (fla, r_fla)
                for jj in order:
                    tt = (NR + jj) * 128
                    rk, r_rk, rv, r_rv = load_kv(s, tt)
                    outs = kv_summ(rk, r_rk, rv, r_rv, 0 if direction == 0 else 4)
                    for h in range(4):
                        pa, par = outs[h]
                        kt, r_kt = kvt.next()
                        S.op("act", I_("activation", out=kt[:], in_=pa, func=AF.Copy, scale=bcoef[:, jj:jj + 1]), reads=[par, r_bc], writes=[r_kt])
                        S.op("dve", I_("scalar_tensor_tensor", out=st32[:, h, :], in0=st32[:, h, :], scalar=acoef[:, jj, h:h + 1], in1=kt[:], op0=ALU.mult, op1=ALU.add),
                             reads=[r_st, r_ac, r_kt], writes=[r_st])
            for n in range(NR - 1, -1, -1):
                S.op("act", I_("activation", out=Lst[:, n, :, :], in_=L32[:], func=AF.Copy), reads=[r_L], writes=[r_Lst[n]])
                if n == 0:
                    break
                rk, r_rk, rv, r_rv = load_kv(s, n * 128)
                outs = kv_summ(rk, r_rk, rv, r_rv, 4)
                for h in range(4):
                    pa, par = outs[h]
                    S.op("dve", I_("scalar_tensor_tensor", out=L32[:, h, :], in0=L32[:, h, :], scalar=dec[:, 4 + h:5 + h], in1=pa, op0=ALU.mult, op1=ALU.add),
                         reads=[r_L, r_dec, par], writes=[r_L])
            for n in range(NR):
                tt = n * 128
                rb_, r_rb = Rbf.next()
                S.op("act", I_("activation", out=rb_[:], in_=R32[:], func=AF.Copy), reads=[r_R], writes=[r_rb])
                rk, r_rk, rv, r_rv = load_kv(s, tt)
                qT, r_qT = qTr.next(); kT, r_kT = kTr.next()
                S.dma("sp", "p2qT%d" % qTr.slot(), I_("dma_start", out=qT[:], in_=SC[s]["RQT"][:, :, tt:tt + 128].rearrange("h d t -> d h t")),
                      reads=[SC[s]["res"]["RQT"]], writes=[r_qT])
                S.dma("sp", "p2kT%d" % kTr.slot(), I_("dma_start", out=kT[:], in_=SC[s]["RKT"][:, :, tt:tt + 128].rearrange("h d t -> d h t")),
                      reads=[SC[s]["res"]["RKT"]], writes=[r_kT])
                ys, r_ys = ysr.next()
                psS, psSr = ps_s.next()
                for h in range(4):
                    S.op("pe", I_("matmul", psS[:, h * 128:(h + 1) * 128], lhsT=kT[:, h, :], rhs=qT[:, h, :], start=True, stop=True),
                         reads=[r_kT, r_qT], writes=[psSr], skip_self=True)
                for h in range(4):
                    AT, r_AT = ATr.next(); qf, r_qf = qfr.next(); qb, r_qb = qbr.next()
                    S.op("dve", I_("tensor_tensor", out=AT[:], in0=psS[:, h * 128:(h + 1) * 128], in1=dmatT[:, h, :], op=ALU.mult),
                         reads=[psSr, r_dmat], writes=[r_AT])
                    S.op("pool", I_("tensor_tensor", out=qf[:], in0=qT[:, h, :], in1=QF[:, h, :], op=ALU.mult), reads=[r_qT, r_QF], writes=[r_qf])
                    S.op("pool", I_("tensor_tensor", out=qb[:], in0=qT[:, h, :], in1=QB[:, h, :], op=ALU.mult), reads=[r_qT, r_QB], writes=[r_qb])
                    po, por = ps_o.next()
                    S.op("pe", I_("matmul", po[:, 0:256], lhsT=AT[:], rhs=rv[:, h * 256:(h + 1) * 256], start=True, stop=False),
                         reads=[r_AT, r_rv], writes=[por], inc=False, skip_self=True)
                    S.op("pe", I_("matmul", po[:, 0:256], lhsT=qf[:], rhs=rb_[:, h, :], start=False, stop=False),
                         reads=[r_qf, r_rb], writes=[por], inc=False, skip_self=True)
                    S.op("pe", I_("matmul", po[:, 0:256], lhsT=qb[:], rhs=Lst[:, n, h, :], start=False, stop=True),
                         reads=[r_qb, r_Lst[n]], writes=[por], skip_self=True)
                    ss, r_ss = ssr.next()
                    S.op("act", I_("activation", out=junk[:], in_=po[:, 0:256], func=AF.Square, accum_out=ss[:, 0:1]), reads=[por], writes=[r_junk, r_ss])
                    S.op("act", I_("activation", out=ss[:, 1:2], in_=ss[:, 0:1], func=AF.Sqrt, bias=epsc[:, 1:2]), reads=[r_ss, r_epsc], writes=[r_ss])
                    S.op("dve", I_("reciprocal", out=ss[:, 1:2], in_=ss[:, 1:2]), reads=[r_ss], writes=[r_ss])
                    S.op("dve", I_("tensor_scalar", out=ys[:, h * 256:(h + 1) * 256], in0=po[:, 0:256], scalar1=ss[:, 1:2], scalar2=16.0, op0=ALU.mult, op1=ALU.mult),
                         reads=[por, r_ss], writes=[r_ys])
                S.dma("sp", "p2ys%d" % ysr.slot(), I_("dma_start", out=SC[s]["Y"][tt:tt + 128, 0:1024], in_=ys[:]), reads=[r_ys], writes=[SC[s]["res"]["Y"]])
                if n < NR - 1:
                    outs = kv_summ(rk, r_rk, rv, r_rv, 0)
                    for h in range(4):
                        pa, par = outs[h]
                        S.op("dve", I_("scalar_tensor_tensor", out=R32[:, h, :], in0=R32[:, h, :], scalar=dec[:, h:h + 1], in1=pa, op0=ALU.mult, op1=ALU.add),
                             reads=[r_R, r_dec, par], writes=[r_R])
        S.barrier()
        st.close()

    def phase3():
        st = contextlib.ExitStack()
        SCmax = max(SA, SBT); SRmax = max(SA, SRB); NKmax = SCmax // 128
        KTr = Ring([(sb(st, "p3_KT%d" % i, [128, SCmax], BF16), Res("KT")) for i in range(2)])
        QTr = Ring([(sb(st, "p3_QT%d" % i, [128, 2, SRmax], BF16), Res("QT")) for i in range(2)])
        for (qt_, r_qt_) in QTr.items:
            S.op("pool", I_("memset", qt_[64:128, 0, :], 0.0), writes=[r_qt_])
            S.op("pool", I_("memset", qt_[0:64, 1, :], 0.0), writes=[r_qt_])
        Vr = Ring([(sb(st, "p3_V%d" % i, [128, NKmax, 130], BF16), Res("V")) for i in range(2)])
        Hr = Ring([(sb(st, "p3_H%d" % i, [128, 6, 512], BF16), Res("H")) for i in range(2)])
        for (v, r_v) in Vr.items:
            S.op("pool", I_("memset", v[:, :, 128:130], 1.0), writes=[r_v])
        ptr = Ring([(sb(st, "p3_pt%d" % i, [128, 512], BF16), Res("pt")) for i in range(8)])
        Oe = [(sb(st, "p3_Oe%d" % i, [128, 130], F32), Res("Oe")) for i in range(4)]
        rcp = Ring([(sb(st, "p3_rc%d" % i, [128, 12], F32), Res("rc")) for i in range(4)])
        junkf = sb(st, "p3_junkf", [128, 128], F32); r_junkf = Res("junkf")
        dr = Ring([(sb(st, "p3_d%d" % i, [128, 128], F32), Res("d")) for i in range(4)])
        junk = sb(st, "p3_junk", [128, 128], BF16); r_junk = Res("junk")
        yor = Ring([(sb(st, "p3_yo%d" % i, [128, 128], BF16), Res("yo")) for i in range(4)])
        psS = Ring(PS[0:3]); psOT = Ring(PS[3:5]); psZ = Ring(PS[5:7]); psE = Ring(PS[7:8])
        OTnr = Ring([(sb(st, "p3_OTn%d" % i, [128, 512], F32), Res("OTn")) for i in range(2)])
        ones_f128 = sb(st, "p3_ones_f128", [128, 128], F32)
        r_onesf128 = Res("onesf128")
        S.op("pool", I_("memset", ones_f128[:], 1.0), writes=[r_onesf128])
        accr = Ring([(sb(st, "p3_acc%d" % i, [128, 512], F32), Res("acc")) for i in range(2)])
        accpr = Ring([(sb(st, "p3_accp%d" % i, [128, 512], F32), Res("accp")) for i in range(2)])
        deferred = []
        DEFER = 6
        OTer = Ring([(sb(st, "p3_OTe%d" % i, [128, 512], F32), Res("OTe")) for i in range(2)])
        ones_f = sb(st, "p3_ones_f", [128, 2], F32); r_onesf = Res("onesf")
        S.op("pool", I_("memset", ones_f[:], 1.0), writes=[r_onesf])
        rnd = {}

        LA = 2
        heads = [(s, h) for s in ("a", "b") for h in range(8)]
        bufs = [(KTr.items[i], QTr.items[i], Vr.items[i], Hr.items[i]) for i in range(2)]

        def emit_loads(hi):
            s, h = heads[hi]
            q = seqs[s]
            NR, NO, SR, SCX = q["nreg"], q["noth"], q["sreg"], q["sctx"]
            NK = NR + NO
            sc = SC[s]; rs = sc["res"]
            b = hi % 2
            (KT, r_KT), (QT, r_QT), (V, r_V), (H, r_H) = bufs[b]
            S.dma("sp", "p3kt%d" % b, I_("dma_start", out=KT[:, 0:SCX], in_=sc["KT"][h, :, :]), reads=[rs["KT"]], writes=[r_KT])
            S.dma("sp", "p3qt%d" % b, I_("dma_start", out=QT[0:64, 0, 0:SR], in_=sc["QT"][h, 0:64, :]), reads=[rs["QT"]], writes=[r_QT])
            S.dma("sp", "p3qt%d" % b, I_("dma_start", out=QT[64:128, 1, 0:SR], in_=sc["QT"][h, 64:128, :]), reads=[rs["QT"]], writes=[r_QT])
            S.dma("sp", "p3v%d" % b, I_("dma_start", out=V[:, 0:NK, 0:128], in_=sc["DVH"][h, :, :, :]), reads=[rs["DVH"]], writes=[r_V])
            for di in range(6):
                delta = di - 1
                off = h * 1280 + 512 - 128 * delta
                S.dma("sp", "p3h%d" % b, I_("dma_start", out=H[:, di, :], in_=bass.AP(GREV.tensor, off, [[1, 128], [1, 512]])),
                      reads=[r_grev], writes=[r_H])

        work = []

        def make_iter(hi, g, c, kb, first_of_head):
            s, h = heads[hi]
            q = seqs[s]
            NR, NO = q["nreg"], q["noth"]
            NK = NR + NO
            sc = SC[s]; rs = sc["res"]
            (KT, r_KT), (QT, r_QT), (V, r_V), (H, r_H) = bufs[hi % 2]
            st_ = {}

            def stage_a():
                ps, psr = psS.next()
                st_["ps"] = (ps, psr)
                near = (kb < NR) and (4 * g - 1 <= kb <= 4 * g + 4)
                S.op("pe", I_("matmul", ps[:, :], lhsT=KT[:, kb * 128:(kb + 1) * 128], rhs=QT[:, c, g * 512:(g + 1) * 512],
                              start=True, stop=(not near)),
                     reads=[r_KT, r_QT], writes=[psr], inc=(not near), skip_self=True)
                if near:
                    di = kb - 4 * g + 1
                    S.op("pe", I_("matmul", ps[:, :], lhsT=jmat_b[:], rhs=H[:, di, :], start=False, stop=True),
                         reads=[r_jb, r_H], writes=[psr], skip_self=True)
                    st_["bias"] = (zero_c[:, 0:1], r_zero)
                elif kb >= NR:
                    st_["bias"] = (oth_fb[:, kb - NR, h:h + 1], r_ofb)
                elif kb < 4 * g - 1:
                    st_["bias"] = (farb[:, h:h + 1], r_farb)
                else:
                    st_["bias"] = (farb[:, 8 + h:9 + h], r_farb)

            def stage_b():
                if first_of_head and hi + 1 < len(heads):
                    emit_loads(hi + 1)
                ps, psr = st_["ps"]
                bias_ap, r_bias = st_["bias"]
                pt, r_pt = ptr.next()
                S.op("act", I_("activation", out=pt[:], in_=ps[:, :], func=AF.Exp, bias=bias_ap),
                     reads=[psr, r_bias], writes=[r_pt])
                if kb == 0:
                    rnd["acc"] = accr.next()
                    rnd["OT"] = psOT.next()
                    rnd["Z"] = psZ.next()
                acc, r_acc = rnd["acc"]
                OT, r_OT = rnd["OT"]
                Zp, r_Zp = rnd["Z"]
                if kb == 0:
                    S.op("dve", I_("tensor_copy", out=acc[:], in_=pt[:]), reads=[r_pt], writes=[r_acc])
                elif kb % 2 == 0:
                    S.op("dve", I_("tensor_tensor", out=acc[:], in0=acc[:], in1=pt[:], op=ALU.add), reads=[r_pt, r_acc], writes=[r_acc])
                S.op("pe", I_("matmul", OT[:, :], lhsT=V[:, kb, 0:128], rhs=pt[:, :], start=(kb == 0), stop=(kb == NK - 1)),
                     reads=[r_pt, r_V], writes=[r_OT], skip_self=True)
                if kb % 2 == 1:
                    S.op("pe", I_("matmul", Zp[:, :], lhsT=ones_b[:], rhs=pt[:, :], start=(kb == 1), stop=False),
                         reads=[r_pt, r_ones], writes=[r_Zp], skip_self=True)
                if kb != NK - 1:
                    return
                rnd["pending"] = (NK, epilogue_fn(acc, r_acc, OT, r_OT, Zp, r_Zp))

            def epilogue_fn(acc, r_acc, OT, r_OT, Zp, r_Zp):
              stt = {}

              def e1():
                S.op("pe", I_("matmul", Zp[:, :], lhsT=ones_f128[:], rhs=acc[:], start=False, stop=True), reads=[r_acc, r_onesf128], writes=[r_Zp], skip_self=True)
                OTe, r_OTe = OTer.next()
                OTh = OTe[:].bitcast(BF16)[:, 0:512]
                OTl = OTe[:].bitcast(BF16)[:, 512:1024]
                S.op("dve", I_("tensor_copy", out=OTh, in_=OT[:, :]), reads=[r_OT], writes=[r_OTe])
                S.op("dve", I_("tensor_tensor", out=OTl, in0=OT[:, :], in1=OTh, op=ALU.subtract), reads=[r_OT, r_OTe], writes=[r_OTe])
                OTn, r_OTn = OTnr.next()
                rc, r_rc = rcp.next()
                stt["rc"] = (rc, r_rc)
                S.op("dve", I_("tensor_tensor", out=OTn[:].rearrange("p (a b) -> p a b", a=4), in0=Zp[:, :].rearrange("p (a b) -> p a b", a=4),
                               in1=ident_f[:].unsqueeze(1).to_broadcast([128, 4, 128]), op=ALU.mult), reads=[r_Zp, r_identf], writes=[r_OTn])
                S.op("dve", I_("reduce_sum", out=rc[:, 0:4], in_=OTn[:].rearrange("p (a b) -> p a b", a=4), axis=AX.X), reads=[r_OTn], writes=[r_rc])
                S.op("dve", I_("reciprocal", out=rc[:, 0:4], in_=rc[:, 0:4]), reads=[r_rc], writes=[r_rc])
                if c == 1:
                    S.op("dve", I_("tensor_scalar", out=rc[:, 0:4], in0=rc[:, 0:4], scalar1=lamc[:, 1:2], scalar2=None, op0=ALU.mult), reads=[r_rc, r_lam], writes=[r_rc])
                po, por = psE.next()
                for qb in range(4):
                    S.op("pe", I_("matmul", po[:, qb * 128:(qb + 1) * 128], lhsT=OTh[:, qb * 128:(qb + 1) * 128], rhs=ident_b[:], start=True, stop=False), reads=[r_OTe, r_identb], writes=[por], inc=False, skip_self=True)
                    S.op("pe", I_("matmul", po[:, qb * 128:(qb + 1) * 128], lhsT=OTl[:, qb * 128:(qb + 1) * 128], rhs=ident_b[:], start=False, stop=True), reads=[r_OTe, r_identb], writes=[por], inc=(qb == 3), skip_self=True)
                stt["d"] = []
                for qb in range(4):
                    pq = po[:, qb * 128:(qb + 1) * 128]
                    oe, r_oe = Oe[qb]
                    if c == 0:
                        S.op("dve", I_("tensor_scalar", out=oe[:, 0:128], in0=pq, scalar1=rc[:, qb:qb + 1], scalar2=None, op0=ALU.mult), reads=[por, r_rc], writes=[r_oe])
                    else:
                        d, r_d = dr.next()
                        stt["d"].append((d, r_d))
                        S.op("dve", I_("scalar_tensor_tensor", out=d[:], in0=pq, scalar=rc[:, qb:qb + 1], in1=oe[:, 0:128], op0=ALU.mult, op1=ALU.add),
                             reads=[por, r_rc, r_oe], writes=[r_d])
                        S.op("dve", I_("tensor_tensor", out=junkf[:], in0=d[:], in1=d[:], op=ALU.mult), reads=[r_d], writes=[r_junkf])
                        S.op("dve", I_("reduce_sum", out=rc[:, 4 + qb:5 + qb], in_=junkf[:], axis=AX.X), reads=[r_junkf], writes=[r_rc])

              def e2():
                rc, r_rc = stt["rc"]
                S.op("act", I_("activation", out=rc[:, 8:12], in_=rc[:, 4:8], func=AF.Ln, bias=epsc[:, 3:4], scale=1.0 / 128), reads=[r_rc, r_epsc], writes=[r_rc])
                S.op("act", I_("activation", out=rc[:, 8:12], in_=rc[:, 8:12], func=AF.Exp, scale=-0.5), reads=[r_rc], writes=[r_rc])

              def e3():
                rc, r_rc = stt["rc"]
                for qb in range(4):
                    d, r_d = stt["d"][qb]
                    yo, r_yo = yor.next()
                    S.op("dve", I_("tensor_scalar", out=d[:], in0=d[:], scalar1=rc[:, 8 + qb:9 + qb], scalar2=None, op0=ALU.mult), reads=[r_d, r_rc], writes=[r_d])
                    S.op("pool", I_("tensor_tensor", out=yo[:], in0=d[:], in1=dng[:], op=ALU.mult), reads=[r_d, r_dng], writes=[r_yo])
                    tq = (g * 4 + qb) * 128
                    S.dma("sp", "p3yo%d" % yor.slot(), I_("dma_start", out=sc["Y"][tq:tq + 128, 1024 + h * 128:1024 + (h + 1) * 128], in_=yo[:]),
                          reads=[r_yo], writes=[rs["Y"]])
              if c == 0:
                  return [(6, e1)]
              return [(6, e1), (22, e2), (26, e3)]
            return stage_a, stage_b

        for hi, (s, h) in enumerate(heads):
            q = seqs[s]
            NK = q["nreg"] + q["noth"]
            first = True
            for g in range(q["nreg"] // 4):
                for c in range(2):
                    for kb in range(NK):
                        work.append(make_iter(hi, g, c, kb, first))
                        first = False
        emit_loads(0)
        for idx in range(len(work) + LA):
            if idx < len(work):
                work[idx][0]()
            if idx >= LA:
                rnd.pop("pending", None)
                work[idx - LA][1]()
                if "pending" in rnd:
                    nk_, stages = rnd.pop("pending")
                    for (off, fn_) in stages:
                        off_ = min(off, nk_ - 2) if off == 6 else min(off, nk_ - 1 + (1 if off == 26 else 0))
                        deferred.append((idx + off_, len(deferred), fn_))
                    deferred.sort(key=lambda t: (t[0], t[1]))
                while deferred and deferred[0][0] <= idx:
                    deferred.pop(0)[2]()
        deferred.sort(key=lambda t: (t[0], t[1]))
        while deferred:
            deferred.pop(0)[2]()
        S.barrier()
        st.close()

    def phase4():
        st = contextlib.ExitStack()
        Wo = sb(st, "p4_Wo", [128, 16, D], BF16); r_Wo = Res("Wo")
        Wc = sb(st, "p4_Wc", [128, 8, 6144], BF16); r_Wc = Res("Wc")
        for j in range(16):
            S.dma("pool", "p4w", I_("dma_start", out=Wo[:, j, :], in_=I["w_out_mix"][j * 128:(j + 1) * 128, :]), writes=[r_Wo])
        for j in range(8):
            S.dma("pool", "p4w", I_("dma_start", out=Wc[:, j, :], in_=I["w_in_conv"][j * 128:(j + 1) * 128, :]), writes=[r_Wc])
        yr = Ring([(sb(st, "p4_y%d" % i, [128, 2048], BF16), Res("y")) for i in range(1)])
        gr = Ring([(sb(st, "p4_g%d" % i, [128, 2048], BF16), Res("g")) for i in range(1)])
        ygr = Ring([(sb(st, "p4_yg%d" % i, [128, 2048], BF16), Res("yg")) for i in range(1)])
        ygTr = Ring([(sb(st, "p4_ygT%d" % i, [128, 16, 128], BF16), Res("ygT")) for i in range(2)])
        xr = Ring([(sb(st, "p4_x%d" % i, [128, D], F32), Res("x")) for i in range(2)])
        x1r = Ring([(sb(st, "p4_x1%d" % i, [128, D], F32), Res("x1")) for i in range(1)])
        ssr = Ring([(sb(st, "p4_ss%d" % i, [128, 2], F32), Res("ss")) for i in range(2)])
        junk = sb(st, "p4_junk", [128, D], BF16); r_junk = Res("junk")
        hr = Ring([(sb(st, "p4_h%d" % i, [128, D], BF16), Res("h")) for i in range(2)])
        hTr = Ring([(sb(st, "p4_hT%d" % i, [128, 8, 512], BF16), Res("hT")) for i in range(1)])
        sgr = Ring([(sb(st, "p4_sg%d" % i, [128, 512], F32), Res("sg")) for i in range(2)])
        fmr = Ring([(sb(st, "p4_fm%d" % i, [128, 512], BF16), Res("fm")) for i in range(4)])
        psT = Ring(PS[0:2]); pso = Ring(PS[2:4]); psfm = Ring(PS[4:8])

        for s in ("a", "b"):
            q = seqs[s]
            sc = SC[s]; rs = sc["res"]
            for g in range(q["nreg"] // 4):
                t0 = g * 512
                hT, r_hT = hTr.next()
                for i in range(4):
                    tt = t0 + i * 128
                    y, r_y = yr.next(); gt, r_gt = gr.next(); x, r_x = xr.next()
                    S.dma("sp", "p4y%d" % yr.slot(), I_("dma_start", out=y[:], in_=sc["Y"][tt:tt + 128, :]), reads=[rs["Y"]], writes=[r_y])
                    S.dma("sp", "p4g%d" % gr.slot(), I_("dma_start", out=gt[:], in_=sc["GT"][tt:tt + 128, :]), reads=[rs["GT"]], writes=[r_gt])
                    S.dma("sp", "p4x%d" % xr.slot(), I_("dma_start", out=x[:], in_=X[s][tt:tt + 128, :]), writes=[r_x])
                    yg, r_yg = ygr.next()
                    S.op("pool", I_("tensor_tensor", out=yg[:], in0=y[:], in1=gt[:], op=ALU.mult), reads=[r_y, r_gt], writes=[r_yg])
                    ygT, r_ygT = ygTr.next()
                    for half in range(2):
                        pt, pr = psT.next()
                        ptb = pt[:].bitcast(BF16)
                        for j in range(8):
                            jj = half * 8 + j
                            S.op("pe", I_("transpose", ptb[:, j * 128:(j + 1) * 128], yg[:, jj * 128:(jj + 1) * 128], ident_b[:]),
                                 reads=[r_yg, r_identb], writes=[pr], inc=(j == 7), skip_self=True)
                        S.op("act", I_("activation", out=ygT[:, half * 8:(half + 1) * 8, :], in_=ptb.rearrange("p (j t) -> p j t", j=8), func=AF.Copy),
                             reads=[pr], writes=[r_ygT])
                    x1, r_x1 = x1r.next()
                    for cg in range(2):
                        po, por = pso.next()
                        for j in range(16):
                            S.op("pe", I_("matmul", po[:, :], lhsT=ygT[:, j, :], rhs=Wo[:, j, cg * 512:(cg + 1) * 512], start=(j == 0), stop=(j == 15)),
                                 reads=[r_ygT, r_Wo], writes=[por], inc=(j == 15), skip_self=True)
                        S.op("dve", I_("tensor_tensor", out=x1[:, cg * 512:(cg + 1) * 512], in0=po[:, :], in1=x[:, cg * 512:(cg + 1) * 512], op=ALU.add),
                             reads=[por, r_x], writes=[r_x1])
                    S.dma("sp", "p4x1%d" % x1r.slot(), I_("dma_start", out=sc["X1"][tt:tt + 128, :], in_=x1[:]), reads=[r_x1], writes=[rs["X1"]])
                    ss, r_ss = ssr.next()
                    S.op("act", I_("activation", out=junk[:], in_=x1[:], func=AF.Square, accum_out=ss[:, 0:1]), reads=[r_x1], writes=[r_junk, r_ss])
                    S.op("act", I_("activation", out=ss[:, 1:2], in_=ss[:, 0:1], func=AF.Sqrt, bias=epsc[:, 0:1]), reads=[r_ss, r_epsc], writes=[r_ss])
                    S.op("dve", I_("reciprocal", out=ss[:, 1:2], in_=ss[:, 1:2]), reads=[r_ss], writes=[r_ss])
                    h, r_h = hr.next()
                    S.op("act", I_("activation", out=x[:], in_=x1[:], func=AF.Copy, scale=ss[:, 1:2]), reads=[r_x1, r_ss, r_x], writes=[r_x])
                    S.op("pool", I_("tensor_tensor", out=h[:], in0=x[:], in1=g2[:], op=ALU.mult), reads=[r_x, r_g2], writes=[r_h])
                    pt, pr = psT.next()
                    ptb = pt[:].bitcast(BF16)
                    for j in range(8):
                        S.op("pe", I_("transpose", ptb[:, j * 128:(j + 1) * 128], h[:, j * 128:(j + 1) * 128], ident_b[:]),
                             reads=[r_h, r_identb], writes=[pr], inc=(j == 7), skip_self=True)
                    S.op("act", I_("activation", out=hT[:, :, i * 128:(i + 1) * 128], in_=ptb.rearrange("p (j t) -> p j t", j=8), func=AF.Copy),
                         reads=[pr], writes=[r_hT])
                def fm_mm(c0):
                    pm, pmr = psfm.next()
                    for j in range(8):
                        S.op("pe", I_("matmul", pm[:, :], lhsT=Wc[:, j, c0:c0 + 128], rhs=hT[:, j, :], start=(j == 0), stop=(j == 7)),
                             reads=[r_hT, r_Wc], writes=[pmr], inc=(j == 7), skip_self=True)
                    return pm, pmr
                for cb in range(16):
                    pa, par = fm_mm(cb * 128)
                    pb, pbr = fm_mm(2048 + cb * 128)
                    sg, r_sg = sgr.next()
                    S.op("act", I_("activation", out=sg[:], in_=pb[:, :], func=AF.Sigmoid), reads=[pbr], writes=[r_sg])
                    fm, r_fm = fmr.next(); slot = fmr.slot()
                    S.op("dve", I_("tensor_tensor", out=fm[:], in0=pa[:, :], in1=sg[:], op=ALU.mult), reads=[par, r_sg], writes=[r_fm])
                    S.dma("sp", "p4fm%d" % slot, I_("dma_start", out=sc["GLU"][cb * 128:(cb + 1) * 128, t0:t0 + 512], in_=fm[:]), reads=[r_fm], writes=[rs["GLU"]])
                    pg, pgr = fm_mm(4096 + cb * 128)
                    fm, r_fm = fmr.next(); slot = fmr.slot()
                    S.op("act", I_("activation", out=fm[:], in_=pg[:, :], func=AF.Silu), reads=[pgr], writes=[r_fm])
                    S.dma("sp", "p4fm%d" % slot, I_("dma_start", out=sc["SG"][cb * 128:(cb + 1) * 128, t0:t0 + 512], in_=fm[:]), reads=[r_fm], writes=[rs["SG"]])
        S.barrier()
        st.close()

    def phase5():
        st = contextlib.ExitStack()
        Wo = sb(st, "p5_Wo", [128, 16, D], BF16); r_Wo = Res("Wo5")
        for j in range(16):
            S.dma("pool", "p5w", I_("dma_start", out=Wo[:, j, :], in_=I["w_out_conv"][j * 128:(j + 1) * 128, :]), writes=[r_Wo])
        winr = Ring([(sb(st, "p5_win%d" % i, [128, 16, 544], BF16), Res("win")) for i in range(2)])
        sgr = Ring([(sb(st, "p5_sg%d" % i, [128, 16, 512], BF16), Res("sg")) for i in range(1)])
        dgr = Ring([(sb(st, "p5_dg%d" % i, [128, 16, 128], BF16), Res("dg")) for i in range(2)])
        dgor = Ring([(sb(st, "p5_dgo%d" % i, [128, 15, 128], BF16), Res("dgo")) for i in range(2)])
        ybuf = sb(st, "p5_y", [128, 16, 512], F32); r_yb = [Res("yb%d" % i) for i in range(16)]
        ybfr = Ring([(sb(st, "p5_ybf%d" % i, [128, 512], BF16), Res("ybf")) for i in range(2)])
        ysqr = Ring([(sb(st, "p5_ysq%d" % i, [128, 512], BF16), Res("ysq")) for i in range(2)])
        mean = sb(st, "p5_mean", [128, 512], F32); r_mean = Res("mean")
        msq = sb(st, "p5_msq", [128, 512], F32); r_msq = Res("msq")
        rstd = sb(st, "p5_rstd", [128, 512], F32); r_rstd = Res("rstd")
        tnr = Ring([(sb(st, "p5_tn%d" % i, [128, 512], F32), Res("tn")) for i in range(2)])
        znr = Ring([(sb(st, "p5_zn%d" % i, [128, 512], F32), Res("zn")) for i in range(2)])
        zT = sb(st, "p5_zT", [128, 16, 512], BF16); r_zT = [Res("zT%d" % i) for i in range(16)]
        x1r = Ring([(sb(st, "p5_x1%d" % i, [128, D], F32), Res("x1")) for i in range(2)])
        x2r = Ring([(sb(st, "p5_x2%d" % i, [128, D], F32), Res("x2")) for i in range(2)])
        ssr = Ring([(sb(st, "p5_ss%d" % i, [128, 2], F32), Res("ss")) for i in range(2)])
        junk = sb(st, "p5_junk", [128, D], BF16); r_junk = Res("junk")
        psy = Ring(PS[0:2]); ps_st, r_pst = PS[2]; ps_sq, r_psq = PS[3]; pso = Ring(PS[4:8])

        for s in ("a", "b"):
            q = seqs[s]
            sc = SC[s]; rs = sc["res"]
            SR = q["sreg"]
            NG = q["nreg"] // 4
            for g in range(NG):
                t0 = g * 512
                win, r_win = winr.next(); sg, r_sg = sgr.next()
                lo = max(t0 - 15, 0); hi = min(t0 + 512 + 15, SR)
                c_lo = lo - (t0 - 15)
                if g == 0 or g == NG - 1:
                    S.op("pool", I_("memset", win[:], 0.0), writes=[r_win])
                for half in range(2):
                    S.dma("sp", "p5win%d" % winr.slot(), I_("dma_start", out=win[:, half * 8:(half + 1) * 8, c_lo:c_lo + (hi - lo)],
                                                                                                                 in_=sc["GLU"][half * 1024:(half + 1) * 1024, lo:hi].rearrange("(cb p) t -> p cb t", p=128)),
                          reads=[rs["GLU"]], writes=[r_win])
                    S.dma("sp", "p5sg%d" % sgr.slot(), I_("dma_start", out=sg[:, half * 8:(half + 1) * 8, :], in_=sc["SG"][half * 1024:(half + 1) * 1024, t0:t0 + 512].rearrange("(cb p) t -> p cb t", p=128)),
                          reads=[rs["SG"]], writes=[r_sg])
                pend_stats = []

                def build_diag(cb_):
                    dg_, r_dg_ = dgr.next()
                    dgo_, r_dgo_ = dgor.next()
                    for k in range(0, 31, 2):
                        S.op("dve", I_("tensor_scalar", out=dg_[:, k // 2, :], in0=ident_f[:], scalar1=cwT[:, cb_, k:k + 1], scalar2=None, op0=ALU.mult),
                             reads=[r_identf, r_cwT], writes=[r_dg_], inc=(k == 30), skip_self=True)
                    for k in range(1, 31, 2):
                        S.op("act", I_("activation", out=dgo_[:, k // 2, :], in_=ident_f[:], func=AF.Copy, scale=cwT[:, cb_, k:k + 1]),
                             reads=[r_identf, r_cwT], writes=[r_dgo_], inc=(k == 29), skip_self=True)
                    return dg_, r_dg_, dgo_, r_dgo_

                nxt_diag = build_diag(0)
                for cb in range(16):
                    dg, r_dg, dgo, r_dgo = nxt_diag
                    nxt_diag = build_diag(cb + 1) if cb < 15 else None
                    py, pyr = psy.next()
                    for k in range(31):
                        S.op("pe", I_("matmul", py[:, :], lhsT=(dg if k % 2 == 0 else dgo)[:, k // 2, :], rhs=win[:, cb, k:k + 512], start=(k == 0), stop=(k == 30)),
                             reads=[r_dg, r_dgo, r_win], writes=[pyr], inc=(k == 30), skip_self=True)
                    while pend_stats:
                        pend_stats.pop(0)()
                    S.op("act", I_("activation", out=ybuf[:, cb, :], in_=py[:, :], func=AF.Identity, bias=cvb[:, cb:cb + 1]), reads=[pyr, r_cvb], writes=[r_yb[cb]])
                    ysq, r_ysq = ysqr.next(); ybf, r_ybf = ybfr.next()
                    S.op("act", I_("activation", out=ysq[:], in_=py[:, :], func=AF.Square, bias=cvb[:, cb:cb + 1]), reads=[pyr, r_cvb], writes=[r_ysq])
                    S.op("dve", I_("tensor_copy", out=ybf[:], in_=ybuf[:, cb, :]), reads=[r_yb[cb]], writes=[r_ybf])
                    def _stats(ybf=ybf, r_ybf=r_ybf, ysq=ysq, r_ysq=r_ysq, cb=cb):
                        S.op("pe", I_("matmul", ps_st[:, :], lhsT=ones_b[:], rhs=ybf[:], start=(cb == 0), stop=(cb == 15)), reads=[r_ybf, r_ones], writes=[r_pst], skip_self=True)
                        S.op("pe", I_("matmul", ps_sq[:, :], lhsT=ones_b[:], rhs=ysq[:], start=(cb == 0), stop=(cb == 15)), reads=[r_ysq, r_ones], writes=[r_psq], skip_self=True)
                    pend_stats.append(_stats)
                while pend_stats:
                    pend_stats.pop(0)()
                S.op("dve", I_("tensor_scalar", out=mean[:], in0=ps_st[:, :], scalar1=1.0 / 2048, scalar2=None, op0=ALU.mult), reads=[r_pst], writes=[r_mean])
                S.op("dve", I_("tensor_tensor", out=msq[:], in0=mean[:], in1=mean[:], op=ALU.mult), reads=[r_mean], writes=[r_msq])
                S.op("dve", I_("scalar_tensor_tensor", out=rstd[:], in0=ps_sq[:, :], scalar=1.0 / 2048, in1=msq[:], op0=ALU.mult, op1=ALU.subtract), reads=[r_psq, r_msq], writes=[r_rstd])
                S.op("act", I_("activation", out=rstd[:], in_=rstd[:], func=AF.Sqrt, bias=epsc[:, 3:4]), reads=[r_rstd, r_epsc], writes=[r_rstd])
                S.op("dve", I_("reciprocal", out=rstd[:], in_=rstd[:]), reads=[r_rstd], writes=[r_rstd])
                for cb in range(16):
                    tn, r_tn = tnr.next(); zn, r_zn = znr.next()
                    S.op("dve", I_("tensor_tensor", out=tn[:], in0=ybuf[:, cb, :], in1=mean[:], op=ALU.subtract), reads=[r_yb[cb], r_mean], writes=[r_tn])
                    S.op("pool", I_("tensor_tensor", out=tn[:], in0=tn[:], in1=rstd[:], op=ALU.mult), reads=[r_tn, r_rstd], writes=[r_tn])
                    S.op("act", I_("activation", out=zn[:], in_=tn[:], func=AF.Silu, bias=cnb[:, cb:cb + 1], scale=cng[:, cb:cb + 1]),
                         reads=[r_tn, r_cng, r_cnb], writes=[r_zn])
                    S.op("dve", I_("tensor_tensor", out=zT[:, cb, :], in0=zn[:], in1=sg[:, cb, :], op=ALU.mult), reads=[r_zn, r_sg], writes=[r_zT[cb]])
                for i in range(4):
                    tt = t0 + i * 128
                    x1, r_x1 = x1r.next()
                    S.dma("sp", "p5x1%d" % x1r.slot(), I_("dma_start", out=x1[:], in_=sc["X1"][tt:tt + 128, :]), reads=[rs["X1"]], writes=[r_x1])
                    x2, r_x2 = x2r.next()
                    for cg in range(2):
                        po, por = pso.next()
                        for cb in range(16):
                            S.op("pe", I_("matmul", po[:, :], lhsT=zT[:, cb, i * 128:(i + 1) * 128], rhs=Wo[:, cb, cg * 512:(cg + 1) * 512], start=(cb == 0), stop=(cb == 15)),
                                 reads=[r_zT[cb], r_Wo], writes=[por], inc=(cb == 15), skip_self=True)
                        S.op("dve", I_("tensor_tensor", out=x2[:, cg * 512:(cg + 1) * 512], in0=po[:, :], in1=x1[:, cg * 512:(cg + 1) * 512], op=ALU.add),
                             reads=[por, r_x1], writes=[r_x2])
                    ss, r_ss = ssr.next()
                    S.op("act", I_("activation", out=junk[:], in_=x2[:], func=AF.Square, accum_out=ss[:, 0:1]), reads=[r_x2], writes=[r_junk, r_ss])
                    S.op("act", I_("activation", out=ss[:, 1:2], in_=ss[:, 0:1], func=AF.Sqrt, bias=epsc[:, 0:1]), reads=[r_ss, r_epsc], writes=[r_ss])
                    S.op("dve", I_("reciprocal", out=ss[:, 1:2], in_=ss[:, 1:2]), reads=[r_ss], writes=[r_ss])
                    S.op("dve", I_("scalar_tensor_tensor", out=x2[:], in0=x2[:], scalar=ss[:, 1:2], in1=gfin[:], op0=ALU.mult, op1=ALU.mult),
                         reads=[r_x2, r_ss, r_gf], writes=[r_x2])
                    S.dma("sp", "p5o%d" % x2r.slot(), I_("dma_start", out=O[s][tt:tt + 128, :], in_=x2[:]), reads=[r_x2], final=True)
        st.close()

    import os as _os
    upto = int(_os.environ.get("K_UPTO", "5"))
    setup()
    if upto >= 1:
        phase1()
    if upto >= 2:
        phase2()
    if upto >= 3:
        phase3()
    if upto >= 4:
        phase4()
    if upto >= 5:
        phase5()
    S.finish()
    S.replay()
    return nc, S


_CACHE = {}


def _prep_common(inp):
    f = lambda a: np.ascontiguousarray(np.asarray(a, dtype=np.float32))
    c = dict(
        norm_g=f(inp["norm_g"]), final_g=f(inp["final_g"]).reshape(1, D), rel_bias=f(inp["rel_bias"]),
        w_in_mix=f(inp["w_in_mix"][0]), ret_decay=f(inp["ret_decay"][0]).reshape(1, 8),
        lam4=np.concatenate([f(inp["lam_q1"][0]), f(inp["lam_k1"][0]), f(inp["lam_q2"][0]), f(inp["lam_k2"][0])]).reshape(1, 256),
        diff_norm_g=f(inp["diff_norm_g"][0]).reshape(1, 128), w_out_mix=f(inp["w_out_mix"][0]),
        w_in_conv=f(inp["w_in_conv"][0]), conv_w=f(inp["conv_w"][0]), conv_b=f(inp["conv_b"][0]).reshape(1, 2048),
        conv_norm_g=f(inp["conv_norm_g"][0]).reshape(1, 2048), conv_norm_b=f(inp["conv_norm_b"][0]).reshape(1, 2048),
        w_out_conv=f(inp["w_out_conv"][0]))
    c.update(_consts())
    return c


def run_config(inp, NBA, NRB, NOB, assign):
    key = (NBA, NRB, NOB)
    if key not in _CACHE:
        _CACHE[key] = build_program(NBA, NRB, NOB)
    nc, _ = _CACHE[key]
    common = _prep_common(inp)
    xs = np.asarray(inp["x_sample"], dtype=np.float32)
    xp = np.asarray(inp["x_prompt"], dtype=np.float32)
    NB = NRB + NOB
    in_maps = []
    for (si, pi, rs) in assign:
        reg = list(range(rs, rs + NRB))
        oth = [b for b in range(NB) if b < rs or b >= rs + NRB]
        order = reg + oth
        tok = (np.array(order)[:, None] * 128 + np.arange(128)[None, :]).reshape(-1)
        m = dict(common)
        m["xa"] = np.ascontiguousarray(xs[si])
        m["xb"] = np.ascontiguousarray(xp[pi][tok])
        m["csa"] = _rope_table(np.arange(NBA * 128))
        m["csb"] = _rope_table(tok)
        fl = np.array([1.0 if b < rs else 0.0 for b in oth], dtype=np.float32).reshape(1, -1)
        if fl.shape[1] == 0:
            fl = np.zeros((1, 1), np.float32)
        m["fl"] = fl
        in_maps.append(m)
    res = run_bass_kernel_spmd(nc, in_maps, core_ids=list(range(len(assign))))
    return res.results


def kernel(**inputs):
    NBA, NRB, NOB = 32, 20, 44
    starts = [0, 14, 30, 44]
    assign = [(c, c // 4, starts[c % 4]) for c in range(8)]
    res = run_config(inputs, NBA, NRB, NOB, assign)
    y_prompt = np.zeros((2, 8192, D), np.float32)
    y_sample = np.zeros((8, 4096, D), np.float32)
    for c in range(8):
        y_sample[c] = res[c]["out_a"]
        qd = c % 4
        lo = (16 * qd - starts[qd]) * 128
        y_prompt[c // 4, qd * 2048:(qd + 1) * 2048] = res[c]["out_b"][lo:lo + 2048]
    return (y_prompt, y_sample)
```
